# Optimizing a Trainium2 kernel written in Bass

```python
import math
import jax, jax.numpy as jnp
from jax import lax
import numpy as np

D_MODEL = 1024
BATCH = 8
SEQ = 4096
DEPTH = 1

CHUNK = 64
QBLK = 128
MIX_WIDTH = D_MODEL
DIFF_QK_DIM = 64
DIFF_V_DIM = 2 * DIFF_QK_DIM
DIFF_HEADS = (MIX_WIDTH // 2) // DIFF_V_DIM
SB_DIM = 64
SB_HEADS = (MIX_WIDTH // 2) // SB_DIM
NUM_BUCKETS = 32
MAX_DISTANCE = 128
D_FF = ((8 * D_MODEL + 3 * 256 - 1) // (3 * 256)) * 256
EPS = 1e-6

DIFF_Q_COLS = DIFF_HEADS * 2 * DIFF_QK_DIM
DIFF_V_COLS = DIFF_HEADS * DIFF_V_DIM
SB_COLS = SB_HEADS * SB_DIM
IN_COLS = 2 * DIFF_Q_COLS + DIFF_V_COLS + 3 * SB_COLS
SPLIT_POINTS = (DIFF_Q_COLS, 2 * DIFF_Q_COLS, 2 * DIFF_Q_COLS + DIFF_V_COLS,
                2 * DIFF_Q_COLS + DIFF_V_COLS + SB_COLS,
                2 * DIFF_Q_COLS + DIFF_V_COLS + 2 * SB_COLS)

kernel_name = "hymba_diffattn_stickbreaking_block"


def rms_norm(x, w):
    xf = x.astype(jnp.float32)
    y = xf * lax.rsqrt(jnp.mean(xf * xf, axis=-1, keepdims=True) + EPS)
    return (y * w.astype(jnp.float32)).astype(x.dtype)


def t5_bucket(rel):
    nb = NUM_BUCKETS // 2
    max_exact = nb // 2
    ret = (rel > 0).astype(jnp.int32) * nb
    n = jnp.abs(rel)
    nf = jnp.maximum(n, 1).astype(jnp.float32)
    large = max_exact + (jnp.log(nf / max_exact) / math.log(MAX_DISTANCE / max_exact)
                         * (nb - max_exact)).astype(jnp.int32)
    large = jnp.minimum(large, nb - 1)
    return ret + jnp.where(n < max_exact, n, large)


def to_blocks(t):
    b, s = t.shape[0], t.shape[1]
    return jnp.moveaxis(t.reshape((b, s // QBLK, QBLK) + t.shape[2:]), 1, 0)


def from_blocks(t):
    t = jnp.moveaxis(t, 0, 1)
    return t.reshape((t.shape[0], t.shape[1] * t.shape[2]) + t.shape[3:])


def diff_attention(q, k, v, lam, rel_bias):
    s = q.shape[1]
    scale = DIFF_QK_DIM ** -0.5
    k_pos = jnp.arange(s, dtype=jnp.int32)
    k_chunk = k_pos // CHUNK

    def one_block(args):
        qb, blk = args
        q_pos = blk * QBLK + jnp.arange(QBLK, dtype=jnp.int32)
        logits = jnp.einsum('bqhcd,bkhcd->bhcqk', qb, k).astype(jnp.float32) * scale
        bias = rel_bias.astype(jnp.float32)[t5_bucket(k_pos[None, :] - q_pos[:, None])]
        logits = logits + jnp.transpose(bias, (2, 0, 1))[None, :, None]
        mask = k_chunk[None, :] <= (q_pos // CHUNK)[:, None]
        logits = jnp.where(mask[None, None, None], logits, -jnp.inf)
        p = jax.nn.softmax(logits, axis=-1)
        attn = p[:, :, 0] - lam * p[:, :, 1]
        return jnp.einsum('bhqk,bkhd->bqhd', attn.astype(v.dtype), v)

    nblk = s // QBLK
    out = lax.map(one_block, (to_blocks(q), jnp.arange(nblk, dtype=jnp.int32)))
    return from_blocks(out)


def stick_breaking_attention(q, k, v):
    s = q.shape[1]
    scale = SB_DIM ** -0.5
    k_pos = jnp.arange(s, dtype=jnp.int32)

    def one_block(args):
        qb, blk = args
        q_pos = blk * QBLK + jnp.arange(QBLK, dtype=jnp.int32)
        z = jnp.einsum('bqhd,bkhd->bhqk', qb, k).astype(jnp.float32) * scale
        mask = (k_pos[None, :] < q_pos[:, None])[None, None]
        log_1mb = jnp.where(mask, jax.nn.log_sigmoid(-z), 0.0)
        rem = lax.cumsum(log_1mb, axis=3, reverse=True) - log_1mb
        a = jnp.where(mask, jnp.exp(jax.nn.log_sigmoid(z) + rem), 0.0)
        return jnp.einsum('bhqk,bkhd->bqhd', a.astype(v.dtype), v)

    nblk = s // QBLK
    out = lax.map(one_block, (to_blocks(q), jnp.arange(nblk, dtype=jnp.int32)))
    return from_blocks(out)


def setup_inputs(seed: int = 0) -> dict:
    key = jax.random.key(seed)
    ks = jax.random.split(key, 20)
    f32 = jnp.float32

    def gain(k, shape):
        return 1.0 + 0.05 * jax.random.normal(k, shape, f32)

    return {
        "x": jax.random.normal(ks[0], (BATCH, SEQ, D_MODEL), f32),
        "norm1_w": gain(ks[1], (DEPTH, D_MODEL)),
        "w_in": jax.random.normal(ks[2], (DEPTH, D_MODEL, IN_COLS), f32) * D_MODEL ** -0.5,
        "q_norm_w": gain(ks[3], (DEPTH, DIFF_QK_DIM)),
        "k_norm_w": gain(ks[4], (DEPTH, DIFF_QK_DIM)),
        "lambda_q1": 0.1 * jax.random.normal(ks[5], (DEPTH, DIFF_QK_DIM), f32),
        "lambda_k1": 0.1 * jax.random.normal(ks[6], (DEPTH, DIFF_QK_DIM), f32),
        "lambda_q2": 0.1 * jax.random.normal(ks[7], (DEPTH, DIFF_QK_DIM), f32),
        "lambda_k2": 0.1 * jax.random.normal(ks[8], (DEPTH, DIFF_QK_DIM), f32),
        "diff_out_norm_w": gain(ks[9], (DEPTH, DIFF_V_DIM)),
        "sb_out_norm_w": gain(ks[10], (DEPTH, SB_DIM)),
        "w_out": jax.random.normal(ks[11], (DEPTH, MIX_WIDTH, D_MODEL), f32) * MIX_WIDTH ** -0.5,
        "norm2_w": gain(ks[12], (DEPTH, D_MODEL)),
        "w_gate": jax.random.normal(ks[13], (DEPTH, D_MODEL, D_FF), f32) * D_MODEL ** -0.5,
        "w_up": jax.random.normal(ks[14], (DEPTH, D_MODEL, D_FF), f32) * D_MODEL ** -0.5,
        "w_down": jax.random.normal(ks[15], (DEPTH, D_FF, D_MODEL), f32) * D_FF ** -0.5,
        "rel_bias": 0.5 * jax.random.normal(ks[16], (NUM_BUCKETS, DIFF_HEADS), f32),
    }


def reference(x, norm1_w, w_in, q_norm_w, k_norm_w, lambda_q1, lambda_k1, lambda_q2,
              lambda_k2, diff_out_norm_w, sb_out_norm_w, w_out, norm2_w, w_gate, w_up,
              w_down, rel_bias):
    b, s, _ = x.shape
    h = x
    for l in range(DEPTH):
        lambda_init = 0.8 - 0.6 * math.exp(-0.3 * l)
        u = rms_norm(h, norm1_w[l])
        proj = jnp.einsum('bsd,de->bse', u, w_in[l])
        dq, dk, dv, sq, sk, sv = jnp.split(proj, SPLIT_POINTS, axis=-1)

        dq = rms_norm(dq.reshape(b, s, DIFF_HEADS, 2, DIFF_QK_DIM), q_norm_w[l])
        dk = rms_norm(dk.reshape(b, s, DIFF_HEADS, 2, DIFF_QK_DIM), k_norm_w[l])
        dv = dv.reshape(b, s, DIFF_HEADS, DIFF_V_DIM)
        lam = (jnp.exp(jnp.sum(lambda_q1[l].astype(jnp.float32) * lambda_k1[l].astype(jnp.float32)))
               - jnp.exp(jnp.sum(lambda_q2[l].astype(jnp.float32) * lambda_k2[l].astype(jnp.float32)))
               + lambda_init)
        y_diff = diff_attention(dq, dk, dv, lam, rel_bias)
        y_diff = rms_norm(y_diff, diff_out_norm_w[l]) * (1.0 - lambda_init)

        sq = sq.reshape(b, s, SB_HEADS, SB_DIM)
        sk = sk.reshape(b, s, SB_HEADS, SB_DIM)
        sv = sv.reshape(b, s, SB_HEADS, SB_DIM)
        y_sb = rms_norm(stick_breaking_attention(sq, sk, sv), sb_out_norm_w[l])

        mix = jnp.concatenate([y_diff.reshape(b, s, DIFF_HEADS * DIFF_V_DIM),
                               y_sb.reshape(b, s, SB_HEADS * SB_DIM)], axis=-1)
        h = h + jnp.einsum('bse,ed->bsd', mix, w_out[l])

        u2 = rms_norm(h, norm2_w[l])
        gate = jnp.einsum('bsd,df->bsf', u2, w_gate[l])
        up = jnp.einsum('bsd,df->bsf', u2, w_up[l])
        h = h + jnp.einsum('bsf,fd->bsd', jax.nn.silu(gate) * up, w_down[l])
    return h
```

```python
import math
from contextlib import ExitStack

import numpy as np
import concourse.bass as bass
import concourse.mybir as mybir
from concourse.bass_utils import run_bass_kernel_spmd

F32 = mybir.dt.float32
BF16 = mybir.dt.bfloat16
AF = mybir.ActivationFunctionType
ALU = mybir.AluOpType
AX = mybir.AxisListType

S = 4096
D = 1024
DFF = 2816
NFC = DFF // 128
EPS = 1e-6
NEG = -30000.0
SELF_SYNC = True
GROUP_ORDER = [4, 5, 6, 7, 0, 1, 2, 3]

C_ID = 0
C_NTRI = 128
C_ONES = 256
C_BD = 384
C_NSA = 512
C_NSB = 640
C_EA = 768
C_EB = 832
C_TM = 896
CBW = 1024
C_MNEG = 1024
C_OH = 1664
NCOL = 2432


class Tok:
    __slots__ = ("eng", "needed", "val", "sem")

    def __init__(self, eng):
        self.eng = eng
        self.needed = False
        self.val = None
        self.sem = None


class Buf:
    __slots__ = ("wr", "rd")

    def __init__(self):
        self.wr = {}
        self.rd = {}


class Prog:
    ENG = ("sp", "act", "pe", "dve", "pool")

    def __init__(self, nc, stack, ndma=32):
        self.nc = nc
        self.ops = {e: [] for e in self.ENG}
        self.esem = {e: stack.enter_context(nc.semaphore("s_" + e)) for e in self.ENG}
        self.dsem = [stack.enter_context(nc.semaphore("d%d" % i)) for i in range(ndma)]
        self.dcount = [0] * ndma
        self.dlast = [None] * ndma
        self.dnext = 0
        self.ndma = ndma
        self.uid = 0
        self.out_toks = []

    def _hazards(self, reads, writes):
        waits = []
        for b in reads:
            waits += list(b.wr.values())
        for b in writes:
            waits += list(b.wr.values())
            waits += list(b.rd.values())
        return waits

    def _update(self, tok, key, reads, writes):
        for b in reads:
            b.rd[key] = tok
        for b in writes:
            b.wr = {key: tok}
            b.rd = {}

    def op(self, eng, fn, reads=(), writes=()):
        waits = self._hazards(reads, writes)
        tok = Tok(eng)
        self.ops[eng].append((waits, fn, tok))
        self._update(tok, eng, reads, writes)
        return tok

    def dma(self, out_ap, in_ap, reads=(), writes=(), eng="sp"):
        k = self.dnext
        self.dnext = (k + 1) % self.ndma
        waits = self._hazards(reads, writes)
        if self.dlast[k] is not None:
            waits.append(self.dlast[k])
        self.dcount[k] += 16
        tok = Tok("dma")
        tok.needed = True
        tok.sem = self.dsem[k]
        tok.val = self.dcount[k]
        sem = self.dsem[k]

        def fn(e, out_ap=out_ap, in_ap=in_ap, sem=sem):
            return e.dma_start(out=out_ap, in_=in_ap).then_inc(sem, 16)

        self.ops[eng].append((waits, fn, tok))
        self.dlast[k] = tok
        self.uid += 1
        self._update(tok, "dma%d" % self.uid, reads, writes)
        return tok

    def finalize(self, block):
        for e in self.ENG:
            for waits, fn, tok in self.ops[e]:
                for w in waits:
                    if w.eng == "dma":
                        continue
                    if w.eng == e and (e == "pe" or e == "sp" or not SELF_SYNC):
                        continue
                    w.needed = True
        for t in self.out_toks:
            t.needed = True
        for e in self.ENG:
            c = 0
            for waits, fn, tok in self.ops[e]:
                if tok.eng != "dma" and tok.needed:
                    c += 1
                    tok.val = c
                    tok.sem = self.esem[e]

        def run(h, e):
            seen = {}
            for waits, fn, tok in self.ops[e]:
                best = {}
                for w in waits:
                    if not w.needed or w.val is None:
                        continue
                    if w.eng == e and (e == "pe" or e == "sp" or not SELF_SYNC):
                        continue
                    sid = id(w.sem)
                    if seen.get(sid, 0) >= w.val:
                        continue
                    if sid not in best or best[sid][1] < w.val:
                        best[sid] = (w.sem, w.val)
                for sid, (sem, val) in best.items():
                    h.wait_ge(sem, val)
                    seen[sid] = val
                ins = fn(h)
                if tok.eng != "dma" and tok.needed:
                    ins.then_inc(self.esem[e], 1)
            if e == "sp":
                for t in self.out_toks:
                    if seen.get(id(t.sem), 0) < t.val:
                        h.wait_ge(t.sem, t.val)
                        seen[id(t.sem)] = t.val

        @block.sync
        def _(h):
            run(h, "sp")

        @block.scalar
        def _(h):
            run(h, "act")

        @block.tensor
        def _(h):
            run(h, "pe")

        @block.vector
        def _(h):
            run(h, "dve")

        @block.gpsimd
        def _(h):
            run(h, "pool")


class Arena:
    BASE = 16640
    END = 229376

    def __init__(self, nc):
        self.nc = nc
        self.off = self.BASE
        self.n = 0

    def alloc(self, shape, dtype):
        nbytes = int(np.prod(shape[1:])) * (4 if dtype == F32 else 2)
        nbytes = (nbytes + 63) // 64 * 64
        assert self.off + nbytes <= self.END, ("SBUF overflow", self.off, nbytes)
        self.n += 1
        t = self.nc.alloc_sbuf_tensor_at("t%d" % self.n, list(shape), dtype, offset=self.off)
        self.off += nbytes
        return t

    def mark(self):
        return self.off

    def reset(self, m):
        self.off = m


def build_nc(debug=False):
    nc = bass.Bass("TRN2", target_bir_lowering=False)
    dt_ = nc.dram_tensor
    x_d = dt_("x", [S, D], F32, kind="ExternalInput")
    win_d = dt_("win", [8, 128, 3072], F32, kind="ExternalInput")
    wout_d = dt_("wout", [128, 8192], F32, kind="ExternalInput")
    wffn_d = dt_("wffn", [NFC, 128, 3072], F32, kind="ExternalInput")
    n1_d = dt_("n1", [1, D], F32, kind="ExternalInput")
    n2_d = dt_("n2", [1, D], F32, kind="ExternalInput")
    pv_d = dt_("pv", [128, 4], F32, kind="ExternalInput")
    lam_d = dt_("lam", [1, 256], F32, kind="ExternalInput")
    rb_d = dt_("rb", [32, 4], F32, kind="ExternalInput")
    cst_d = dt_("cst", [128, NCOL], F32, kind="ExternalInput")
    y_d = dt_("y", [S, D], F32, kind="ExternalOutput")
    kind_s = "ExternalOutput" if debug else "Internal"
    wi_s = dt_("wi_s", [8, 128, 3072], BF16, kind="Internal")
    wo_s = dt_("wo_s", [128, 8192], BF16, kind="Internal")
    wf_s = dt_("wf_s", [NFC, 128, 3072], BF16, kind="Internal")
    mix_s = dt_("mix_s", [8, 128, S], BF16, kind=kind_s)
    flat_s = dt_("flat_s", [4, 128, 768], F32, kind="Internal")

    stack = ExitStack()
    with stack:
        P = Prog(nc, stack)
        ps = stack.enter_context(nc.psum_tensor("ps", [128, 8, 512], F32))
        bank = [Buf() for _ in range(8)]

        def psT(b):
            return ps[:, b, :].bitcast(BF16)

        A = Arena(nc)
        dbg_list = []

        def dbg(name, ap, shape, dtype, reads):
            if not debug:
                return
            t = dt_(name, list(shape), dtype, kind="ExternalOutput")
            tk = P.dma(t.ap(), ap, reads=reads)
            P.out_toks.append(tk)
        cb = A.alloc([128, CBW], BF16)
        bhi = A.alloc([128, 4, 640], BF16)
        blo = A.alloc([128, 4, 640], BF16)
        pvt = A.alloc([128, 16], F32)
        b15 = A.alloc([128, 4], F32)
        B_cb, B_bband, B_pvt, B_b15 = Buf(), Buf(), Buf(), Buf()
        ident = cb[:, C_ID:C_ID + 128]
        ntri = cb[:, C_NTRI:C_NTRI + 128]
        ones_b = cb[:, C_ONES:C_ONES + 128]
        bd64 = cb[:, C_BD:C_BD + 128]
        nselA = cb[0:34, C_NSA:C_NSA + 128]
        nselB = cb[0:34, C_NSB:C_NSB + 128]
        EA = cb[:, C_EA:C_EA + 34]
        EB = cb[:, C_EB:C_EB + 34]
        trim = cb[:, C_TM:C_TM + 128]
        gq8 = pvt[:, 4:5]
        gk = pvt[:, 1:2]
        gdo8 = pvt[:, 5:6]
        gsb = pvt[:, 3:4]
        neglam = pvt[:, 6:7]
        m_conv = A.mark()
        st32 = [A.alloc([128, 3072], F32)]
        st16 = [A.alloc([128, 3072], BF16)]
        m_persist = A.mark()
        uT = A.alloc([128, 8, S], BF16)
        B_uT = [Buf() for _ in range(32)]
        m_p2 = A.mark()

        cst = A.alloc([128, NCOL], F32)
        bband = A.alloc([128, 4, 640], F32)
        B_bb32 = Buf()
        lamb = A.alloc([128, 256], F32)
        ltmp = A.alloc([128, 128], F32)
        rb32 = A.alloc([32, 4], F32)
        rbrep = A.alloc([32, 4, 128], F32)
        gb = [A.alloc([128, 768], F32) for _ in range(2)]
        B_cst, B_lamb, B_ltmp, B_rb = Buf(), Buf(), Buf(), Buf()
        B_g = [Buf(), Buf()]
        B_rbrep = Buf()
        B_flat = Buf()

        P.dma(cst[:, :], cst_d.ap(), writes=[B_cst])
        P.dma(pvt[:, 0:4], pv_d.ap(), writes=[B_pvt])
        P.dma(lamb[:, :], bass.AP(lam_d, 0, [[0, 128], [1, 256]]), writes=[B_lamb])
        P.dma(b15[:, :], bass.AP(rb_d, 15 * 4, [[0, 128], [1, 4]]), writes=[B_b15])
        P.dma(rb32[:, :], rb_d.ap(), writes=[B_rb])
        P.op("dve", lambda e: e.tensor_copy(out=cb[:, :], in_=cst[:, 0:CBW]), reads=[B_cst], writes=[B_cb])
        P.op("dve", lambda e: e.tensor_scalar(out=pvt[:, 4:5], in0=pvt[:, 0:1], scalar1=0.125, scalar2=None, op0=ALU.mult),
             reads=[], writes=[B_pvt])
        P.op("dve", lambda e: e.tensor_scalar(out=pvt[:, 5:6], in0=pvt[:, 2:3], scalar1=0.8, scalar2=None, op0=ALU.mult),
             reads=[], writes=[B_pvt])
        P.op("dve", lambda e: e.tensor_tensor(out=ltmp[:, 0:64], in0=lamb[:, 0:64], in1=lamb[:, 64:128], op=ALU.mult),
             reads=[B_lamb], writes=[B_ltmp])
        P.op("dve", lambda e: e.tensor_tensor(out=ltmp[:, 64:128], in0=lamb[:, 128:192], in1=lamb[:, 192:256], op=ALU.mult),
             reads=[B_lamb], writes=[B_ltmp])
        P.op("dve", lambda e: e.reduce_sum(out=pvt[:, 9:10], in_=ltmp[:, 0:64], axis=AX.X), reads=[B_ltmp], writes=[B_pvt])
        P.op("dve", lambda e: e.reduce_sum(out=pvt[:, 10:11], in_=ltmp[:, 64:128], axis=AX.X), reads=[B_ltmp], writes=[B_pvt])
        P.op("act", lambda e: e.activation(out=pvt[:, 7:9], in_=pvt[:, 9:11], func=AF.Exp), reads=[B_pvt], writes=[B_pvt])
        P.op("dve", lambda e: e.tensor_tensor(out=pvt[:, 6:7], in0=pvt[:, 8:9], in1=pvt[:, 7:8], op=ALU.subtract),
             reads=[B_pvt], writes=[B_pvt])
        P.op("dve", lambda e: e.tensor_scalar(out=pvt[:, 6:7], in0=pvt[:, 6:7], scalar1=-0.2, scalar2=None, op0=ALU.add),
             reads=[B_pvt], writes=[B_pvt])
        A.reset(A.mark())

        st32.append(A.alloc([128, 3072], F32))
        st16.append(A.alloc([128, 3072], BF16))
        B_st32 = [Buf(), Buf()]
        B_st16 = [Buf(), Buf()]
        B_wi = [Buf() for _ in range(8)]
        B_wo = Buf()
        B_wf = [Buf() for _ in range(NFC)]
        slabs = []
        for g in GROUP_ORDER:
            slabs.append((win_d.ap()[g], wi_s.ap()[g], B_wi[g], 3072))
        for c, (lo, hi) in enumerate([(0, 3072), (3072, 6144), (6144, 8192)]):
            slabs.append((wout_d.ap()[:, lo:hi], wo_s.ap()[:, lo:hi], B_wo, hi - lo))
        for fc in range(NFC):
            slabs.append((wffn_d.ap()[fc], wf_s.ap()[fc], B_wf[fc], 3072))
        conv_state = {"i": 0, "pending": None, "npair": 2}

        def conv_flush():
            pend = conv_state["pending"]
            if pend is None:
                return
            conv_state["pending"] = None
            dst, bdst, w, k = pend
            old_wr = dict(bdst.wr)
            P.dma(dst, st16[k][:, 0:w], reads=[B_st16[k]], writes=[bdst])
            for kk, vv in old_wr.items():
                if kk.startswith("dma"):
                    bdst.wr[kk] = vv

        def emit_convert(n=1):
            for _ in range(n):
                i = conv_state["i"]
                if i >= len(slabs):
                    conv_flush()
                    return
                conv_state["i"] = i + 1
                src, dst, bdst, w = slabs[i]
                k = i % conv_state["npair"]
                if conv_state["npair"] == 1:
                    conv_flush()
                P.dma(st32[k][:, 0:w], src, writes=[B_st32[k]])
                if conv_state["npair"] == 2:
                    conv_flush()
                P.op("pool", lambda e, k=k, w=w: e.tensor_copy(out=st16[k][:, 0:w], in_=st32[k][:, 0:w]),
                     reads=[B_st32[k]], writes=[B_st16[k]])
                conv_state["pending"] = (dst, bdst, w, k)

        w1b = A.alloc([128, D], F32)
        B_w1b = Buf()
        xb = [A.alloc([128, D], F32) for _ in range(3)]
        B_xb = [Buf() for _ in range(3)]
        xn = [A.alloc([128, D], BF16) for _ in range(2)]
        B_xn = [Buf(), Buf()]
        junk = A.alloc([128, D], BF16)
        B_junk = Buf()
        st1 = A.alloc([128, 8], F32)
        B_st1 = [Buf(), Buf()]
        P.dma(w1b[:, :], bass.AP(n1_d, 0, [[0, 128], [1, D]]), writes=[B_w1b])

        def norm_block(src_ap, B_src, wbt, B_wbt, xn_t, B_xn_t, stt, B_stt, col):
            P.op("act", lambda e: e.activation(out=junk[:, :], in_=src_ap, func=AF.Square, accum_out=stt[:, col:col + 1]),
                 reads=[B_src], writes=[B_junk, B_stt])
            P.op("act", lambda e: e.activation(out=stt[:, col + 1:col + 2], in_=stt[:, col:col + 1], func=AF.Ln, scale=1.0 / D, bias=EPS),
                 reads=[B_stt], writes=[B_stt])
            P.op("act", lambda e: e.activation(out=stt[:, col + 2:col + 3], in_=stt[:, col + 1:col + 2], func=AF.Exp, scale=-0.5),
                 reads=[B_stt], writes=[B_stt])
            P.op("dve", lambda e: e.scalar_tensor_tensor(out=xn_t[:, :], in0=src_ap, scalar=stt[:, col + 2:col + 3], in1=wbt[:, :],
                                                          op0=ALU.mult, op1=ALU.mult),
                 reads=[B_src, B_stt, B_wbt], writes=[B_xn_t])

        def transpose_block(xn_t, B_xn_t, bk, dst_ap, B_dst, evac_eng):
            pv = psT(bk)
            for kc in range(8):
                P.op("pe", lambda e, kc=kc: e.transpose(out=pv[:, kc * 128:(kc + 1) * 128], in_=xn_t[:, kc * 128:(kc + 1) * 128], identity=ident),
                     reads=[B_xn_t, B_cb], writes=[bank[bk]])
            src = pv.rearrange("p (k t) -> p k t", k=8)
            if evac_eng == "act":
                P.op("act", lambda e: e.copy(out=dst_ap, in_=src), reads=[bank[bk]], writes=[B_dst])
            else:
                P.op("dve", lambda e: e.tensor_copy(out=dst_ap, in_=src), reads=[bank[bk]], writes=[B_dst])

        xa = x_d.ap()

        def p1_A(tb):
            k3 = tb % 3
            k2 = tb % 2
            P.dma(xb[k3][:, :], xa[tb * 128:(tb + 1) * 128, :], writes=[B_xb[k3]])
            if tb < 2:
                emit_convert(1)
            norm_block(xb[k3][:, :], B_xb[k3], w1b, B_w1b, xn[k2], B_xn[k2], st1, B_st1[k2], 4 * k2)

        def p1_B(tb):
            k2 = tb % 2
            transpose_block(xn[k2], B_xn[k2], 6 + k2, uT[:, :, tb * 128:(tb + 1) * 128], B_uT[tb], "dve")

        for tb in range(33):
            if tb < 32:
                p1_A(tb)
            if tb >= 1:
                p1_B(tb - 1)
        for h in range(4):
            P.op("dve", lambda e, h=h: e.tensor_scalar(out=rbrep[:, h, :], in0=cst[0:32, C_ONES:C_ONES + 128], scalar1=rb32[:, h:h + 1], scalar2=None, op0=ALU.mult),
                 reads=[B_cst, B_rb], writes=[B_rbrep])
        for h in range(4):
            k = h % 2
            P.op("pe", lambda e, h=h: e.matmul(ps[:, 0, :], lhsT=rbrep[:, h, :], rhs=cst[0:32, C_OH:C_OH + 512], start=True, stop=True),
                 reads=[B_rbrep, B_cst], writes=[bank[0]])
            P.op("pe", lambda e, h=h: e.matmul(ps[:, 1, 0:256], lhsT=rbrep[:, h, :], rhs=cst[0:32, C_OH + 512:C_OH + 768], start=True, stop=True),
                 reads=[B_rbrep, B_cst], writes=[bank[1]])
            P.op("dve", lambda e, k=k: e.tensor_copy(out=gb[k][:, 0:512], in_=ps[:, 0, :]), reads=[bank[0]], writes=[B_g[k]])
            P.op("dve", lambda e, k=k: e.tensor_copy(out=gb[k][:, 512:768], in_=ps[:, 1, 0:256]), reads=[bank[1]], writes=[B_g[k]])
            P.dma(flat_s.ap()[h], gb[k][:, :], reads=[B_g[k]], writes=[B_flat])
            P.dma(bband[:, h, :], bass.AP(flat_s, h * 128 * 768 + 127, [[767, 128], [1, 640]]), reads=[B_flat], writes=[B_bb32])
            P.op("dve", lambda e, h=h: e.tensor_tensor(out=bband[:, h, :], in0=bband[:, h, :], in1=cst[:, C_MNEG:C_MNEG + 640], op=ALU.add),
                 reads=[B_cst], writes=[B_bb32])
            P.op("dve", lambda e, h=h: e.tensor_copy(out=bhi[:, h, :], in_=bband[:, h, :]), reads=[B_bb32], writes=[B_bband])
            P.op("dve", lambda e, h=h: e.tensor_tensor(out=blo[:, h, :], in0=bband[:, h, :], in1=bhi[:, h, :], op=ALU.subtract),
                 reads=[B_bb32], writes=[B_bband])
        conv_flush()
        conv_state["npair"] = 1
        dbg("dbg_uT", uT[:, :, :], [128, 8, S], BF16, B_uT)

        A.reset(m_p2)
        qT = [A.alloc([128, S], BF16) for _ in range(2)]
        kT = [A.alloc([128, S], BF16) for _ in range(2)]
        Vt = [A.alloc([128, 32, 128], BF16) for _ in range(2)]
        B_qT = [[Buf() for _ in range(8)] for _ in range(2)]
        B_kT = [[Buf() for _ in range(8)] for _ in range(2)]
        B_V = [[Buf() for _ in range(8)] for _ in range(2)]
        wsl = [A.alloc([128, 8, 384], BF16) for _ in range(2)]
        B_wsl = [Buf(), Buf()]
        esb = [A.alloc([128, 2, 512], F32) for _ in range(2)]
        B_esb = [Buf(), Buf()]
        spb = [A.alloc([128, 2, 512], BF16) for _ in range(3)]
        B_spb = [Buf() for _ in range(3)]
        Ab = [A.alloc([128, 2, 512], BF16) for _ in range(2)]
        B_Ab = [Buf(), Buf()]
        c32 = A.alloc([34, 512], F32)
        HL = A.alloc([34, 512], BF16)
        B_c32, B_HL = Buf(), Buf()
        sqb = [A.alloc([128, 512], BF16) for _ in range(2)]
        B_sqb = [Buf(), Buf()]
        rsb = [A.alloc([128, 512], F32) for _ in range(2)]
        B_rsb = [Buf(), Buf()]
        pp = [A.alloc([128, 512], F32) for _ in range(5)]
        B_pp = [Buf() for _ in range(5)]
        mt = [A.alloc([128, 512], BF16) for _ in range(2)]
        B_mt = [Buf(), Buf()]
        B_mix = [[Buf() for _ in range(8)] for _ in range(8)]
        p1_bufs = [B_w1b, B_junk] + B_xb + B_xn + B_st1 + [B_cst, B_lamb, B_ltmp, B_rb, B_rbrep, B_bb32, B_st32[1], B_st16[1]] + B_g
        p2_first = {"done": False}

        def p2_guard():
            return p1_bufs if not p2_first["done"] else []

        pbank = {"i": 0}

        def next_bank(cands):
            b = cands[pbank["i"] % len(cands)]
            pbank["i"] += 1
            return b

        def load_w(g, sl):
            guard = p2_guard()
            p2_first["done"] = True
            P.dma(wsl[sl][:, :, :].rearrange("p k c -> p (k c)"), wi_s.ap()[g], reads=[B_wi[g]], writes=[B_wsl[sl]] + guard)

        def project(g, sl):
            is_diff = g < 4
            for tt in range(8):
                tsl = slice(tt * 512, (tt + 1) * 512)
                for which in range(2):
                    bk = next_bank([0, 1, 2, 3])
                    dstT, B_dst = (qT, B_qT) if which == 0 else (kT, B_kT)
                    off = which * 128
                    for kc in range(8):
                        P.op("pe", lambda e, bk=bk, kc=kc, off=off, tsl=tsl: e.matmul(ps[:, bk, :], lhsT=wsl[sl][:, kc, off:off + 128], rhs=uT[:, kc, tsl],
                                                                                   start=(kc == 0), stop=(kc == 7)),
                             reads=[B_wsl[sl]] + B_uT[tt * 4:tt * 4 + 4], writes=[bank[bk]])
                    dst_ap = dstT[sl][:, tsl]
                    if not is_diff:
                        if which == 0:
                            P.op("act", lambda e, bk=bk, dst_ap=dst_ap: e.mul(dst_ap, ps[:, bk, :], 0.125),
                                 reads=[bank[bk]], writes=[B_dst[sl][tt]])
                        else:
                            P.op("dve", lambda e, bk=bk, dst_ap=dst_ap: e.tensor_copy(out=dst_ap, in_=ps[:, bk, :]),
                                 reads=[bank[bk]], writes=[B_dst[sl][tt]])
                    else:
                        k2 = (tt * 2 + which) % 2
                        b2 = 4 + k2
                        P.op("act", lambda e, bk=bk, k2=k2: e.activation(out=sqb[k2][:, :], in_=ps[:, bk, :], func=AF.Square),
                             reads=[bank[bk]], writes=[B_sqb[k2]])
                        P.op("pe", lambda e, b2=b2, k2=k2: e.matmul(ps[:, b2, :], lhsT=bd64, rhs=sqb[k2][:, :], start=True, stop=True),
                             reads=[B_sqb[k2], B_cb], writes=[bank[b2]])
                        P.op("act", lambda e, b2=b2, k2=k2: e.activation(out=rsb[k2][:, :], in_=ps[:, b2, :], func=AF.Ln, scale=1.0 / 64, bias=EPS),
                             reads=[bank[b2]], writes=[B_rsb[k2]])
                        P.op("act", lambda e, k2=k2: e.activation(out=rsb[k2][:, :], in_=rsb[k2][:, :], func=AF.Exp, scale=-0.5),
                             reads=[B_rsb[k2]], writes=[B_rsb[k2]])
                        gsc = gq8 if which == 0 else gk
                        P.op("dve", lambda e, bk=bk, k2=k2, dst_ap=dst_ap, gsc=gsc: e.scalar_tensor_tensor(
                            out=dst_ap, in0=ps[:, bk, :], scalar=gsc, in1=rsb[k2][:, :], op0=ALU.mult, op1=ALU.mult),
                             reads=[bank[bk], B_rsb[k2], B_pvt], writes=[B_dst[sl][tt]])
                bk = next_bank([6, 7])
                for i4 in range(4):
                    tb = tt * 4 + i4
                    for kc in range(8):
                        P.op("pe", lambda e, bk=bk, kc=kc, tb=tb, i4=i4: e.matmul(ps[:, bk, i4 * 128:(i4 + 1) * 128],
                                                                                 lhsT=uT[:, kc, tb * 128:(tb + 1) * 128], rhs=wsl[sl][:, kc, 256:384],
                                                                                 start=(kc == 0), stop=(kc == 7)),
                             reads=[B_wsl[sl], B_uT[tb]], writes=[bank[bk]])
                veng = "dve" if (tt % 2 == 0) else "act"
                vdst = Vt[sl][:, tt * 4:(tt + 1) * 4, :]
                vsrc = ps[:, bk, :].rearrange("p (a b) -> p a b", a=4)
                if veng == "dve":
                    P.op("dve", lambda e, vdst=vdst, vsrc=vsrc: e.tensor_copy(out=vdst, in_=vsrc), reads=[bank[bk]], writes=[B_V[sl][tt]])
                else:
                    P.op("act", lambda e, vdst=vdst, vsrc=vsrc: e.copy(out=vdst, in_=vsrc), reads=[bank[bk]], writes=[B_V[sl][tt]])

        def project_units(g, sl):
            is_diff = g < 4
            units = []

            def mk_qk(tt, which):
                def unit(bA, bB):
                    tsl = slice(tt * 512, (tt + 1) * 512)
                    dstT, B_dst = (qT, B_qT) if which == 0 else (kT, B_kT)
                    off = which * 128
                    for kc in range(8):
                        P.op("pe", lambda e, kc=kc: e.matmul(ps[:, bA, :], lhsT=wsl[sl][:, kc, off:off + 128], rhs=uT[:, kc, tsl],
                                                             start=(kc == 0), stop=(kc == 7)),
                             reads=[B_wsl[sl]] + B_uT[tt * 4:tt * 4 + 4], writes=[bank[bA]])
                    dst_ap = dstT[sl][:, tsl]
                    if not is_diff:
                        if which == 0:
                            P.op("dve", lambda e: e.tensor_scalar(out=dst_ap, in0=ps[:, bA, :], scalar1=0.125, scalar2=None, op0=ALU.mult),
                                 reads=[bank[bA]], writes=[B_dst[sl][tt]])
                        else:
                            P.op("dve", lambda e: e.tensor_copy(out=dst_ap, in_=ps[:, bA, :]), reads=[bank[bA]], writes=[B_dst[sl][tt]])
                    else:
                        k2 = (tt * 2 + which) % 2
                        P.op("act", lambda e: e.activation(out=sqb[k2][:, :], in_=ps[:, bA, :], func=AF.Square),
                             reads=[bank[bA]], writes=[B_sqb[k2]])
                        P.op("pe", lambda e: e.matmul(ps[:, bB, :], lhsT=bd64, rhs=sqb[k2][:, :], start=True, stop=True),
                             reads=[B_sqb[k2], B_cb], writes=[bank[bB]])
                        P.op("act", lambda e: e.activation(out=rsb[k2][:, :], in_=ps[:, bB, :], func=AF.Ln, scale=1.0 / 64, bias=EPS),
                             reads=[bank[bB]], writes=[B_rsb[k2]])
                        P.op("act", lambda e: e.activation(out=rsb[k2][:, :], in_=rsb[k2][:, :], func=AF.Exp, scale=-0.5),
                             reads=[B_rsb[k2]], writes=[B_rsb[k2]])
                        gsc = gq8 if which == 0 else gk
                        P.op("dve", lambda e: e.scalar_tensor_tensor(out=dst_ap, in0=ps[:, bA, :], scalar=gsc, in1=rsb[k2][:, :], op0=ALU.mult, op1=ALU.mult),
                             reads=[bank[bA], B_rsb[k2], B_pvt], writes=[B_dst[sl][tt]])
                return unit

            def mk_v(tt, half):
                def unit(bA, bB):
                    for i2 in range(2):
                        tb = tt * 4 + half * 2 + i2
                        for kc in range(8):
                            P.op("pe", lambda e, kc=kc, tb=tb, i2=i2: e.matmul(ps[:, bA, i2 * 128:(i2 + 1) * 128],
                                                                                lhsT=uT[:, kc, tb * 128:(tb + 1) * 128], rhs=wsl[sl][:, kc, 256:384],
                                                                                start=(kc == 0), stop=(kc == 7)),
                                 reads=[B_wsl[sl], B_uT[tb]], writes=[bank[bA]])
                    vdst = Vt[sl][:, tt * 4 + half * 2:tt * 4 + half * 2 + 2, :]
                    vsrc = ps[:, bA, 0:256].rearrange("p (a b) -> p a b", a=2)
                    P.op("dve", lambda e: e.tensor_copy(out=vdst, in_=vsrc), reads=[bank[bA]], writes=[B_V[sl][tt]])
                return unit

            for tt in range(8):
                units.append(mk_qk(tt, 0))
                units.append(mk_qk(tt, 1))
                units.append(mk_v(tt, 0))
                units.append(mk_v(tt, 1))
            return units

        def step_geom(t, j):
            m = j - 4 * t
            if m >= 0:
                return 128 * m, 512 - 128 * m, "diag", 0
            if m == -1:
                return 0, 512, "near", 128
            return 0, 512, "far", 0

        def out_norm_store(g, t, y_ap, y_bufs, gain_ap, lhs_ones, div, ssb):
            k2 = t % 2
            P.op("act", lambda e: e.activation(out=sqb[k2][:, :], in_=y_ap, func=AF.Square), reads=y_bufs, writes=[B_sqb[k2]])
            P.op("pe", lambda e: e.matmul(ps[:, ssb, :], lhsT=lhs_ones, rhs=sqb[k2][:, :], start=True, stop=True),
                 reads=[B_sqb[k2], B_cb], writes=[bank[ssb]])
            P.op("act", lambda e: e.activation(out=rsb[k2][:, :], in_=ps[:, ssb, :], func=AF.Ln, scale=1.0 / div, bias=EPS),
                 reads=[bank[ssb]], writes=[B_rsb[k2]])
            P.op("act", lambda e: e.activation(out=rsb[k2][:, :], in_=rsb[k2][:, :], func=AF.Exp, scale=-0.5),
                 reads=[B_rsb[k2]], writes=[B_rsb[k2]])
            P.op("dve", lambda e: e.scalar_tensor_tensor(out=mt[k2][:, :], in0=y_ap, scalar=gain_ap, in1=rsb[k2][:, :], op0=ALU.mult, op1=ALU.mult),
                 reads=y_bufs + [B_rsb[k2], B_pvt], writes=[B_mt[k2]])
            P.dma(mix_s.ap()[g, :, t * 512:(t + 1) * 512], mt[k2][:, :], reads=[B_mt[k2]], writes=[B_mix[g][t]])
            emit_convert(1)

        def attn_diff(g, sl):
            h = g
            deferred = []
            for t in range(8):
                ns = 4 * t + 4
                qbuf = [B_qT[sl][t]]

                def Z(s):
                    j = 4 * t + 3 - s
                    qlo, N, kind, u0 = step_geom(t, j)
                    st_ = (s % 2) * 2
                    ksl = slice(j * 128, (j + 1) * 128)
                    qsl = slice(t * 512 + qlo, (t + 1) * 512)
                    far = (kind == "far")
                    for c in range(2):
                        P.op("pe", lambda e, c=c: e.matmul(ps[:, st_ + c, qlo:512], lhsT=kT[sl][c * 64:(c + 1) * 64, ksl], rhs=qT[sl][c * 64:(c + 1) * 64, qsl],
                                                           start=True, stop=far),
                             reads=qbuf + [B_kT[sl][j // 4]], writes=[bank[st_ + c]])
                    if not far:
                        for c in range(2):
                            P.op("pe", lambda e, c=c: e.matmul(ps[:, st_ + c, qlo:512], lhsT=ident, rhs=bhi[:, h, u0:u0 + N], start=False, stop=False),
                                 reads=[B_bband, B_cb], writes=[bank[st_ + c]])
                            P.op("pe", lambda e, c=c: e.matmul(ps[:, st_ + c, qlo:512], lhsT=ident, rhs=blo[:, h, u0:u0 + N], start=False, stop=True),
                                 reads=[B_bband, B_cb], writes=[bank[st_ + c]])

                def E(s):
                    j = 4 * t + 3 - s
                    qlo, N, kind, u0 = step_geom(t, j)
                    st_ = (s % 2) * 2
                    k2 = s % 2
                    if kind == "far":
                        P.op("act", lambda e: e.activation(out=Ab[k2][:, :, qlo:512], in_=ps[:, st_:st_ + 2, qlo:512], func=AF.Exp, bias=b15[:, h:h + 1]),
                             reads=[bank[st_], bank[st_ + 1], B_b15], writes=[B_Ab[k2]])
                    else:
                        P.op("act", lambda e: e.activation(out=Ab[k2][:, :, qlo:512], in_=ps[:, st_:st_ + 2, qlo:512], func=AF.Exp),
                             reads=[bank[st_], bank[st_ + 1]], writes=[B_Ab[k2]])

                def PV(s):
                    j = 4 * t + 3 - s
                    qlo, N, kind, u0 = step_geom(t, j)
                    k2 = s % 2
                    st0 = (s == 0)
                    sp0 = (s == ns - 1)
                    for c in range(2):
                        P.op("pe", lambda e, c=c: e.matmul(ps[:, 4 + c, qlo:512], lhsT=Vt[sl][:, j, :], rhs=Ab[k2][:, c, qlo:512], start=st0, stop=sp0,
                                                           skip_group_check=True),
                             reads=[B_Ab[k2], B_V[sl][j // 4]], writes=[bank[4 + c]])
                        P.op("pe", lambda e, c=c: e.matmul(ps[:, 6 + c, qlo:512], lhsT=ones_b, rhs=Ab[k2][:, c, qlo:512], start=st0, stop=sp0,
                                                           skip_group_check=True),
                             reads=[B_Ab[k2], B_cb], writes=[bank[6 + c]])

                for it in range(-1, ns):
                    if it + 1 < ns:
                        Z(it + 1)
                        E(it + 1)
                    if it >= 0:
                        PV(it)
                    if it == 1 and deferred:
                        deferred.pop(0)()
                for c in range(2):
                    P.op("dve", lambda e, c=c: e.tensor_copy(out=pp[2 + c][:, :], in_=ps[:, 4 + c, :]), reads=[bank[4 + c]], writes=[B_pp[2 + c]])
                    P.op("act", lambda e, c=c: e.activation(out=pp[c][:, :], in_=ps[:, 6 + c, :], func=AF.Ln), reads=[bank[6 + c]], writes=[B_pp[c]])

                def stage2(t=t):
                    for c in range(2):
                        P.op("act", lambda e, c=c: e.activation(out=pp[c][:, :], in_=pp[c][:, :], func=AF.Exp, scale=-1.0), reads=[B_pp[c]], writes=[B_pp[c]])
                        P.op("dve", lambda e, c=c: e.tensor_tensor(out=pp[2 + c][:, :], in0=pp[2 + c][:, :], in1=pp[c][:, :], op=ALU.mult),
                             reads=[B_pp[c]], writes=[B_pp[2 + c]])
                    P.op("dve", lambda e: e.scalar_tensor_tensor(out=pp[4][:, :], in0=pp[3][:, :], scalar=neglam, in1=pp[2][:, :], op0=ALU.mult, op1=ALU.add),
                         reads=[B_pp[2], B_pp[3], B_pvt], writes=[B_pp[4]])
                    out_norm_store(g, t, pp[4][:, :], [B_pp[4]], gdo8, ones_b, 128.0, 0)

                deferred.append(stage2)
                if t == 7:
                    deferred.pop(0)()

        def attn_sb(g, sl, hosted=None):
            deferred = []
            hosted = hosted if hosted is not None else []
            zc = {"n": 0}
            hcount = {"n": 0}
            for t in range(8):
                ns = 4 * t + 4
                qbuf = [B_qT[sl][t]]
                P.op("dve", lambda e: e.memset(c32[:, :], 0.0), writes=[B_c32])
                P.op("dve", lambda e: e.memset(HL[:, :], 0.0), writes=[B_HL])

                def geo(s):
                    j = 4 * t + 3 - s
                    m = j - 4 * t
                    qlo = 128 * m if m >= 0 else 0
                    return j, m, qlo

                zset = {}

                def Astage(s):
                    j, m, qlo = geo(s)
                    zset[s] = (zc["n"] % 3) * 2
                    zc["n"] += 1
                    st_ = zset[s]
                    ksl = slice(j * 128, (j + 1) * 128)
                    qsl = slice(t * 512 + qlo, (t + 1) * 512)
                    for c in range(2):
                        P.op("pe", lambda e, c=c: e.matmul(ps[:, st_ + c, qlo:512], lhsT=kT[sl][c * 64:(c + 1) * 64, ksl], rhs=qT[sl][c * 64:(c + 1) * 64, qsl],
                                                           start=True, stop=True),
                             reads=qbuf + [B_kT[sl][j // 4]], writes=[bank[st_ + c]])
                    k2 = s % 2
                    k3 = s % 3
                    P.op("act", lambda e: e.activation(out=esb[k2][:, :, qlo:512], in_=ps[:, st_:st_ + 2, qlo:512], func=AF.Exp),
                         reads=[bank[st_], bank[st_ + 1]], writes=[B_esb[k2]])
                    P.op("act", lambda e: e.activation(out=spb[k3][:, :, qlo:512], in_=esb[k2][:, :, qlo:512], func=AF.Ln, bias=1.0),
                         reads=[B_esb[k2]], writes=[B_spb[k3]])
                    if m >= 0:
                        for c in range(2):
                            P.op("dve", lambda e, c=c: e.tensor_tensor(out=spb[k3][:, c, qlo:qlo + 128], in0=spb[k3][:, c, qlo:qlo + 128], in1=trim, op=ALU.mult),
                                 reads=[B_cb], writes=[B_spb[k3]])

                def Bstage(s):
                    j, m, qlo = geo(s)
                    st_ = zset[s]
                    k3 = s % 3
                    last = (s == 0)
                    for c in range(2):
                        P.op("pe", lambda e, c=c: e.matmul(ps[:, st_ + c, qlo:512], lhsT=ntri, rhs=spb[k3][:, c, qlo:512], start=False, stop=last, skip_group_check=True),
                             reads=[B_spb[k3], B_cb], writes=[bank[st_ + c]])
                    if s > 0:
                        for c in range(2):
                            P.op("pe", lambda e, c=c: e.matmul(ps[:, st_ + c, qlo:512], lhsT=(nselA if c == 0 else nselB), rhs=HL[0:34, qlo:512], start=False, stop=True, skip_group_check=True),
                                 reads=[B_HL, B_cb], writes=[bank[st_ + c]])
                    if s < ns - 1:
                        P.op("pe", lambda e: e.matmul(ps[0:34, 7, qlo:512], lhsT=EA, rhs=spb[k3][:, 0, qlo:512], start=True, stop=False),
                             reads=[B_spb[k3], B_cb], writes=[bank[7]])
                        P.op("pe", lambda e: e.matmul(ps[0:34, 7, qlo:512], lhsT=EB, rhs=spb[k3][:, 1, qlo:512], start=False, stop=True),
                             reads=[B_spb[k3], B_cb], writes=[bank[7]])
                        P.op("dve", lambda e: e.tensor_tensor(out=c32[:, qlo:512], in0=c32[:, qlo:512], in1=ps[0:34, 7, qlo:512], op=ALU.add),
                             reads=[bank[7]], writes=[B_c32])
                        P.op("dve", lambda e: e.tensor_copy(out=HL[:, qlo:512], in_=c32[:, qlo:512]), reads=[B_c32], writes=[B_HL])
                        P.op("dve", lambda e: e.tensor_tensor(out=HL[32:34, qlo:512], in0=c32[32:34, qlo:512], in1=HL[32:34, qlo:512], op=ALU.subtract),
                             reads=[B_c32], writes=[B_HL])

                def E2(s):
                    j, m, qlo = geo(s)
                    st_ = zset[s]
                    k2 = s % 2
                    P.op("act", lambda e: e.activation(out=Ab[k2][:, :, qlo:512], in_=ps[:, st_:st_ + 2, qlo:512], func=AF.Exp),
                         reads=[bank[st_], bank[st_ + 1]], writes=[B_Ab[k2]])
                    if m >= 0:
                        for c in range(2):
                            P.op("dve", lambda e, c=c: e.tensor_tensor(out=Ab[k2][:, c, qlo:qlo + 128], in0=Ab[k2][:, c, qlo:qlo + 128], in1=trim, op=ALU.mult),
                                 reads=[B_cb], writes=[B_Ab[k2]])

                def PV(s):
                    j, m, qlo = geo(s)
                    k2 = s % 2
                    st0 = (s == 0)
                    sp0 = (s == ns - 1)
                    for c in range(2):
                        P.op("pe", lambda e, c=c: e.matmul(ps[c * 64:(c + 1) * 64, 6, qlo:512], lhsT=Vt[sl][:, j, c * 64:(c + 1) * 64], rhs=Ab[k2][:, c, qlo:512],
                                                           start=st0, stop=sp0, skip_group_check=True),
                             reads=[B_Ab[k2], B_V[sl][j // 4]], writes=[bank[6]])

                for it in range(-2, ns):
                    if 0 <= it + 2 < ns:
                        Astage(it + 2)
                    if 0 <= it + 1 < ns:
                        Bstage(it + 1)
                        E2(it + 1)
                    if it >= 0:
                        PV(it)
                    if it == 1 and deferred:
                        deferred.pop(0)()
                    if hosted and it >= 0:
                        hcount["n"] += 1
                        if hcount["n"] % 3 == 0:
                            bA = (zc["n"] % 3) * 2
                            zc["n"] += 1
                            hosted.pop(0)(bA, bA + 1)
                P.op("dve", lambda e: e.tensor_copy(out=pp[4][:, :], in_=ps[:, 6, :]), reads=[bank[6]], writes=[B_pp[4]])
                deferred.append(lambda t=t: out_norm_store(g, t, pp[4][:, :], [B_pp[4]], gsb, bd64, 64.0, 7))
                if t == 7:
                    deferred.pop(0)()
            while hosted:
                bA = (zc["n"] % 3) * 2
                zc["n"] += 1
                hosted.pop(0)(bA, bA + 1)

        order = GROUP_ORDER
        load_w(order[0], 0)
        project(order[0], 0)
        for i, g in enumerate(order):
            sl = i % 2
            nxt = order[i + 1] if i + 1 < 8 else None
            if nxt is not None:
                load_w(nxt, (i + 1) % 2)
            if g >= 4:
                hosted = project_units(nxt, (i + 1) % 2) if nxt is not None else None
                attn_sb(g, sl, hosted)
            else:
                attn_diff(g, sl)
                if nxt is not None:
                    project(nxt, (i + 1) % 2)
        emit_convert(100)

        A.reset(m_conv)
        p2_bufs = ([B_wsl[0], B_wsl[1], B_c32, B_HL, B_st32[0], B_st16[0]] + B_esb + B_spb + B_Ab + B_sqb + B_rsb + B_pp + B_mt + B_uT
                   + [b for sl_ in range(2) for b in B_qT[sl_] + B_kT[sl_] + B_V[sl_]])
        wo = A.alloc([128, 8, D], BF16)
        B_wo_sb = Buf()
        w2b = A.alloc([128, D], F32)
        B_w2b = Buf()
        xh = [A.alloc([128, 4, D], F32) for _ in range(2)]
        B_xh = [[Buf() for _ in range(4)] for _ in range(2)]
        mxt = [A.alloc([128, 8, 512], BF16) for _ in range(2)]
        B_mxt = [Buf(), Buf()]
        u2 = [A.alloc([128, D], BF16) for _ in range(4)]
        B_u2 = [Buf() for _ in range(4)]
        u2T = [A.alloc([128, 8, 512], BF16) for _ in range(2)]
        B_u2T = [[Buf() for _ in range(4)] for _ in range(2)]
        actT = A.alloc([128, NFC, 512], BF16)
        B_actT = [Buf() for _ in range(NFC)]
        sg = [A.alloc([128, 512], F32) for _ in range(2)]
        B_sg = [Buf(), Buf()]
        wgu = [A.alloc([128, 2048], BF16) for _ in range(3)]
        B_wgu = [Buf() for _ in range(3)]
        wd = [A.alloc([128, NFC, 512], BF16) for _ in range(2)]
        B_wd = [[Buf(), Buf()], [Buf(), Buf()]]
        ot = [A.alloc([128, 512], F32) for _ in range(4)]
        B_ot = [Buf() for _ in range(4)]
        junk3 = A.alloc([128, D], BF16)
        st3 = A.alloc([128, 8], F32)
        B_st3 = [Buf(), Buf()]
        B_junk3 = Buf()

        def p3w(extra):
            return extra + p2_bufs

        P.dma(wo[:, :, :].rearrange("p k c -> p (k c)"), wo_s.ap(), reads=[B_wo], writes=p3w([B_wo_sb]))
        P.dma(w2b[:, :], bass.AP(n2_d, 0, [[0, 128], [1, D]]), writes=p3w([B_w2b]))
        first_p3 = {"f": True}

        def p3_load_x(tt):
            k = tt % 2
            extra = p2_bufs if tt < 2 else []
            P.dma(xh[k][:, :, :], xa[tt * 512:(tt + 1) * 512, :].rearrange("(a p) d -> p a d", p=128), writes=B_xh[k] + extra)

        def p3_load_m(tt):
            k = tt % 2
            extra = p2_bufs if tt < 2 else []
            P.dma(mxt[k][:, :, :], mix_s.ap()[:, :, tt * 512:(tt + 1) * 512].rearrange("e p t -> p e t"),
                  reads=[B_mix[g_][tt] for g_ in range(8)], writes=[B_mxt[k]] + extra)

        def p3_loads(tt):
            p3_load_x(tt)
            p3_load_m(tt)

        def load_wgu(tt, fc):
            k3 = fc % 3
            P.dma(wgu[k3][:, :], wf_s.ap()[fc, :, 0:2048], reads=[B_wf[fc]], writes=[B_wgu[k3]] + (p2_bufs if (tt == 0 and fc < 3) else []))

        def norm_block3(src_ap, B_src, xn_t, B_xn_t, col, B_stt):
            P.op("act", lambda e: e.activation(out=junk3[:, :], in_=src_ap, func=AF.Square, accum_out=st3[:, col:col + 1]),
                 reads=[B_src], writes=[B_junk3, B_stt])
            P.op("act", lambda e: e.activation(out=st3[:, col + 1:col + 2], in_=st3[:, col:col + 1], func=AF.Ln, scale=1.0 / D, bias=EPS),
                 reads=[B_stt], writes=[B_stt])
            P.op("act", lambda e: e.activation(out=st3[:, col + 2:col + 3], in_=st3[:, col + 1:col + 2], func=AF.Exp, scale=-0.5),
                 reads=[B_stt], writes=[B_stt])
            P.op("dve", lambda e: e.scalar_tensor_tensor(out=xn_t[:, :], in0=src_ap, scalar=st3[:, col + 2:col + 3], in1=w2b[:, :],
                                                          op0=ALU.mult, op1=ALU.mult),
                 reads=[B_src, B_stt, B_w2b], writes=[B_xn_t])

        ya = y_d.ap()

        def X1(tt):
            k = tt % 2
            for tb in range(4):
                for dh in range(2):
                    bk = 4 + (tb * 2 + dh) % 4
                    for ec in range(8):
                        P.op("pe", lambda e, bk=bk, ec=ec, tb=tb, dh=dh, k=k: e.matmul(ps[:, bk, :], lhsT=mxt[k][:, ec, tb * 128:(tb + 1) * 128],
                                                                                       rhs=wo[:, ec, dh * 512:(dh + 1) * 512], start=(ec == 0), stop=(ec == 7)),
                             reads=[B_mxt[k], B_wo_sb], writes=[bank[bk]])
                    P.op("dve", lambda e, bk=bk, tb=tb, dh=dh, k=k: e.tensor_tensor(out=xh[k][:, tb, dh * 512:(dh + 1) * 512], in0=ps[:, bk, :],
                                                                                      in1=xh[k][:, tb, dh * 512:(dh + 1) * 512], op=ALU.add),
                         reads=[bank[bk]], writes=[B_xh[k][tb]])
                norm_block3(xh[k][:, tb, :], B_xh[k][tb], u2[tb], B_u2[tb], 4 * (tb % 2), B_st3[tb % 2])

        def X2(tt):
            for tb in range(4):
                transpose_block(u2[tb], B_u2[tb], 6 + tb % 2, u2T[tt % 2][:, :, tb * 128:(tb + 1) * 128], B_u2T[tt % 2][tb], "dve")

        def Y1(tt):
            for fc in range(NFC):
                k3 = fc % 3
                if fc >= 3 or tt == 0:
                    load_wgu(tt, fc)
                if fc in (2, 6, 10, 14):
                    ci = (2, 6, 10, 14).index(fc)
                    dh_, c_ = ci // 2, ci % 2
                    P.dma(wd[dh_][:, 11 * c_:11 * c_ + 11, :],
                          wf_s.ap()[11 * c_:11 * c_ + 11, :, 2048 + dh_ * 512:2048 + (dh_ + 1) * 512].rearrange("f p d -> p f d"),
                          reads=B_wf[11 * c_:11 * c_ + 11], writes=[B_wd[dh_][c_]] + (p2_bufs if tt == 0 else []))
                bg = 4 + fc % 2
                bu = 6 + fc % 2
                ut = u2T[tt % 2]
                for kc in range(8):
                    P.op("pe", lambda e, kc=kc, k3=k3, bg=bg, ut=ut: e.matmul(ps[:, bg, :], lhsT=wgu[k3][:, kc * 128:(kc + 1) * 128], rhs=ut[:, kc, :],
                                                                                start=(kc == 0), stop=(kc == 7)),
                         reads=[B_wgu[k3]] + B_u2T[tt % 2], writes=[bank[bg]])
                for kc in range(8):
                    P.op("pe", lambda e, kc=kc, k3=k3, bu=bu, ut=ut: e.matmul(ps[:, bu, :], lhsT=wgu[k3][:, 1024 + kc * 128:1024 + (kc + 1) * 128], rhs=ut[:, kc, :],
                                                                                start=(kc == 0), stop=(kc == 7)),
                         reads=[B_wgu[k3]] + B_u2T[tt % 2], writes=[bank[bu]])
                s2 = fc % 2
                P.op("act", lambda e, s2=s2, bg=bg: e.activation(out=sg[s2][:, :], in_=ps[:, bg, :], func=AF.Silu),
                     reads=[bank[bg]], writes=[B_sg[s2]] + (p2_bufs if (tt == 0 and fc < 2) else []))
                P.op("dve", lambda e, s2=s2, bu=bu, fc=fc: e.tensor_tensor(out=actT[:, fc, :], in0=sg[s2][:, :], in1=ps[:, bu, :], op=ALU.mult),
                     reads=[B_sg[s2], bank[bu]], writes=[B_actT[fc]] + (p2_bufs if (tt == 0 and fc == 0) else []))

        def Y2(tt, dh):
            k = tt % 2
            for fc in range(NFC):
                for tb in range(4):
                    P.op("pe", lambda e, fc=fc, tb=tb, dh=dh: e.matmul(ps[:, tb, :], lhsT=actT[:, fc, tb * 128:(tb + 1) * 128], rhs=wd[dh][:, fc, :],
                                                                          start=(fc == 0), stop=(fc == NFC - 1)),
                         reads=[B_actT[fc], B_wd[dh][fc // 11]], writes=[bank[tb]])
            for tb in range(4):
                o = tb
                P.op("dve", lambda e, tb=tb, dh=dh, o=o, k=k: e.tensor_tensor(out=ot[o][:, :], in0=ps[:, tb, :], in1=xh[k][:, tb, dh * 512:(dh + 1) * 512], op=ALU.add),
                     reads=[bank[tb], B_xh[k][tb]], writes=[B_ot[o]] + (p2_bufs if (tt == 0 and dh == 0) else []))
                tk = P.dma(ya[tt * 512 + tb * 128:tt * 512 + (tb + 1) * 128, dh * 512:(dh + 1) * 512], ot[o][:, :], reads=[B_ot[o]])
                P.out_toks.append(tk)

        p3_loads(0)
        p3_loads(1)
        X1(0)
        X2(0)
        for tt in range(8):
            Y1(tt)
            if tt + 1 < 8:
                for fc_ in range(3):
                    load_wgu(tt + 1, fc_)
                X1(tt + 1)
            if tt + 2 < 8:
                p3_load_m(tt + 2)
            Y2(tt, 0)
            if tt + 1 < 8:
                X2(tt + 1)
            Y2(tt, 1)
            if tt + 2 < 8:
                p3_load_x(tt + 2)

        block = stack.enter_context(nc.Block())
        P.finalize(block)
    return nc


def _t5_bucket_np(rel):
    nb = 16
    max_exact = 8
    ret = (rel > 0).astype(np.int32) * nb
    n = np.abs(rel)
    nf = np.maximum(n, 1).astype(np.float32)
    large = max_exact + (np.log(nf / np.float32(max_exact)) / np.float32(math.log(128 / max_exact))
                         * np.float32(nb - max_exact)).astype(np.int32)
    large = np.minimum(large, nb - 1)
    return ret + np.where(n < max_exact, n, large)


def _const_table():
    c = np.zeros((128, NCOL), np.float32)
    i = np.arange(128)
    c[:, C_ID:C_ID + 128] = np.eye(128, dtype=np.float32)
    c[:, C_NTRI:C_NTRI + 128] = -(i[:, None] >= i[None, :]).astype(np.float32)
    c[:, C_ONES:C_ONES + 128] = 1.0
    c[:, C_BD:C_BD + 128] = ((i[:, None] // 64) == (i[None, :] // 64)).astype(np.float32)
    c[0, C_NSA:C_NSA + 128] = -1.0
    c[32, C_NSA:C_NSA + 128] = -1.0
    c[1, C_NSB:C_NSB + 128] = -1.0
    c[33, C_NSB:C_NSB + 128] = -1.0
    c[:, C_EA + 0] = 1.0
    c[:, C_EA + 32] = 1.0
    c[:, C_EB + 1] = 1.0
    c[:, C_EB + 33] = 1.0
    c[:, C_TM:C_TM + 128] = (i[:, None] < i[None, :]).astype(np.float32)
    u = np.arange(640)
    c[:, C_MNEG:C_MNEG + 640] = np.where((i[:, None] // 64) > (u[None, :] // 64), NEG, 0.0).astype(np.float32)
    s = np.arange(767)
    bk = _t5_bucket_np((127 - s).astype(np.int32))
    c[bk, C_OH + s] = 1.0
    return c


_NC_CACHE = {}


def _prep_shared(inp):
    w_in = np.asarray(inp["w_in"][0], np.float32)
    cols = []
    for g in range(8):
        base = 0 if g < 4 else 1536
        gi = g % 4
        cols.append(np.concatenate([np.arange(base + gi * 128, base + gi * 128 + 128),
                                    np.arange(base + 512 + gi * 128, base + 512 + gi * 128 + 128),
                                    np.arange(base + 1024 + gi * 128, base + 1024 + gi * 128 + 128)]))
    win = np.empty((8, 128, 3072), np.float32)
    w4 = w_in.reshape(8, 128, 3072)
    for g in range(8):
        win[g] = np.transpose(w4[:, :, cols[g]], (1, 0, 2)).reshape(128, 3072)
    wout = np.ascontiguousarray(np.transpose(np.asarray(inp["w_out"][0], np.float32).reshape(8, 128, 1024), (1, 0, 2)).reshape(128, 8192))
    wg = np.asarray(inp["w_gate"][0], np.float32).reshape(8, 128, NFC, 128)
    wu = np.asarray(inp["w_up"][0], np.float32).reshape(8, 128, NFC, 128)
    wdn = np.asarray(inp["w_down"][0], np.float32).reshape(NFC, 128, 1024)
    wffn = np.empty((NFC, 128, 3072), np.float32)
    wffn[:, :, 0:1024] = np.transpose(wg, (2, 1, 0, 3)).reshape(NFC, 128, 1024)
    wffn[:, :, 1024:2048] = np.transpose(wu, (2, 1, 0, 3)).reshape(NFC, 128, 1024)
    wffn[:, :, 2048:3072] = wdn
    p = np.arange(128)
    pv = np.stack([np.asarray(inp["q_norm_w"][0])[p % 64], np.asarray(inp["k_norm_w"][0])[p % 64],
                   np.asarray(inp["diff_out_norm_w"][0])[p], np.asarray(inp["sb_out_norm_w"][0])[p % 64]], axis=1).astype(np.float32)
    lam = np.concatenate([np.asarray(inp["lambda_q1"][0]), np.asarray(inp["lambda_k1"][0]),
                          np.asarray(inp["lambda_q2"][0]), np.asarray(inp["lambda_k2"][0])]).astype(np.float32)[None, :]
    return {
        "win": win, "wout": wout, "wffn": wffn,
        "n1": np.asarray(inp["norm1_w"], np.float32).reshape(1, D),
        "n2": np.asarray(inp["norm2_w"], np.float32).reshape(1, D),
        "pv": np.ascontiguousarray(pv), "lam": np.ascontiguousarray(lam),
        "rb": np.ascontiguousarray(np.asarray(inp["rel_bias"], np.float32)),
        "cst": _const_table(),
    }


def kernel(**inputs):
    x = np.asarray(inputs["x"], np.float32)
    nb = x.shape[0]
    shared = _prep_shared(inputs)
    if "nc" not in _NC_CACHE:
        _NC_CACHE["nc"] = build_nc()
    nc = _NC_CACHE["nc"]
    in_maps = []
    for b in range(nb):
        m = dict(shared)
        m["x"] = np.ascontiguousarray(x[b])
        in_maps.append(m)
    res = run_bass_kernel_spmd(nc, in_maps, core_ids=list(range(nb)))
    return np.stack([np.asarray(r["y"], np.float32) for r in res.results], axis=0)
```

```python
import math
from contextlib import ExitStack

import numpy as np
import concourse.bass as bass
import concourse.mybir as mybir
from concourse.bass_utils import run_bass_kernel_spmd

F32 = mybir.dt.float32
BF16 = mybir.dt.bfloat16
AF = mybir.ActivationFunctionType
ALU = mybir.AluOpType
AX = mybir.AxisListType

S = 4096
D = 1024
DFF = 2816
NFC = DFF // 128
EPS = 1e-6
NEG = -30000.0
SELF_SYNC = True
GROUP_ORDER = [4, 5, 6, 7, 0, 1, 2, 3]

C_ID = 0
C_NTRI = 128
C_ONES = 256
C_BD = 384
C_NSA = 512
C_NSB = 640
C_EA = 768
C_EB = 832
C_TM = 896
CBW = 1024
C_MNEG = 1024
C_OH = 1664
NCOL = 2432


class Tok:
    __slots__ = ("eng", "needed", "val", "sem")

    def __init__(self, eng):
        self.eng = eng
        self.needed = False
        self.val = None
        self.sem = None


class Buf:
    __slots__ = ("wr", "rd")

    def __init__(self):
        self.wr = {}
        self.rd = {}


class Prog:
    ENG = ("sp", "act", "pe", "dve", "pool")

    def __init__(self, nc, stack, ndma=32):
        self.nc = nc
        self.ops = {e: [] for e in self.ENG}
        self.esem = {e: stack.enter_context(nc.semaphore("s_" + e)) for e in self.ENG}
        self.dsem = [stack.enter_context(nc.semaphore("d%d" % i)) for i in range(ndma)]
        self.dcount = [0] * ndma
        self.dlast = [None] * ndma
        self.dnext = 0
        self.ndma = ndma
        self.uid = 0
        self.out_toks = []

    def _hazards(self, reads, writes):
        waits = []
        for b in reads:
            waits += list(b.wr.values())
        for b in writes:
            waits += list(b.wr.values())
            waits += list(b.rd.values())
        return waits

    def _update(self, tok, key, reads, writes):
        for b in reads:
            b.rd[key] = tok
        for b in writes:
            b.wr = {key: tok}
            b.rd = {}

    def op(self, eng, fn, reads=(), writes=()):
        waits = self._hazards(reads, writes)
        tok = Tok(eng)
        self.ops[eng].append((waits, fn, tok))
        self._update(tok, eng, reads, writes)
        return tok

    def dma(self, out_ap, in_ap, reads=(), writes=(), eng="sp"):
        k = self.dnext
        self.dnext = (k + 1) % self.ndma
        waits = self._hazards(reads, writes)
        if self.dlast[k] is not None:
            waits.append(self.dlast[k])
        self.dcount[k] += 16
        tok = Tok("dma")
        tok.needed = True
        tok.sem = self.dsem[k]
        tok.val = self.dcount[k]
        sem = self.dsem[k]

        def fn(e, out_ap=out_ap, in_ap=in_ap, sem=sem):
            return e.dma_start(out=out_ap, in_=in_ap).then_inc(sem, 16)

        self.ops[eng].append((waits, fn, tok))
        self.dlast[k] = tok
        self.uid += 1
        self._update(tok, "dma%d" % self.uid, reads, writes)
        return tok

    def finalize(self, block):
        for e in self.ENG:
            for waits, fn, tok in self.ops[e]:
                for w in waits:
                    if w.eng == "dma":
                        continue
                    if w.eng == e and (e == "pe" or e == "sp" or not SELF_SYNC):
                        continue
                    w.needed = True
        for t in self.out_toks:
            t.needed = True
        for e in self.ENG:
            c = 0
            for waits, fn, tok in self.ops[e]:
                if tok.eng != "dma" and tok.needed:
                    c += 1
                    tok.val = c
                    tok.sem = self.esem[e]

        def run(h, e):
            seen = {}
            for waits, fn, tok in self.ops[e]:
                best = {}
                for w in waits:
                    if not w.needed or w.val is None:
                        continue
                    if w.eng == e and (e == "pe" or e == "sp" or not SELF_SYNC):
                        continue
                    sid = id(w.sem)
                    if seen.get(sid, 0) >= w.val:
                        continue
                    if sid not in best or best[sid][1] < w.val:
                        best[sid] = (w.sem, w.val)
                for sid, (sem, val) in best.items():
                    h.wait_ge(sem, val)
                    seen[sid] = val
                ins = fn(h)
                if tok.eng != "dma" and tok.needed:
                    ins.then_inc(self.esem[e], 1)
            if e == "sp":
                for t in self.out_toks:
                    if seen.get(id(t.sem), 0) < t.val:
                        h.wait_ge(t.sem, t.val)
                        seen[id(t.sem)] = t.val

        @block.sync
        def _(h):
            run(h, "sp")

        @block.scalar
        def _(h):
            run(h, "act")

        @block.tensor
        def _(h):
            run(h, "pe")

        @block.vector
        def _(h):
            run(h, "dve")

        @block.gpsimd
        def _(h):
            run(h, "pool")


class Arena:
    BASE = 16640
    END = 229376

    def __init__(self, nc):
        self.nc = nc
        self.off = self.BASE
        self.n = 0

    def alloc(self, shape, dtype):
        nbytes = int(np.prod(shape[1:])) * (4 if dtype == F32 else 2)
        nbytes = (nbytes + 63) // 64 * 64
        assert self.off + nbytes <= self.END, ("SBUF overflow", self.off, nbytes)
        self.n += 1
        t = self.nc.alloc_sbuf_tensor_at("t%d" % self.n, list(shape), dtype, offset=self.off)
        self.off += nbytes
        return t

    def mark(self):
        return self.off

    def reset(self, m):
        self.off = m


def build_nc(debug=False):
    nc = bass.Bass("TRN2", target_bir_lowering=False)
    dt_ = nc.dram_tensor
    x_d = dt_("x", [S, D], F32, kind="ExternalInput")
    win_d = dt_("win", [8, 128, 3072], F32, kind="ExternalInput")
    wout_d = dt_("wout", [128, 8192], F32, kind="ExternalInput")
    wffn_d = dt_("wffn", [NFC, 128, 3072], F32, kind="ExternalInput")
    n1_d = dt_("n1", [1, D], F32, kind="ExternalInput")
    n2_d = dt_("n2", [1, D], F32, kind="ExternalInput")
    pv_d = dt_("pv", [128, 4], F32, kind="ExternalInput")
    lam_d = dt_("lam", [1, 256], F32, kind="ExternalInput")
    rb_d = dt_("rb", [32, 4], F32, kind="ExternalInput")
    cst_d = dt_("cst", [128, NCOL], F32, kind="ExternalInput")
    y_d = dt_("y", [S, D], F32, kind="ExternalOutput")
    kind_s = "ExternalOutput" if debug else "Internal"
    wi_s = dt_("wi_s", [8, 128, 3072], BF16, kind="Internal")
    wo_s = dt_("wo_s", [128, 8192], BF16, kind="Internal")
    wf_s = dt_("wf_s", [NFC, 128, 3072], BF16, kind="Internal")
    mix_s = dt_("mix_s", [8, 128, S], BF16, kind=kind_s)
    flat_s = dt_("flat_s", [4, 128, 768], F32, kind="Internal")

    stack = ExitStack()
    with stack:
        P = Prog(nc, stack)
        ps = stack.enter_context(nc.psum_tensor("ps", [128, 8, 512], F32))
        bank = [Buf() for _ in range(8)]
        bank7hi = Buf()

        def psT(b):
            return ps[:, b, :].bitcast(BF16)

        A = Arena(nc)
        dbg_list = []

        def dbg(name, ap, shape, dtype, reads):
            if not debug:
                return
            t = dt_(name, list(shape), dtype, kind="ExternalOutput")
            tk = P.dma(t.ap(), ap, reads=reads)
            P.out_toks.append(tk)
        cb = A.alloc([128, CBW], BF16)
        bhi = A.alloc([128, 4, 640], BF16)
        blo = A.alloc([128, 4, 640], BF16)
        pvt = A.alloc([128, 16], F32)
        b15 = A.alloc([128, 4], F32)
        B_cb, B_bband, B_pvt, B_b15 = Buf(), Buf(), Buf(), Buf()
        ident = cb[:, C_ID:C_ID + 128]
        ntri = cb[:, C_NTRI:C_NTRI + 128]
        ones_b = cb[:, C_ONES:C_ONES + 128]
        bd64 = cb[:, C_BD:C_BD + 128]
        nselA = cb[0:34, C_NSA:C_NSA + 128]
        nselB = cb[0:34, C_NSB:C_NSB + 128]
        EA = cb[:, C_EA:C_EA + 34]
        EB = cb[:, C_EB:C_EB + 34]
        trim = cb[:, C_TM:C_TM + 128]
        gq8 = pvt[:, 4:5]
        gk = pvt[:, 1:2]
        gdo8 = pvt[:, 5:6]
        gsb = pvt[:, 3:4]
        neglam = pvt[:, 6:7]
        m_conv = A.mark()
        st32 = [A.alloc([128, 3072], F32)]
        st16 = [A.alloc([128, 3072], BF16)]
        m_persist = A.mark()
        uT = A.alloc([128, 8, S], BF16)
        B_uT = [Buf() for _ in range(32)]
        m_p2 = A.mark()

        cst = A.alloc([128, NCOL], F32)
        bband = A.alloc([128, 4, 640], F32)
        B_bb32 = Buf()
        lamb = A.alloc([128, 256], F32)
        ltmp = A.alloc([128, 128], F32)
        rb32 = A.alloc([32, 4], F32)
        rbrep = A.alloc([32, 4, 128], F32)
        gb = [A.alloc([128, 768], F32) for _ in range(2)]
        B_cst, B_lamb, B_ltmp, B_rb = Buf(), Buf(), Buf(), Buf()
        B_g = [Buf(), Buf()]
        B_rbrep = Buf()
        B_flat = Buf()

        P.dma(cst[:, :], cst_d.ap(), writes=[B_cst])
        P.dma(pvt[:, 0:4], pv_d.ap(), writes=[B_pvt])
        P.dma(lamb[:, :], bass.AP(lam_d, 0, [[0, 128], [1, 256]]), writes=[B_lamb])
        P.dma(b15[:, :], bass.AP(rb_d, 15 * 4, [[0, 128], [1, 4]]), writes=[B_b15])
        P.dma(rb32[:, :], rb_d.ap(), writes=[B_rb])
        P.op("dve", lambda e: e.tensor_copy(out=cb[:, :], in_=cst[:, 0:CBW]), reads=[B_cst], writes=[B_cb])
        P.op("dve", lambda e: e.tensor_scalar(out=pvt[:, 4:5], in0=pvt[:, 0:1], scalar1=0.125, scalar2=None, op0=ALU.mult),
             reads=[], writes=[B_pvt])
        P.op("dve", lambda e: e.tensor_scalar(out=pvt[:, 5:6], in0=pvt[:, 2:3], scalar1=0.8, scalar2=None, op0=ALU.mult),
             reads=[], writes=[B_pvt])
        P.op("dve", lambda e: e.tensor_tensor(out=ltmp[:, 0:64], in0=lamb[:, 0:64], in1=lamb[:, 64:128], op=ALU.mult),
             reads=[B_lamb], writes=[B_ltmp])
        P.op("dve", lambda e: e.tensor_tensor(out=ltmp[:, 64:128], in0=lamb[:, 128:192], in1=lamb[:, 192:256], op=ALU.mult),
             reads=[B_lamb], writes=[B_ltmp])
        P.op("dve", lambda e: e.reduce_sum(out=pvt[:, 9:10], in_=ltmp[:, 0:64], axis=AX.X), reads=[B_ltmp], writes=[B_pvt])
        P.op("dve", lambda e: e.reduce_sum(out=pvt[:, 10:11], in_=ltmp[:, 64:128], axis=AX.X), reads=[B_ltmp], writes=[B_pvt])
        P.op("act", lambda e: e.activation(out=pvt[:, 7:9], in_=pvt[:, 9:11], func=AF.Exp), reads=[B_pvt], writes=[B_pvt])
        P.op("dve", lambda e: e.tensor_tensor(out=pvt[:, 6:7], in0=pvt[:, 8:9], in1=pvt[:, 7:8], op=ALU.subtract),
             reads=[B_pvt], writes=[B_pvt])
        P.op("dve", lambda e: e.tensor_scalar(out=pvt[:, 6:7], in0=pvt[:, 6:7], scalar1=-0.2, scalar2=None, op0=ALU.add),
             reads=[B_pvt], writes=[B_pvt])
        A.reset(A.mark())

        st32.append(A.alloc([128, 3072], F32))
        st16.append(A.alloc([128, 3072], BF16))
        B_st32 = [Buf(), Buf()]
        B_st16 = [Buf(), Buf()]
        B_wi = [Buf() for _ in range(8)]
        B_wo = Buf()
        B_wf = [Buf() for _ in range(NFC)]
        slabs = []
        for g in GROUP_ORDER:
            slabs.append((win_d.ap()[g], wi_s.ap()[g], B_wi[g], 3072))
        for c, (lo, hi) in enumerate([(0, 3072), (3072, 6144), (6144, 8192)]):
            slabs.append((wout_d.ap()[:, lo:hi], wo_s.ap()[:, lo:hi], B_wo, hi - lo))
        for fc in range(NFC):
            slabs.append((wffn_d.ap()[fc], wf_s.ap()[fc], B_wf[fc], 3072))
        conv_state = {"i": 0, "pending": None, "npair": 2}

        def conv_flush():
            pend = conv_state["pending"]
            if pend is None:
                return
            conv_state["pending"] = None
            dst, bdst, w, k = pend
            old_wr = dict(bdst.wr)
            P.dma(dst, st16[k][:, 0:w], reads=[B_st16[k]], writes=[bdst])
            for kk, vv in old_wr.items():
                if kk.startswith("dma"):
                    bdst.wr[kk] = vv

        def conv_cast_bg():
            ld = conv_state.get("loaded")
            if ld is None:
                return
            conv_state["loaded"] = None
            dst, bdst, w = ld
            for c0 in range(0, w, 1024):
                c1 = min(w, c0 + 1024)
                P.op("dve", lambda e, c0=c0, c1=c1: e.tensor_copy(out=st16[0][:, c0:c1], in_=st32[0][:, c0:c1]),
                     reads=[B_st32[0]], writes=[B_st16[0]])
            conv_state["pending"] = (dst, bdst, w, 0)
            conv_flush()

        def emit_convert(n=1):
            for _ in range(n):
                if conv_state["npair"] == 2:
                    i = conv_state["i"]
                    if i >= len(slabs):
                        conv_flush()
                        return
                    conv_state["i"] = i + 1
                    src, dst, bdst, w = slabs[i]
                    k = i % 2
                    P.dma(st32[k][:, 0:w], src, writes=[B_st32[k]])
                    conv_flush()
                    P.op("pool", lambda e, k=k, w=w: e.tensor_copy(out=st16[k][:, 0:w], in_=st32[k][:, 0:w]),
                         reads=[B_st32[k]], writes=[B_st16[k]])
                    conv_state["pending"] = (dst, bdst, w, k)
                else:
                    conv_cast_bg()
                    i = conv_state["i"]
                    if i >= len(slabs):
                        return
                    conv_state["i"] = i + 1
                    src, dst, bdst, w = slabs[i]
                    P.dma(st32[0][:, 0:w], src, writes=[B_st32[0]])
                    conv_state["loaded"] = (dst, bdst, w)

        w1b = A.alloc([128, D], F32)
        B_w1b = Buf()
        xb = [A.alloc([128, D], F32) for _ in range(3)]
        B_xb = [Buf() for _ in range(3)]
        xn = [A.alloc([128, D], BF16) for _ in range(2)]
        B_xn = [Buf(), Buf()]
        junk = A.alloc([128, D], BF16)
        B_junk = Buf()
        st1 = A.alloc([128, 8], F32)
        B_st1 = [Buf(), Buf()]
        P.dma(w1b[:, :], bass.AP(n1_d, 0, [[0, 128], [1, D]]), writes=[B_w1b])

        def norm_block(src_ap, B_src, wbt, B_wbt, xn_t, B_xn_t, stt, B_stt, col):
            P.op("act", lambda e: e.activation(out=junk[:, :], in_=src_ap, func=AF.Square, accum_out=stt[:, col:col + 1]),
                 reads=[B_src], writes=[B_junk, B_stt])
            P.op("act", lambda e: e.activation(out=stt[:, col + 1:col + 2], in_=stt[:, col:col + 1], func=AF.Ln, scale=1.0 / D, bias=EPS),
                 reads=[B_stt], writes=[B_stt])
            P.op("act", lambda e: e.activation(out=stt[:, col + 2:col + 3], in_=stt[:, col + 1:col + 2], func=AF.Exp, scale=-0.5),
                 reads=[B_stt], writes=[B_stt])
            P.op("dve", lambda e: e.scalar_tensor_tensor(out=xn_t[:, :], in0=src_ap, scalar=stt[:, col + 2:col + 3], in1=wbt[:, :],
                                                          op0=ALU.mult, op1=ALU.mult),
                 reads=[B_src, B_stt, B_wbt], writes=[B_xn_t])

        def transpose_block(xn_t, B_xn_t, bk, dst_ap, B_dst, evac_eng):
            pv = psT(bk)
            for kc in range(8):
                P.op("pe", lambda e, kc=kc: e.transpose(out=pv[:, kc * 128:(kc + 1) * 128], in_=xn_t[:, kc * 128:(kc + 1) * 128], identity=ident),
                     reads=[B_xn_t, B_cb], writes=[bank[bk]])
            src = pv.rearrange("p (k t) -> p k t", k=8)
            if evac_eng == "act":
                P.op("act", lambda e: e.copy(out=dst_ap, in_=src), reads=[bank[bk]], writes=[B_dst])
            else:
                P.op("dve", lambda e: e.tensor_copy(out=dst_ap, in_=src), reads=[bank[bk]], writes=[B_dst])

        xa = x_d.ap()

        def p1_A(tb):
            k3 = tb % 3
            k2 = tb % 2
            P.dma(xb[k3][:, :], xa[tb * 128:(tb + 1) * 128, :], writes=[B_xb[k3]])
            if tb < 2:
                emit_convert(1)
            norm_block(xb[k3][:, :], B_xb[k3], w1b, B_w1b, xn[k2], B_xn[k2], st1, B_st1[k2], 4 * k2)

        def p1_B(tb):
            k2 = tb % 2
            transpose_block(xn[k2], B_xn[k2], 6 + k2, uT[:, :, tb * 128:(tb + 1) * 128], B_uT[tb], "dve")

        for tb in range(33):
            if tb < 32:
                p1_A(tb)
            if tb >= 1:
                p1_B(tb - 1)
        for h in range(4):
            P.op("dve", lambda e, h=h: e.tensor_scalar(out=rbrep[:, h, :], in0=cst[0:32, C_ONES:C_ONES + 128], scalar1=rb32[:, h:h + 1], scalar2=None, op0=ALU.mult),
                 reads=[B_cst, B_rb], writes=[B_rbrep])
        for h in range(4):
            k = h % 2
            P.op("pe", lambda e, h=h: e.matmul(ps[:, 0, :], lhsT=rbrep[:, h, :], rhs=cst[0:32, C_OH:C_OH + 512], start=True, stop=True),
                 reads=[B_rbrep, B_cst], writes=[bank[0]])
            P.op("pe", lambda e, h=h: e.matmul(ps[:, 1, 0:256], lhsT=rbrep[:, h, :], rhs=cst[0:32, C_OH + 512:C_OH + 768], start=True, stop=True),
                 reads=[B_rbrep, B_cst], writes=[bank[1]])
            P.op("dve", lambda e, k=k: e.tensor_copy(out=gb[k][:, 0:512], in_=ps[:, 0, :]), reads=[bank[0]], writes=[B_g[k]])
            P.op("dve", lambda e, k=k: e.tensor_copy(out=gb[k][:, 512:768], in_=ps[:, 1, 0:256]), reads=[bank[1]], writes=[B_g[k]])
            P.dma(flat_s.ap()[h], gb[k][:, :], reads=[B_g[k]], writes=[B_flat])
            P.dma(bband[:, h, :], bass.AP(flat_s, h * 128 * 768 + 127, [[767, 128], [1, 640]]), reads=[B_flat], writes=[B_bb32])
            P.op("dve", lambda e, h=h: e.tensor_tensor(out=bband[:, h, :], in0=bband[:, h, :], in1=cst[:, C_MNEG:C_MNEG + 640], op=ALU.add),
                 reads=[B_cst], writes=[B_bb32])
            P.op("dve", lambda e, h=h: e.tensor_copy(out=bhi[:, h, :], in_=bband[:, h, :]), reads=[B_bb32], writes=[B_bband])
            P.op("dve", lambda e, h=h: e.tensor_tensor(out=blo[:, h, :], in0=bband[:, h, :], in1=bhi[:, h, :], op=ALU.subtract),
                 reads=[B_bb32], writes=[B_bband])
        conv_flush()
        conv_state["npair"] = 1
        dbg("dbg_uT", uT[:, :, :], [128, 8, S], BF16, B_uT)

        A.reset(m_p2)
        qT = [A.alloc([128, S], BF16) for _ in range(2)]
        kT = [A.alloc([128, S], BF16) for _ in range(2)]
        Vt = [A.alloc([128, 32, 128], BF16) for _ in range(2)]
        B_qT = [[Buf() for _ in range(8)] for _ in range(2)]
        B_kT = [[Buf() for _ in range(8)] for _ in range(2)]
        B_V = [[Buf() for _ in range(8)] for _ in range(2)]
        wsl = [A.alloc([128, 8, 384], BF16) for _ in range(2)]
        B_wsl = [Buf(), Buf()]
        esb = [A.alloc([128, 2, 512], F32) for _ in range(2)]
        B_esb = [Buf(), Buf()]
        spb = [A.alloc([128, 2, 512], BF16) for _ in range(3)]
        B_spb = [Buf() for _ in range(3)]
        Ab = [A.alloc([128, 2, 512], BF16) for _ in range(2)]
        B_Ab = [Buf(), Buf()]
        c32 = A.alloc([34, 512], F32)
        HL = A.alloc([34, 512], BF16)
        B_c32, B_HL = Buf(), Buf()
        sqb = [A.alloc([128, 512], BF16) for _ in range(2)]
        B_sqb = [Buf(), Buf()]
        rsb = [A.alloc([128, 512], F32) for _ in range(2)]
        B_rsb = [Buf(), Buf()]
        pp = [A.alloc([128, 512], F32) for _ in range(5)]
        B_pp = [Buf() for _ in range(5)]
        mt = [A.alloc([128, 512], BF16) for _ in range(2)]
        B_mt = [Buf(), Buf()]
        B_mix = [[Buf() for _ in range(8)] for _ in range(8)]
        p1_bufs = [B_w1b, B_junk] + B_xb + B_xn + B_st1 + [B_cst, B_lamb, B_ltmp, B_rb, B_rbrep, B_bb32, B_st32[1], B_st16[1]] + B_g
        p2_first = {"done": False}

        def p2_guard():
            return p1_bufs if not p2_first["done"] else []

        pbank = {"i": 0}

        def next_bank(cands):
            b = cands[pbank["i"] % len(cands)]
            pbank["i"] += 1
            return b

        def load_w(g, sl):
            guard = p2_guard()
            p2_first["done"] = True
            P.dma(wsl[sl][:, :, :].rearrange("p k c -> p (k c)"), wi_s.ap()[g], reads=[B_wi[g]], writes=[B_wsl[sl]] + guard)

        def project(g, sl):
            is_diff = g < 4
            for tt in range(8):
                tsl = slice(tt * 512, (tt + 1) * 512)
                for which in range(2):
                    bk = next_bank([0, 1, 2, 3])
                    dstT, B_dst = (qT, B_qT) if which == 0 else (kT, B_kT)
                    off = which * 128
                    for kc in range(8):
                        P.op("pe", lambda e, bk=bk, kc=kc, off=off, tsl=tsl: e.matmul(ps[:, bk, :], lhsT=wsl[sl][:, kc, off:off + 128], rhs=uT[:, kc, tsl],
                                                                                   start=(kc == 0), stop=(kc == 7)),
                             reads=[B_wsl[sl]] + B_uT[tt * 4:tt * 4 + 4], writes=[bank[bk]])
                    dst_ap = dstT[sl][:, tsl]
                    if not is_diff:
                        if which == 0:
                            P.op("act", lambda e, bk=bk, dst_ap=dst_ap: e.mul(dst_ap, ps[:, bk, :], 0.125),
                                 reads=[bank[bk]], writes=[B_dst[sl][tt]])
                        else:
                            P.op("dve", lambda e, bk=bk, dst_ap=dst_ap: e.tensor_copy(out=dst_ap, in_=ps[:, bk, :]),
                                 reads=[bank[bk]], writes=[B_dst[sl][tt]])
                    else:
                        k2 = (tt * 2 + which) % 2
                        b2 = 4 + k2
                        P.op("act", lambda e, bk=bk, k2=k2: e.activation(out=sqb[k2][:, :], in_=ps[:, bk, :], func=AF.Square),
                             reads=[bank[bk]], writes=[B_sqb[k2]])
                        P.op("pe", lambda e, b2=b2, k2=k2: e.matmul(ps[:, b2, :], lhsT=bd64, rhs=sqb[k2][:, :], start=True, stop=True),
                             reads=[B_sqb[k2], B_cb], writes=[bank[b2]])
                        P.op("act", lambda e, b2=b2, k2=k2: e.activation(out=rsb[k2][:, :], in_=ps[:, b2, :], func=AF.Ln, scale=1.0 / 64, bias=EPS),
                             reads=[bank[b2]], writes=[B_rsb[k2]])
                        P.op("act", lambda e, k2=k2: e.activation(out=rsb[k2][:, :], in_=rsb[k2][:, :], func=AF.Exp, scale=-0.5),
                             reads=[B_rsb[k2]], writes=[B_rsb[k2]])
                        gsc = gq8 if which == 0 else gk
                        P.op("dve", lambda e, bk=bk, k2=k2, dst_ap=dst_ap, gsc=gsc: e.scalar_tensor_tensor(
                            out=dst_ap, in0=ps[:, bk, :], scalar=gsc, in1=rsb[k2][:, :], op0=ALU.mult, op1=ALU.mult),
                             reads=[bank[bk], B_rsb[k2], B_pvt], writes=[B_dst[sl][tt]])
                bk = next_bank([6, 7])
                for i4 in range(4):
                    tb = tt * 4 + i4
                    for kc in range(8):
                        P.op("pe", lambda e, bk=bk, kc=kc, tb=tb, i4=i4: e.matmul(ps[:, bk, i4 * 128:(i4 + 1) * 128],
                                                                                 lhsT=uT[:, kc, tb * 128:(tb + 1) * 128], rhs=wsl[sl][:, kc, 256:384],
                                                                                 start=(kc == 0), stop=(kc == 7)),
                             reads=[B_wsl[sl], B_uT[tb]], writes=[bank[bk]])
                veng = "dve" if (tt % 2 == 0) else "act"
                vdst = Vt[sl][:, tt * 4:(tt + 1) * 4, :]
                vsrc = ps[:, bk, :].rearrange("p (a b) -> p a b", a=4)
                if veng == "dve":
                    P.op("dve", lambda e, vdst=vdst, vsrc=vsrc: e.tensor_copy(out=vdst, in_=vsrc), reads=[bank[bk]], writes=[B_V[sl][tt]])
                else:
                    P.op("act", lambda e, vdst=vdst, vsrc=vsrc: e.copy(out=vdst, in_=vsrc), reads=[bank[bk]], writes=[B_V[sl][tt]])

        def project_units(g, sl):
            is_diff = g < 4
            units = []

            def mk_qk(tt, which):
                def unit(bA, bB):
                    tsl = slice(tt * 512, (tt + 1) * 512)
                    dstT, B_dst = (qT, B_qT) if which == 0 else (kT, B_kT)
                    off = which * 128
                    for kc in range(8):
                        P.op("pe", lambda e, kc=kc: e.matmul(ps[:, bA, :], lhsT=wsl[sl][:, kc, off:off + 128], rhs=uT[:, kc, tsl],
                                                             start=(kc == 0), stop=(kc == 7)),
                             reads=[B_wsl[sl]] + B_uT[tt * 4:tt * 4 + 4], writes=[bank[bA]])
                    dst_ap = dstT[sl][:, tsl]
                    if not is_diff:
                        if which == 0:
                            P.op("dve", lambda e: e.tensor_scalar(out=dst_ap, in0=ps[:, bA, :], scalar1=0.125, scalar2=None, op0=ALU.mult),
                                 reads=[bank[bA]], writes=[B_dst[sl][tt]])
                        else:
                            P.op("dve", lambda e: e.tensor_copy(out=dst_ap, in_=ps[:, bA, :]), reads=[bank[bA]], writes=[B_dst[sl][tt]])
                    else:
                        k2 = (tt * 2 + which) % 2
                        P.op("act", lambda e: e.activation(out=sqb[k2][:, :], in_=ps[:, bA, :], func=AF.Square),
                             reads=[bank[bA]], writes=[B_sqb[k2]])
                        P.op("pe", lambda e: e.matmul(ps[:, bB, :], lhsT=bd64, rhs=sqb[k2][:, :], start=True, stop=True),
                             reads=[B_sqb[k2], B_cb], writes=[bank[bB]])
                        P.op("act", lambda e: e.activation(out=rsb[k2][:, :], in_=ps[:, bB, :], func=AF.Ln, scale=1.0 / 64, bias=EPS),
                             reads=[bank[bB]], writes=[B_rsb[k2]])
                        P.op("act", lambda e: e.activation(out=rsb[k2][:, :], in_=rsb[k2][:, :], func=AF.Exp, scale=-0.5),
                             reads=[B_rsb[k2]], writes=[B_rsb[k2]])
                        gsc = gq8 if which == 0 else gk
                        P.op("dve", lambda e: e.scalar_tensor_tensor(out=dst_ap, in0=ps[:, bA, :], scalar=gsc, in1=rsb[k2][:, :], op0=ALU.mult, op1=ALU.mult),
                             reads=[bank[bA], B_rsb[k2], B_pvt], writes=[B_dst[sl][tt]])
                return unit

            def mk_v(tt, half):
                def unit(bA, bB):
                    for i2 in range(2):
                        tb = tt * 4 + half * 2 + i2
                        for kc in range(8):
                            P.op("pe", lambda e, kc=kc, tb=tb, i2=i2: e.matmul(ps[:, bA, i2 * 128:(i2 + 1) * 128],
                                                                                lhsT=uT[:, kc, tb * 128:(tb + 1) * 128], rhs=wsl[sl][:, kc, 256:384],
                                                                                start=(kc == 0), stop=(kc == 7)),
                                 reads=[B_wsl[sl], B_uT[tb]], writes=[bank[bA]])
                    vdst = Vt[sl][:, tt * 4 + half * 2:tt * 4 + half * 2 + 2, :]
                    vsrc = ps[:, bA, 0:256].rearrange("p (a b) -> p a b", a=2)
                    P.op("dve", lambda e: e.tensor_copy(out=vdst, in_=vsrc), reads=[bank[bA]], writes=[B_V[sl][tt]])
                return unit

            for tt in range(8):
                units.append(mk_qk(tt, 0))
                units.append(mk_qk(tt, 1))
                units.append(mk_v(tt, 0))
                units.append(mk_v(tt, 1))
            return units

        def project_units_half(g, sl):
            units = []

            def mk_qk(tt, which, hf):
                def unit():
                    tsl = slice(tt * 512, (tt + 1) * 512)
                    dstT, B_dst = (qT, B_qT) if which == 0 else (kT, B_kT)
                    off = which * 128 + hf * 64
                    for kc in range(8):
                        P.op("pe", lambda e, kc=kc: e.matmul(ps[64:128, 7, :], lhsT=wsl[sl][:, kc, off:off + 64], rhs=uT[:, kc, tsl],
                                                             start=(kc == 0), stop=(kc == 7)),
                             reads=[B_wsl[sl]] + B_uT[tt * 4:tt * 4 + 4], writes=[bank7hi])
                    dst_ap = dstT[sl][hf * 64:(hf + 1) * 64, tsl]
                    if which == 0:
                        P.op("dve", lambda e: e.tensor_scalar(out=dst_ap, in0=ps[64:128, 7, :], scalar1=0.125, scalar2=None, op0=ALU.mult),
                             reads=[bank7hi], writes=[B_dst[sl][tt]])
                    else:
                        P.op("dve", lambda e: e.tensor_copy(out=dst_ap, in_=ps[64:128, 7, :]), reads=[bank7hi], writes=[B_dst[sl][tt]])
                return unit

            def mk_v(tb):
                def unit():
                    for hb in range(2):
                        for kc in range(8):
                            P.op("pe", lambda e, kc=kc, hb=hb: e.matmul(ps[64:128, 7, hb * 128:(hb + 1) * 128],
                                                                         lhsT=uT[:, kc, tb * 128 + hb * 64:tb * 128 + hb * 64 + 64], rhs=wsl[sl][:, kc, 256:384],
                                                                         start=(kc == 0), stop=(kc == 7)),
                                 reads=[B_wsl[sl], B_uT[tb]], writes=[bank7hi])
                    for hb in range(2):
                        P.op("dve", lambda e, hb=hb: e.tensor_copy(out=Vt[sl][hb * 64:(hb + 1) * 64, tb, :], in_=ps[64:128, 7, hb * 128:(hb + 1) * 128]),
                             reads=[bank7hi], writes=[B_V[sl][tb // 4]])
                return unit

            for tt in range(8):
                for hf in range(2):
                    units.append(mk_qk(tt, 0, hf))
                for hf in range(2):
                    units.append(mk_qk(tt, 1, hf))
                for i4 in range(4):
                    units.append(mk_v(tt * 4 + i4))
            return units

        def step_geom(t, j):
            m = j - 4 * t
            if m >= 0:
                return 128 * m, 512 - 128 * m, "diag", 0
            if m == -1:
                return 0, 512, "near", 128
            return 0, 512, "far", 0

        def out_norm_store(g, t, y_ap, y_bufs, gain_ap, lhs_ones, div, ssb):
            k2 = t % 2
            ssbufs = [bank[ssb]] + ([bank7hi] if ssb == 7 else [])
            P.op("act", lambda e: e.activation(out=sqb[k2][:, :], in_=y_ap, func=AF.Square), reads=y_bufs, writes=[B_sqb[k2]])
            P.op("pe", lambda e: e.matmul(ps[:, ssb, :], lhsT=lhs_ones, rhs=sqb[k2][:, :], start=True, stop=True),
                 reads=[B_sqb[k2], B_cb], writes=ssbufs)
            P.op("act", lambda e: e.activation(out=rsb[k2][:, :], in_=ps[:, ssb, :], func=AF.Ln, scale=1.0 / div, bias=EPS),
                 reads=ssbufs, writes=[B_rsb[k2]])
            P.op("act", lambda e: e.activation(out=rsb[k2][:, :], in_=rsb[k2][:, :], func=AF.Exp, scale=-0.5),
                 reads=[B_rsb[k2]], writes=[B_rsb[k2]])
            P.op("dve", lambda e: e.scalar_tensor_tensor(out=mt[k2][:, :], in0=y_ap, scalar=gain_ap, in1=rsb[k2][:, :], op0=ALU.mult, op1=ALU.mult),
                 reads=y_bufs + [B_rsb[k2], B_pvt], writes=[B_mt[k2]])
            P.dma(mix_s.ap()[g, :, t * 512:(t + 1) * 512], mt[k2][:, :], reads=[B_mt[k2]], writes=[B_mix[g][t]])
            emit_convert(1)

        def attn_diff(g, sl):
            h = g
            deferred = []
            for t in range(8):
                ns = 4 * t + 4
                qbuf = [B_qT[sl][t]]

                def Z(s):
                    j = 4 * t + 3 - s
                    qlo, N, kind, u0 = step_geom(t, j)
                    st_ = (s % 2) * 2
                    ksl = slice(j * 128, (j + 1) * 128)
                    qsl = slice(t * 512 + qlo, (t + 1) * 512)
                    far = (kind == "far")
                    for c in range(2):
                        P.op("pe", lambda e, c=c: e.matmul(ps[:, st_ + c, qlo:512], lhsT=kT[sl][c * 64:(c + 1) * 64, ksl], rhs=qT[sl][c * 64:(c + 1) * 64, qsl],
                                                           start=True, stop=far),
                             reads=qbuf + [B_kT[sl][j // 4]], writes=[bank[st_ + c]])
                    if not far:
                        for c in range(2):
                            P.op("pe", lambda e, c=c: e.matmul(ps[:, st_ + c, qlo:512], lhsT=ident, rhs=bhi[:, h, u0:u0 + N], start=False, stop=False),
                                 reads=[B_bband, B_cb], writes=[bank[st_ + c]])
                            P.op("pe", lambda e, c=c: e.matmul(ps[:, st_ + c, qlo:512], lhsT=ident, rhs=blo[:, h, u0:u0 + N], start=False, stop=True),
                                 reads=[B_bband, B_cb], writes=[bank[st_ + c]])

                def E(s):
                    j = 4 * t + 3 - s
                    qlo, N, kind, u0 = step_geom(t, j)
                    st_ = (s % 2) * 2
                    k2 = s % 2
                    if kind == "far":
                        P.op("act", lambda e: e.activation(out=Ab[k2][:, :, qlo:512], in_=ps[:, st_:st_ + 2, qlo:512], func=AF.Exp, bias=b15[:, h:h + 1]),
                             reads=[bank[st_], bank[st_ + 1], B_b15], writes=[B_Ab[k2]])
                    else:
                        P.op("act", lambda e: e.activation(out=Ab[k2][:, :, qlo:512], in_=ps[:, st_:st_ + 2, qlo:512], func=AF.Exp),
                             reads=[bank[st_], bank[st_ + 1]], writes=[B_Ab[k2]])

                def PV(s):
                    j = 4 * t + 3 - s
                    qlo, N, kind, u0 = step_geom(t, j)
                    k2 = s % 2
                    st0 = (s == 0)
                    sp0 = (s == ns - 1)
                    for c in range(2):
                        P.op("pe", lambda e, c=c: e.matmul(ps[:, 4 + c, qlo:512], lhsT=Vt[sl][:, j, :], rhs=Ab[k2][:, c, qlo:512], start=st0, stop=sp0,
                                                           skip_group_check=True),
                             reads=[B_Ab[k2], B_V[sl][j // 4]], writes=[bank[4 + c]])
                        P.op("pe", lambda e, c=c: e.matmul(ps[:, 6 + c, qlo:512], lhsT=ones_b, rhs=Ab[k2][:, c, qlo:512], start=st0, stop=sp0,
                                                           skip_group_check=True),
                             reads=[B_Ab[k2], B_cb], writes=[bank[6 + c]])

                for it in range(-1, ns):
                    if it + 1 < ns:
                        Z(it + 1)
                        E(it + 1)
                    if it >= 0:
                        PV(it)
                    if it == 1 and deferred:
                        deferred.pop(0)()
                for c in range(2):
                    P.op("dve", lambda e, c=c: e.tensor_copy(out=pp[2 + c][:, :], in_=ps[:, 4 + c, :]), reads=[bank[4 + c]], writes=[B_pp[2 + c]])
                    P.op("act", lambda e, c=c: e.activation(out=pp[c][:, :], in_=ps[:, 6 + c, :], func=AF.Ln), reads=[bank[6 + c]], writes=[B_pp[c]])

                def stage2(t=t):
                    for c in range(2):
                        P.op("act", lambda e, c=c: e.activation(out=pp[c][:, :], in_=pp[c][:, :], func=AF.Exp, scale=-1.0), reads=[B_pp[c]], writes=[B_pp[c]])
                        P.op("dve", lambda e, c=c: e.tensor_tensor(out=pp[2 + c][:, :], in0=pp[2 + c][:, :], in1=pp[c][:, :], op=ALU.mult),
                             reads=[B_pp[c]], writes=[B_pp[2 + c]])
                    P.op("dve", lambda e: e.scalar_tensor_tensor(out=pp[4][:, :], in0=pp[3][:, :], scalar=neglam, in1=pp[2][:, :], op0=ALU.mult, op1=ALU.add),
                         reads=[B_pp[2], B_pp[3], B_pvt], writes=[B_pp[4]])
                    out_norm_store(g, t, pp[4][:, :], [B_pp[4]], gdo8, ones_b, 128.0, 0)

                deferred.append(stage2)
                if t == 7:
                    deferred.pop(0)()

        def attn_sb(g, sl, hosted=None):
            deferred = []
            hosted = hosted if hosted is not None else []
            zc = {"n": 0}
            hcount = {"n": 0}
            for t in range(8):
                ns = 4 * t + 4
                qbuf = [B_qT[sl][t]]
                P.op("dve", lambda e: e.memset(c32[:, :], 0.0), writes=[B_c32])
                P.op("dve", lambda e: e.memset(HL[:, :], 0.0), writes=[B_HL])

                def geo(s):
                    j = 4 * t + 3 - s
                    m = j - 4 * t
                    qlo = 128 * m if m >= 0 else 0
                    return j, m, qlo

                zset = {}

                def Astage(s):
                    j, m, qlo = geo(s)
                    zset[s] = (zc["n"] % 3) * 2
                    zc["n"] += 1
                    st_ = zset[s]
                    ksl = slice(j * 128, (j + 1) * 128)
                    qsl = slice(t * 512 + qlo, (t + 1) * 512)
                    for c in range(2):
                        P.op("pe", lambda e, c=c: e.matmul(ps[:, st_ + c, qlo:512], lhsT=kT[sl][c * 64:(c + 1) * 64, ksl], rhs=qT[sl][c * 64:(c + 1) * 64, qsl],
                                                           start=True, stop=True),
                             reads=qbuf + [B_kT[sl][j // 4]], writes=[bank[st_ + c]])
                    k2 = s % 2
                    k3 = s % 3
                    P.op("act", lambda e: e.activation(out=esb[k2][:, :, qlo:512], in_=ps[:, st_:st_ + 2, qlo:512], func=AF.Exp),
                         reads=[bank[st_], bank[st_ + 1]], writes=[B_esb[k2]])
                    P.op("act", lambda e: e.activation(out=spb[k3][:, :, qlo:512], in_=esb[k2][:, :, qlo:512], func=AF.Ln, bias=1.0),
                         reads=[B_esb[k2]], writes=[B_spb[k3]])
                    if m >= 0:
                        for c in range(2):
                            P.op("dve", lambda e, c=c: e.tensor_tensor(out=spb[k3][:, c, qlo:qlo + 128], in0=spb[k3][:, c, qlo:qlo + 128], in1=trim, op=ALU.mult),
                                 reads=[B_cb], writes=[B_spb[k3]])

                def Bstage(s):
                    j, m, qlo = geo(s)
                    st_ = zset[s]
                    k3 = s % 3
                    last = (s == 0)
                    for c in range(2):
                        P.op("pe", lambda e, c=c: e.matmul(ps[:, st_ + c, qlo:512], lhsT=ntri, rhs=spb[k3][:, c, qlo:512], start=False, stop=last, skip_group_check=True),
                             reads=[B_spb[k3], B_cb], writes=[bank[st_ + c]])
                    if s > 0:
                        for c in range(2):
                            P.op("pe", lambda e, c=c: e.matmul(ps[:, st_ + c, qlo:512], lhsT=(nselA if c == 0 else nselB), rhs=HL[0:34, qlo:512], start=False, stop=True, skip_group_check=True),
                                 reads=[B_HL, B_cb], writes=[bank[st_ + c]])
                    if s < ns - 1:
                        P.op("pe", lambda e: e.matmul(ps[0:34, 7, qlo:512], lhsT=EA, rhs=spb[k3][:, 0, qlo:512], start=True, stop=False),
                             reads=[B_spb[k3], B_cb], writes=[bank[7]])
                        P.op("pe", lambda e: e.matmul(ps[0:34, 7, qlo:512], lhsT=EB, rhs=spb[k3][:, 1, qlo:512], start=False, stop=True),
                             reads=[B_spb[k3], B_cb], writes=[bank[7]])
                        P.op("dve", lambda e: e.tensor_tensor(out=c32[:, qlo:512], in0=c32[:, qlo:512], in1=ps[0:34, 7, qlo:512], op=ALU.add),
                             reads=[bank[7]], writes=[B_c32])
                        P.op("dve", lambda e: e.tensor_copy(out=HL[:, qlo:512], in_=c32[:, qlo:512]), reads=[B_c32], writes=[B_HL])
                        P.op("dve", lambda e: e.tensor_tensor(out=HL[32:34, qlo:512], in0=c32[32:34, qlo:512], in1=HL[32:34, qlo:512], op=ALU.subtract),
                             reads=[B_c32], writes=[B_HL])

                def E2(s):
                    j, m, qlo = geo(s)
                    st_ = zset[s]
                    k2 = s % 2
                    P.op("act", lambda e: e.activation(out=Ab[k2][:, :, qlo:512], in_=ps[:, st_:st_ + 2, qlo:512], func=AF.Exp),
                         reads=[bank[st_], bank[st_ + 1]], writes=[B_Ab[k2]])
                    if m >= 0:
                        for c in range(2):
                            P.op("dve", lambda e, c=c: e.tensor_tensor(out=Ab[k2][:, c, qlo:qlo + 128], in0=Ab[k2][:, c, qlo:qlo + 128], in1=trim, op=ALU.mult),
                                 reads=[B_cb], writes=[B_Ab[k2]])

                def PV(s):
                    j, m, qlo = geo(s)
                    k2 = s % 2
                    st0 = (s == 0)
                    sp0 = (s == ns - 1)
                    for c in range(2):
                        P.op("pe", lambda e, c=c: e.matmul(ps[c * 64:(c + 1) * 64, 6, qlo:512], lhsT=Vt[sl][:, j, c * 64:(c + 1) * 64], rhs=Ab[k2][:, c, qlo:512],
                                                           start=st0, stop=sp0, skip_group_check=True),
                             reads=[B_Ab[k2], B_V[sl][j // 4]], writes=[bank[6]])

                for it in range(-2, ns):
                    if 0 <= it + 2 < ns:
                        Astage(it + 2)
                    if 0 <= it + 1 < ns:
                        Bstage(it + 1)
                        E2(it + 1)
                    if it >= 0:
                        PV(it)
                    if it == 1 and deferred:
                        deferred.pop(0)()
                    if hosted and it >= 0:
                        hcount["n"] += 1
                        if hcount["n"] % 2 == 0:
                            hosted.pop(0)()
                P.op("dve", lambda e: e.tensor_copy(out=pp[4][:, :], in_=ps[:, 6, :]), reads=[bank[6]], writes=[B_pp[4]])
                deferred.append(lambda t=t: out_norm_store(g, t, pp[4][:, :], [B_pp[4]], gsb, bd64, 64.0, 7))
                if t == 7:
                    deferred.pop(0)()
            while hosted:
                hosted.pop(0)()

        order = GROUP_ORDER
        load_w(order[0], 0)
        project(order[0], 0)
        for i, g in enumerate(order):
            sl = i % 2
            nxt = order[i + 1] if i + 1 < 8 else None
            if nxt is not None:
                load_w(nxt, (i + 1) % 2)
            if g >= 4:
                hosted = project_units_half(nxt, (i + 1) % 2) if (nxt is not None and nxt >= 4) else None
                attn_sb(g, sl, hosted)
                if nxt is not None and nxt < 4:
                    project(nxt, (i + 1) % 2)
            else:
                attn_diff(g, sl)
                if nxt is not None:
                    project(nxt, (i + 1) % 2)
        emit_convert(100)

        A.reset(m_conv)
        p2_bufs = ([B_wsl[0], B_wsl[1], B_c32, B_HL, B_st32[0], B_st16[0]] + B_esb + B_spb + B_Ab + B_sqb + B_rsb + B_pp + B_mt + B_uT
                   + [b for sl_ in range(2) for b in B_qT[sl_] + B_kT[sl_] + B_V[sl_]])
        wo = A.alloc([128, 8, D], BF16)
        B_wo_sb = Buf()
        w2b = A.alloc([128, D], F32)
        B_w2b = Buf()
        xh = [A.alloc([128, 4, D], F32) for _ in range(2)]
        B_xh = [[Buf() for _ in range(4)] for _ in range(2)]
        mxt = [A.alloc([128, 8, 512], BF16) for _ in range(2)]
        B_mxt = [Buf(), Buf()]
        u2 = [A.alloc([128, D], BF16) for _ in range(4)]
        B_u2 = [Buf() for _ in range(4)]
        u2T = [A.alloc([128, 8, 512], BF16) for _ in range(2)]
        B_u2T = [[Buf() for _ in range(4)] for _ in range(2)]
        actT = A.alloc([128, NFC, 512], BF16)
        B_actT = [Buf() for _ in range(NFC)]
        sg = [A.alloc([128, 512], F32) for _ in range(2)]
        B_sg = [Buf(), Buf()]
        wgu = [A.alloc([128, 2048], BF16) for _ in range(3)]
        B_wgu = [Buf() for _ in range(3)]
        wd = [A.alloc([128, NFC, 512], BF16) for _ in range(2)]
        B_wd = [[Buf(), Buf()], [Buf(), Buf()]]
        ot = [A.alloc([128, 512], F32) for _ in range(4)]
        B_ot = [Buf() for _ in range(4)]
        junk3 = A.alloc([128, D], BF16)
        st3 = A.alloc([128, 8], F32)
        B_st3 = [Buf(), Buf()]
        B_junk3 = Buf()

        def p3w(extra):
            return extra + p2_bufs

        P.dma(wo[:, :, :].rearrange("p k c -> p (k c)"), wo_s.ap(), reads=[B_wo], writes=p3w([B_wo_sb]))
        P.dma(w2b[:, :], bass.AP(n2_d, 0, [[0, 128], [1, D]]), writes=p3w([B_w2b]))
        first_p3 = {"f": True}

        def p3_load_x(tt):
            k = tt % 2
            extra = p2_bufs if tt < 2 else []
            P.dma(xh[k][:, :, :], xa[tt * 512:(tt + 1) * 512, :].rearrange("(a p) d -> p a d", p=128), writes=B_xh[k] + extra)

        def p3_load_m(tt):
            k = tt % 2
            extra = p2_bufs if tt < 2 else []
            P.dma(mxt[k][:, :, :], mix_s.ap()[:, :, tt * 512:(tt + 1) * 512].rearrange("e p t -> p e t"),
                  reads=[B_mix[g_][tt] for g_ in range(8)], writes=[B_mxt[k]] + extra)

        def p3_loads(tt):
            p3_load_x(tt)
            p3_load_m(tt)

        def load_wgu(tt, fc):
            k3 = fc % 3
            P.dma(wgu[k3][:, :], wf_s.ap()[fc, :, 0:2048], reads=[B_wf[fc]], writes=[B_wgu[k3]] + (p2_bufs if (tt == 0 and fc < 3) else []))

        def norm_block3(src_ap, B_src, xn_t, B_xn_t, col, B_stt):
            P.op("act", lambda e: e.activation(out=junk3[:, :], in_=src_ap, func=AF.Square, accum_out=st3[:, col:col + 1]),
                 reads=[B_src], writes=[B_junk3, B_stt])
            P.op("act", lambda e: e.activation(out=st3[:, col + 1:col + 2], in_=st3[:, col:col + 1], func=AF.Ln, scale=1.0 / D, bias=EPS),
                 reads=[B_stt], writes=[B_stt])
            P.op("act", lambda e: e.activation(out=st3[:, col + 2:col + 3], in_=st3[:, col + 1:col + 2], func=AF.Exp, scale=-0.5),
                 reads=[B_stt], writes=[B_stt])
            P.op("dve", lambda e: e.scalar_tensor_tensor(out=xn_t[:, :], in0=src_ap, scalar=st3[:, col + 2:col + 3], in1=w2b[:, :],
                                                          op0=ALU.mult, op1=ALU.mult),
                 reads=[B_src, B_stt, B_w2b], writes=[B_xn_t])

        ya = y_d.ap()

        def X1(tt):
            k = tt % 2
            for tb in range(4):
                for dh in range(2):
                    bk = 4 + (tb * 2 + dh) % 4
                    for ec in range(8):
                        P.op("pe", lambda e, bk=bk, ec=ec, tb=tb, dh=dh, k=k: e.matmul(ps[:, bk, :], lhsT=mxt[k][:, ec, tb * 128:(tb + 1) * 128],
                                                                                       rhs=wo[:, ec, dh * 512:(dh + 1) * 512], start=(ec == 0), stop=(ec == 7)),
                             reads=[B_mxt[k], B_wo_sb], writes=[bank[bk]])
                    P.op("dve", lambda e, bk=bk, tb=tb, dh=dh, k=k: e.tensor_tensor(out=xh[k][:, tb, dh * 512:(dh + 1) * 512], in0=ps[:, bk, :],
                                                                                      in1=xh[k][:, tb, dh * 512:(dh + 1) * 512], op=ALU.add),
                         reads=[bank[bk]], writes=[B_xh[k][tb]])
                norm_block3(xh[k][:, tb, :], B_xh[k][tb], u2[tb], B_u2[tb], 4 * (tb % 2), B_st3[tb % 2])

        def X2(tt):
            for tb in range(4):
                transpose_block(u2[tb], B_u2[tb], 6 + tb % 2, u2T[tt % 2][:, :, tb * 128:(tb + 1) * 128], B_u2T[tt % 2][tb], "dve")

        def Y1(tt):
            for fc in range(NFC):
                k3 = fc % 3
                if fc >= 3 or tt == 0:
                    load_wgu(tt, fc)
                if fc in (2, 6, 10, 14):
                    ci = (2, 6, 10, 14).index(fc)
                    dh_, c_ = ci // 2, ci % 2
                    P.dma(wd[dh_][:, 11 * c_:11 * c_ + 11, :],
                          wf_s.ap()[11 * c_:11 * c_ + 11, :, 2048 + dh_ * 512:2048 + (dh_ + 1) * 512].rearrange("f p d -> p f d"),
                          reads=B_wf[11 * c_:11 * c_ + 11], writes=[B_wd[dh_][c_]] + (p2_bufs if tt == 0 else []))
                bg = 4 + fc % 2
                bu = 6 + fc % 2
                ut = u2T[tt % 2]
                for kc in range(8):
                    P.op("pe", lambda e, kc=kc, k3=k3, bg=bg, ut=ut: e.matmul(ps[:, bg, :], lhsT=wgu[k3][:, kc * 128:(kc + 1) * 128], rhs=ut[:, kc, :],
                                                                                start=(kc == 0), stop=(kc == 7)),
                         reads=[B_wgu[k3]] + B_u2T[tt % 2], writes=[bank[bg]])
                for kc in range(8):
                    P.op("pe", lambda e, kc=kc, k3=k3, bu=bu, ut=ut: e.matmul(ps[:, bu, :], lhsT=wgu[k3][:, 1024 + kc * 128:1024 + (kc + 1) * 128], rhs=ut[:, kc, :],
                                                                                start=(kc == 0), stop=(kc == 7)),
                         reads=[B_wgu[k3]] + B_u2T[tt % 2], writes=[bank[bu]])
                s2 = fc % 2
                P.op("act", lambda e, s2=s2, bg=bg: e.activation(out=sg[s2][:, :], in_=ps[:, bg, :], func=AF.Silu),
                     reads=[bank[bg]], writes=[B_sg[s2]] + (p2_bufs if (tt == 0 and fc < 2) else []))
                P.op("dve", lambda e, s2=s2, bu=bu, fc=fc: e.tensor_tensor(out=actT[:, fc, :], in0=sg[s2][:, :], in1=ps[:, bu, :], op=ALU.mult),
                     reads=[B_sg[s2], bank[bu]], writes=[B_actT[fc]] + (p2_bufs if (tt == 0 and fc == 0) else []))

        def Y2(tt, dh):
            k = tt % 2
            for fc in range(NFC):
                for tb in range(4):
                    P.op("pe", lambda e, fc=fc, tb=tb, dh=dh: e.matmul(ps[:, tb, :], lhsT=actT[:, fc, tb * 128:(tb + 1) * 128], rhs=wd[dh][:, fc, :],
                                                                          start=(fc == 0), stop=(fc == NFC - 1)),
                         reads=[B_actT[fc], B_wd[dh][fc // 11]], writes=[bank[tb]])
            for tb in range(4):
                o = tb
                P.op("dve", lambda e, tb=tb, dh=dh, o=o, k=k: e.tensor_tensor(out=ot[o][:, :], in0=ps[:, tb, :], in1=xh[k][:, tb, dh * 512:(dh + 1) * 512], op=ALU.add),
                     reads=[bank[tb], B_xh[k][tb]], writes=[B_ot[o]] + (p2_bufs if (tt == 0 and dh == 0) else []))
                tk = P.dma(ya[tt * 512 + tb * 128:tt * 512 + (tb + 1) * 128, dh * 512:(dh + 1) * 512], ot[o][:, :], reads=[B_ot[o]])
                P.out_toks.append(tk)

        p3_loads(0)
        p3_loads(1)
        X1(0)
        X2(0)
        for tt in range(8):
            Y1(tt)
            if tt + 1 < 8:
                for fc_ in range(3):
                    load_wgu(tt + 1, fc_)
                X1(tt + 1)
            if tt + 2 < 8:
                p3_load_m(tt + 2)
            Y2(tt, 0)
            if tt + 1 < 8:
                X2(tt + 1)
            Y2(tt, 1)
            if tt + 2 < 8:
                p3_load_x(tt + 2)

        block = stack.enter_context(nc.Block())
        P.finalize(block)
    return nc


def _t5_bucket_np(rel):
    nb = 16
    max_exact = 8
    ret = (rel > 0).astype(np.int32) * nb
    n = np.abs(rel)
    nf = np.maximum(n, 1).astype(np.float32)
    large = max_exact + (np.log(nf / np.float32(max_exact)) / np.float32(math.log(128 / max_exact))
                         * np.float32(nb - max_exact)).astype(np.int32)
    large = np.minimum(large, nb - 1)
    return ret + np.where(n < max_exact, n, large)


def _const_table():
    c = np.zeros((128, NCOL), np.float32)
    i = np.arange(128)
    c[:, C_ID:C_ID + 128] = np.eye(128, dtype=np.float32)
    c[:, C_NTRI:C_NTRI + 128] = -(i[:, None] >= i[None, :]).astype(np.float32)
    c[:, C_ONES:C_ONES + 128] = 1.0
    c[:, C_BD:C_BD + 128] = ((i[:, None] // 64) == (i[None, :] // 64)).astype(np.float32)
    c[0, C_NSA:C_NSA + 128] = -1.0
    c[32, C_NSA:C_NSA + 128] = -1.0
    c[1, C_NSB:C_NSB + 128] = -1.0
    c[33, C_NSB:C_NSB + 128] = -1.0
    c[:, C_EA + 0] = 1.0
    c[:, C_EA + 32] = 1.0
    c[:, C_EB + 1] = 1.0
    c[:, C_EB + 33] = 1.0
    c[:, C_TM:C_TM + 128] = (i[:, None] < i[None, :]).astype(np.float32)
    u = np.arange(640)
    c[:, C_MNEG:C_MNEG + 640] = np.where((i[:, None] // 64) > (u[None, :] // 64), NEG, 0.0).astype(np.float32)
    s = np.arange(767)
    bk = _t5_bucket_np((127 - s).astype(np.int32))
    c[bk, C_OH + s] = 1.0
    return c


_NC_CACHE = {}


def _prep_shared(inp):
    w_in = np.asarray(inp["w_in"][0], np.float32)
    cols = []
    for g in range(8):
        base = 0 if g < 4 else 1536
        gi = g % 4
        cols.append(np.concatenate([np.arange(base + gi * 128, base + gi * 128 + 128),
                                    np.arange(base + 512 + gi * 128, base + 512 + gi * 128 + 128),
                                    np.arange(base + 1024 + gi * 128, base + 1024 + gi * 128 + 128)]))
    win = np.empty((8, 128, 3072), np.float32)
    w4 = w_in.reshape(8, 128, 3072)
    for g in range(8):
        win[g] = np.transpose(w4[:, :, cols[g]], (1, 0, 2)).reshape(128, 3072)
    wout = np.ascontiguousarray(np.transpose(np.asarray(inp["w_out"][0], np.float32).reshape(8, 128, 1024), (1, 0, 2)).reshape(128, 8192))
    wg = np.asarray(inp["w_gate"][0], np.float32).reshape(8, 128, NFC, 128)
    wu = np.asarray(inp["w_up"][0], np.float32).reshape(8, 128, NFC, 128)
    wdn = np.asarray(inp["w_down"][0], np.float32).reshape(NFC, 128, 1024)
    wffn = np.empty((NFC, 128, 3072), np.float32)
    wffn[:, :, 0:1024] = np.transpose(wg, (2, 1, 0, 3)).reshape(NFC, 128, 1024)
    wffn[:, :, 1024:2048] = np.transpose(wu, (2, 1, 0, 3)).reshape(NFC, 128, 1024)
    wffn[:, :, 2048:3072] = wdn
    p = np.arange(128)
    pv = np.stack([np.asarray(inp["q_norm_w"][0])[p % 64], np.asarray(inp["k_norm_w"][0])[p % 64],
                   np.asarray(inp["diff_out_norm_w"][0])[p], np.asarray(inp["sb_out_norm_w"][0])[p % 64]], axis=1).astype(np.float32)
    lam = np.concatenate([np.asarray(inp["lambda_q1"][0]), np.asarray(inp["lambda_k1"][0]),
                          np.asarray(inp["lambda_q2"][0]), np.asarray(inp["lambda_k2"][0])]).astype(np.float32)[None, :]
    return {
        "win": win, "wout": wout, "wffn": wffn,
        "n1": np.asarray(inp["norm1_w"], np.float32).reshape(1, D),
        "n2": np.asarray(inp["norm2_w"], np.float32).reshape(1, D),
        "pv": np.ascontiguousarray(pv), "lam": np.ascontiguousarray(lam),
        "rb": np.ascontiguousarray(np.asarray(inp["rel_bias"], np.float32)),
        "cst": _const_table(),
    }


def kernel(**inputs):
    x = np.asarray(inputs["x"], np.float32)
    nb = x.shape[0]
    shared = _prep_shared(inputs)
    if "nc" not in _NC_CACHE:
        _NC_CACHE["nc"] = build_nc()
    nc = _NC_CACHE["nc"]
    in_maps = []
    for b in range(nb):
        m = dict(shared)
        m["x"] = np.ascontiguousarray(x[b])
        in_maps.append(m)
    res = run_bass_kernel_spmd(nc, in_maps, core_ids=list(range(nb)))
    return np.stack([np.asarray(r["y"], np.float32) for r in res.results], axis=0)
```

```python
import math
from contextlib import ExitStack

import numpy as np
import concourse.bass as bass
import concourse.mybir as mybir
from concourse.bass_utils import run_bass_kernel_spmd

F32 = mybir.dt.float32
BF16 = mybir.dt.bfloat16
AF = mybir.ActivationFunctionType
ALU = mybir.AluOpType
AX = mybir.AxisListType

S = 4096
D = 1024
DFF = 2816
NFC = DFF // 128
EPS = 1e-6
NEG = -30000.0
SELF_SYNC = True
GROUP_ORDER = [0, 1, 2, 3, 4, 5, 6, 7]

C_ID = 0
C_NTRI = 128
C_ONES = 256
C_BD = 384
C_NSA = 512
C_NSB = 640
C_EA = 768
C_EB = 832
C_TM = 896
CBW = 1024
C_MNEG = 1024
C_OH = 1664
NCOL = 2432


class Tok:
    __slots__ = ("eng", "needed", "val", "sem")

    def __init__(self, eng):
        self.eng = eng
        self.needed = False
        self.val = None
        self.sem = None


class Buf:
    __slots__ = ("wr", "rd")

    def __init__(self):
        self.wr = {}
        self.rd = {}


class Prog:
    ENG = ("sp", "act", "pe", "dve", "pool")

    def __init__(self, nc, stack, ndma=32):
        self.nc = nc
        self.ops = {e: [] for e in self.ENG}
        self.esem = {e: stack.enter_context(nc.semaphore("s_" + e)) for e in self.ENG}
        self.dsem = [stack.enter_context(nc.semaphore("d%d" % i)) for i in range(ndma)]
        self.dcount = [0] * ndma
        self.dlast = [None] * ndma
        self.dnext = 0
        self.ndma = ndma
        self.uid = 0
        self.out_toks = []

    def _hazards(self, reads, writes):
        waits = []
        for b in reads:
            waits += list(b.wr.values())
        for b in writes:
            waits += list(b.wr.values())
            waits += list(b.rd.values())
        return waits

    def _update(self, tok, key, reads, writes):
        for b in reads:
            b.rd[key] = tok
        for b in writes:
            b.wr = {key: tok}
            b.rd = {}

    def op(self, eng, fn, reads=(), writes=()):
        waits = self._hazards(reads, writes)
        tok = Tok(eng)
        self.ops[eng].append((waits, fn, tok))
        self._update(tok, eng, reads, writes)
        return tok

    def dma(self, out_ap, in_ap, reads=(), writes=(), eng="sp"):
        k = self.dnext
        self.dnext = (k + 1) % self.ndma
        waits = self._hazards(reads, writes)
        if self.dlast[k] is not None:
            waits.append(self.dlast[k])
        self.dcount[k] += 16
        tok = Tok("dma")
        tok.needed = True
        tok.sem = self.dsem[k]
        tok.val = self.dcount[k]
        sem = self.dsem[k]

        def fn(e, out_ap=out_ap, in_ap=in_ap, sem=sem):
            return e.dma_start(out=out_ap, in_=in_ap).then_inc(sem, 16)

        self.ops[eng].append((waits, fn, tok))
        self.dlast[k] = tok
        self.uid += 1
        self._update(tok, "dma%d" % self.uid, reads, writes)
        return tok

    def finalize(self, block):
        for e in self.ENG:
            for waits, fn, tok in self.ops[e]:
                for w in waits:
                    if w.eng == "dma":
                        continue
                    if w.eng == e and (e == "pe" or e == "sp" or not SELF_SYNC):
                        continue
                    w.needed = True
        for t in self.out_toks:
            t.needed = True
        for e in self.ENG:
            c = 0
            for waits, fn, tok in self.ops[e]:
                if tok.eng != "dma" and tok.needed:
                    c += 1
                    tok.val = c
                    tok.sem = self.esem[e]

        def run(h, e):
            seen = {}
            for waits, fn, tok in self.ops[e]:
                best = {}
                for w in waits:
                    if not w.needed or w.val is None:
                        continue
                    if w.eng == e and (e == "pe" or e == "sp" or not SELF_SYNC):
                        continue
                    sid = id(w.sem)
                    if seen.get(sid, 0) >= w.val:
                        continue
                    if sid not in best or best[sid][1] < w.val:
                        best[sid] = (w.sem, w.val)
                for sid, (sem, val) in best.items():
                    h.wait_ge(sem, val)
                    seen[sid] = val
                ins = fn(h)
                if tok.eng != "dma" and tok.needed:
                    ins.then_inc(self.esem[e], 1)
            if e == "sp":
                for t in self.out_toks:
                    if seen.get(id(t.sem), 0) < t.val:
                        h.wait_ge(t.sem, t.val)
                        seen[id(t.sem)] = t.val

        @block.sync
        def _(h):
            run(h, "sp")

        @block.scalar
        def _(h):
            run(h, "act")

        @block.tensor
        def _(h):
            run(h, "pe")

        @block.vector
        def _(h):
            run(h, "dve")

        @block.gpsimd
        def _(h):
            run(h, "pool")


class Arena:
    BASE = 16640
    END = 229376

    def __init__(self, nc):
        self.nc = nc
        self.off = self.BASE
        self.n = 0

    def alloc(self, shape, dtype):
        nbytes = int(np.prod(shape[1:])) * (4 if dtype == F32 else 2)
        nbytes = (nbytes + 63) // 64 * 64
        assert self.off + nbytes <= self.END, ("SBUF overflow", self.off, nbytes)
        self.n += 1
        t = self.nc.alloc_sbuf_tensor_at("t%d" % self.n, list(shape), dtype, offset=self.off)
        self.off += nbytes
        return t

    def mark(self):
        return self.off

    def reset(self, m):
        self.off = m


def build_nc(debug=False):
    nc = bass.Bass("TRN2", target_bir_lowering=False)
    dt_ = nc.dram_tensor
    x_d = dt_("x", [S, D], F32, kind="ExternalInput")
    win_d = dt_("win", [8, 128, 3072], F32, kind="ExternalInput")
    wout_d = dt_("wout", [128, 8192], F32, kind="ExternalInput")
    wffn_d = dt_("wffn", [NFC, 128, 3072], F32, kind="ExternalInput")
    n1_d = dt_("n1", [1, D], F32, kind="ExternalInput")
    n2_d = dt_("n2", [1, D], F32, kind="ExternalInput")
    pv_d = dt_("pv", [128, 4], F32, kind="ExternalInput")
    lam_d = dt_("lam", [1, 256], F32, kind="ExternalInput")
    rb_d = dt_("rb", [32, 4], F32, kind="ExternalInput")
    cst_d = dt_("cst", [128, NCOL], F32, kind="ExternalInput")
    y_d = dt_("y", [S, D], F32, kind="ExternalOutput")
    kind_s = "ExternalOutput" if debug else "Internal"
    wi_s = dt_("wi_s", [8, 128, 3072], BF16, kind="Internal")
    wo_s = dt_("wo_s", [128, 8192], BF16, kind="Internal")
    wf_s = dt_("wf_s", [NFC, 128, 3072], BF16, kind="Internal")
    mix_s = dt_("mix_s", [8, 128, S], BF16, kind=kind_s)
    flat_s = dt_("flat_s", [4, 128, 768], F32, kind="Internal")

    stack = ExitStack()
    with stack:
        P = Prog(nc, stack)
        ps = stack.enter_context(nc.psum_tensor("ps", [128, 8, 512], F32))
        bank = [Buf() for _ in range(8)]
        bank7hi = Buf()

        def psT(b):
            return ps[:, b, :].bitcast(BF16)

        A = Arena(nc)
        dbg_list = []

        def dbg(name, ap, shape, dtype, reads):
            if not debug:
                return
            t = dt_(name, list(shape), dtype, kind="ExternalOutput")
            tk = P.dma(t.ap(), ap, reads=reads)
            P.out_toks.append(tk)
        cb = A.alloc([128, CBW], BF16)
        bhi = A.alloc([128, 4, 640], BF16)
        blo = A.alloc([128, 4, 640], BF16)
        pvt = A.alloc([128, 16], F32)
        b15 = A.alloc([128, 4], F32)
        B_cb, B_bband, B_pvt, B_b15 = Buf(), Buf(), Buf(), Buf()
        ident = cb[:, C_ID:C_ID + 128]
        ntri = cb[:, C_NTRI:C_NTRI + 128]
        ones_b = cb[:, C_ONES:C_ONES + 128]
        bd64 = cb[:, C_BD:C_BD + 128]
        nselA = cb[0:34, C_NSA:C_NSA + 128]
        nselB = cb[0:34, C_NSB:C_NSB + 128]
        EA = cb[:, C_EA:C_EA + 34]
        EB = cb[:, C_EB:C_EB + 34]
        trim = cb[:, C_TM:C_TM + 128]
        gq8 = pvt[:, 4:5]
        gk = pvt[:, 1:2]
        gdo8 = pvt[:, 5:6]
        gsb = pvt[:, 3:4]
        neglam = pvt[:, 6:7]
        m_conv = A.mark()
        st32 = [A.alloc([128, 3072], F32)]
        st16 = [A.alloc([128, 3072], BF16)]
        m_persist = A.mark()
        uT = A.alloc([128, 8, S], BF16)
        B_uT = [Buf() for _ in range(32)]
        m_p2 = A.mark()

        cst = A.alloc([128, NCOL], F32)
        bband = A.alloc([128, 4, 640], F32)
        B_bb32 = Buf()
        lamb = A.alloc([128, 256], F32)
        ltmp = A.alloc([128, 128], F32)
        rb32 = A.alloc([32, 4], F32)
        rbrep = A.alloc([32, 4, 128], F32)
        gb = [A.alloc([128, 768], F32) for _ in range(2)]
        B_cst, B_lamb, B_ltmp, B_rb = Buf(), Buf(), Buf(), Buf()
        B_g = [Buf(), Buf()]
        B_rbrep = Buf()
        B_flat = Buf()

        P.dma(cst[:, :], cst_d.ap(), writes=[B_cst])
        P.dma(pvt[:, 0:4], pv_d.ap(), writes=[B_pvt])
        P.dma(lamb[:, :], bass.AP(lam_d, 0, [[0, 128], [1, 256]]), writes=[B_lamb])
        P.dma(b15[:, :], bass.AP(rb_d, 15 * 4, [[0, 128], [1, 4]]), writes=[B_b15])
        P.dma(rb32[:, :], rb_d.ap(), writes=[B_rb])
        P.op("dve", lambda e: e.tensor_copy(out=cb[:, :], in_=cst[:, 0:CBW]), reads=[B_cst], writes=[B_cb])
        P.op("dve", lambda e: e.tensor_scalar(out=pvt[:, 4:5], in0=pvt[:, 0:1], scalar1=0.125, scalar2=None, op0=ALU.mult),
             reads=[], writes=[B_pvt])
        P.op("dve", lambda e: e.tensor_scalar(out=pvt[:, 5:6], in0=pvt[:, 2:3], scalar1=0.8, scalar2=None, op0=ALU.mult),
             reads=[], writes=[B_pvt])
        P.op("dve", lambda e: e.tensor_tensor(out=ltmp[:, 0:64], in0=lamb[:, 0:64], in1=lamb[:, 64:128], op=ALU.mult),
             reads=[B_lamb], writes=[B_ltmp])
        P.op("dve", lambda e: e.tensor_tensor(out=ltmp[:, 64:128], in0=lamb[:, 128:192], in1=lamb[:, 192:256], op=ALU.mult),
             reads=[B_lamb], writes=[B_ltmp])
        P.op("dve", lambda e: e.reduce_sum(out=pvt[:, 9:10], in_=ltmp[:, 0:64], axis=AX.X), reads=[B_ltmp], writes=[B_pvt])
        P.op("dve", lambda e: e.reduce_sum(out=pvt[:, 10:11], in_=ltmp[:, 64:128], axis=AX.X), reads=[B_ltmp], writes=[B_pvt])
        P.op("act", lambda e: e.activation(out=pvt[:, 7:9], in_=pvt[:, 9:11], func=AF.Exp), reads=[B_pvt], writes=[B_pvt])
        P.op("dve", lambda e: e.tensor_tensor(out=pvt[:, 6:7], in0=pvt[:, 8:9], in1=pvt[:, 7:8], op=ALU.subtract),
             reads=[B_pvt], writes=[B_pvt])
        P.op("dve", lambda e: e.tensor_scalar(out=pvt[:, 6:7], in0=pvt[:, 6:7], scalar1=-0.2, scalar2=None, op0=ALU.add),
             reads=[B_pvt], writes=[B_pvt])
        A.reset(A.mark())

        st32.append(A.alloc([128, 3072], F32))
        st16.append(A.alloc([128, 3072], BF16))
        B_st32 = [Buf(), Buf()]
        B_st16 = [Buf(), Buf()]
        B_wi = [Buf() for _ in range(8)]
        B_wo = Buf()
        B_wf = [Buf() for _ in range(NFC)]
        slabs = []
        for g in GROUP_ORDER:
            slabs.append((win_d.ap()[g], wi_s.ap()[g], B_wi[g], 3072))
        for c, (lo, hi) in enumerate([(0, 3072), (3072, 6144), (6144, 8192)]):
            slabs.append((wout_d.ap()[:, lo:hi], wo_s.ap()[:, lo:hi], B_wo, hi - lo))
        for fc in range(NFC):
            slabs.append((wffn_d.ap()[fc], wf_s.ap()[fc], B_wf[fc], 3072))
        conv_state = {"i": 0, "pending": None, "npair": 2}

        def conv_flush():
            pend = conv_state["pending"]
            if pend is None:
                return
            conv_state["pending"] = None
            dst, bdst, w, k = pend
            old_wr = dict(bdst.wr)
            P.dma(dst, st16[k][:, 0:w], reads=[B_st16[k]], writes=[bdst])
            for kk, vv in old_wr.items():
                if kk.startswith("dma"):
                    bdst.wr[kk] = vv

        def conv_cast_bg():
            ld = conv_state.get("loaded")
            if ld is None:
                return
            conv_state["loaded"] = None
            dst, bdst, w = ld
            for c0 in range(0, w, 1024):
                c1 = min(w, c0 + 1024)
                P.op("dve", lambda e, c0=c0, c1=c1: e.tensor_copy(out=st16[0][:, c0:c1], in_=st32[0][:, c0:c1]),
                     reads=[B_st32[0]], writes=[B_st16[0]])
            conv_state["pending"] = (dst, bdst, w, 0)
            conv_flush()

        def emit_convert(n=1):
            for _ in range(n):
                if conv_state["npair"] == 2:
                    i = conv_state["i"]
                    if i >= len(slabs):
                        conv_flush()
                        return
                    conv_state["i"] = i + 1
                    src, dst, bdst, w = slabs[i]
                    k = i % 2
                    P.dma(st32[k][:, 0:w], src, writes=[B_st32[k]])
                    conv_flush()
                    P.op("pool", lambda e, k=k, w=w: e.tensor_copy(out=st16[k][:, 0:w], in_=st32[k][:, 0:w]),
                         reads=[B_st32[k]], writes=[B_st16[k]])
                    conv_state["pending"] = (dst, bdst, w, k)
                else:
                    conv_cast_bg()
                    i = conv_state["i"]
                    if i >= len(slabs):
                        return
                    conv_state["i"] = i + 1
                    src, dst, bdst, w = slabs[i]
                    P.dma(st32[0][:, 0:w], src, writes=[B_st32[0]])
                    conv_state["loaded"] = (dst, bdst, w)

        w1b = A.alloc([128, D], F32)
        B_w1b = Buf()
        xb = [A.alloc([128, D], F32) for _ in range(3)]
        B_xb = [Buf() for _ in range(3)]
        xn = [A.alloc([128, D], BF16) for _ in range(2)]
        B_xn = [Buf(), Buf()]
        junk = A.alloc([128, D], BF16)
        B_junk = Buf()
        st1 = A.alloc([128, 8], F32)
        B_st1 = [Buf(), Buf()]
        P.dma(w1b[:, :], bass.AP(n1_d, 0, [[0, 128], [1, D]]), writes=[B_w1b])

        def norm_block(src_ap, B_src, wbt, B_wbt, xn_t, B_xn_t, stt, B_stt, col):
            P.op("act", lambda e: e.activation(out=junk[:, :], in_=src_ap, func=AF.Square, accum_out=stt[:, col:col + 1]),
                 reads=[B_src], writes=[B_junk, B_stt])
            P.op("act", lambda e: e.activation(out=stt[:, col + 1:col + 2], in_=stt[:, col:col + 1], func=AF.Ln, scale=1.0 / D, bias=EPS),
                 reads=[B_stt], writes=[B_stt])
            P.op("act", lambda e: e.activation(out=stt[:, col + 2:col + 3], in_=stt[:, col + 1:col + 2], func=AF.Exp, scale=-0.5),
                 reads=[B_stt], writes=[B_stt])
            P.op("dve", lambda e: e.scalar_tensor_tensor(out=xn_t[:, :], in0=src_ap, scalar=stt[:, col + 2:col + 3], in1=wbt[:, :],
                                                          op0=ALU.mult, op1=ALU.mult),
                 reads=[B_src, B_stt, B_wbt], writes=[B_xn_t])

        def transpose_block(xn_t, B_xn_t, bk, dst_ap, B_dst, evac_eng):
            pv = psT(bk)
            for kc in range(8):
                P.op("pe", lambda e, kc=kc: e.transpose(out=pv[:, kc * 128:(kc + 1) * 128], in_=xn_t[:, kc * 128:(kc + 1) * 128], identity=ident),
                     reads=[B_xn_t, B_cb], writes=[bank[bk]])
            src = pv.rearrange("p (k t) -> p k t", k=8)
            if evac_eng == "act":
                P.op("act", lambda e: e.copy(out=dst_ap, in_=src), reads=[bank[bk]], writes=[B_dst])
            else:
                P.op("dve", lambda e: e.tensor_copy(out=dst_ap, in_=src), reads=[bank[bk]], writes=[B_dst])

        xa = x_d.ap()

        def p1_A(tb):
            k3 = tb % 3
            k2 = tb % 2
            P.dma(xb[k3][:, :], xa[tb * 128:(tb + 1) * 128, :], writes=[B_xb[k3]])
            if tb < 2:
                emit_convert(1)
            norm_block(xb[k3][:, :], B_xb[k3], w1b, B_w1b, xn[k2], B_xn[k2], st1, B_st1[k2], 4 * k2)

        def p1_B(tb):
            k2 = tb % 2
            transpose_block(xn[k2], B_xn[k2], 6 + k2, uT[:, :, tb * 128:(tb + 1) * 128], B_uT[tb], "dve")

        for tb in range(33):
            if tb < 32:
                p1_A(tb)
            if tb >= 1:
                p1_B(tb - 1)
        for h in range(4):
            P.op("dve", lambda e, h=h: e.tensor_scalar(out=rbrep[:, h, :], in0=cst[0:32, C_ONES:C_ONES + 128], scalar1=rb32[:, h:h + 1], scalar2=None, op0=ALU.mult),
                 reads=[B_cst, B_rb], writes=[B_rbrep])
        for h in range(4):
            k = h % 2
            P.op("pe", lambda e, h=h: e.matmul(ps[:, 0, :], lhsT=rbrep[:, h, :], rhs=cst[0:32, C_OH:C_OH + 512], start=True, stop=True),
                 reads=[B_rbrep, B_cst], writes=[bank[0]])
            P.op("pe", lambda e, h=h: e.matmul(ps[:, 1, 0:256], lhsT=rbrep[:, h, :], rhs=cst[0:32, C_OH + 512:C_OH + 768], start=True, stop=True),
                 reads=[B_rbrep, B_cst], writes=[bank[1]])
            P.op("dve", lambda e, k=k: e.tensor_copy(out=gb[k][:, 0:512], in_=ps[:, 0, :]), reads=[bank[0]], writes=[B_g[k]])
            P.op("dve", lambda e, k=k: e.tensor_copy(out=gb[k][:, 512:768], in_=ps[:, 1, 0:256]), reads=[bank[1]], writes=[B_g[k]])
            P.dma(flat_s.ap()[h], gb[k][:, :], reads=[B_g[k]], writes=[B_flat])
            P.dma(bband[:, h, :], bass.AP(flat_s, h * 128 * 768 + 127, [[767, 128], [1, 640]]), reads=[B_flat], writes=[B_bb32])
            P.op("dve", lambda e, h=h: e.tensor_tensor(out=bband[:, h, :], in0=bband[:, h, :], in1=cst[:, C_MNEG:C_MNEG + 640], op=ALU.add),
                 reads=[B_cst], writes=[B_bb32])
            P.op("dve", lambda e, h=h: e.tensor_copy(out=bhi[:, h, :], in_=bband[:, h, :]), reads=[B_bb32], writes=[B_bband])
            P.op("dve", lambda e, h=h: e.tensor_tensor(out=blo[:, h, :], in0=bband[:, h, :], in1=bhi[:, h, :], op=ALU.subtract),
                 reads=[B_bb32], writes=[B_bband])
        conv_flush()
        conv_state["npair"] = 1
        dbg("dbg_uT", uT[:, :, :], [128, 8, S], BF16, B_uT)

        A.reset(m_p2)
        qT = [A.alloc([128, S], BF16) for _ in range(2)]
        kT = [A.alloc([128, S], BF16) for _ in range(2)]
        Vt = [A.alloc([128, 32, 128], BF16) for _ in range(2)]
        B_qT = [[Buf() for _ in range(8)] for _ in range(2)]
        B_kT = [[Buf() for _ in range(8)] for _ in range(2)]
        B_V = [[Buf() for _ in range(8)] for _ in range(2)]
        wsl = [A.alloc([128, 8, 384], BF16) for _ in range(2)]
        B_wsl = [Buf(), Buf()]
        esb = [A.alloc([128, 2, 512], F32) for _ in range(2)]
        B_esb = [Buf(), Buf()]
        spb = [A.alloc([128, 2, 512], BF16) for _ in range(3)]
        B_spb = [Buf() for _ in range(3)]
        Ab = [A.alloc([128, 2, 512], BF16) for _ in range(2)]
        B_Ab = [Buf(), Buf()]
        c32 = A.alloc([34, 512], F32)
        HL = A.alloc([34, 512], BF16)
        B_c32, B_HL = Buf(), Buf()
        sqb = [A.alloc([128, 512], BF16) for _ in range(2)]
        B_sqb = [Buf(), Buf()]
        rsb = [A.alloc([128, 512], F32) for _ in range(2)]
        B_rsb = [Buf(), Buf()]
        pp = [A.alloc([128, 512], F32) for _ in range(5)]
        B_pp = [Buf() for _ in range(5)]
        mt = [A.alloc([128, 512], BF16) for _ in range(2)]
        B_mt = [Buf(), Buf()]
        B_mix = [[Buf() for _ in range(8)] for _ in range(8)]
        p1_bufs = [B_w1b, B_junk] + B_xb + B_xn + B_st1 + [B_cst, B_lamb, B_ltmp, B_rb, B_rbrep, B_bb32, B_st32[1], B_st16[1]] + B_g
        p2_first = {"done": False}

        def p2_guard():
            return p1_bufs if not p2_first["done"] else []

        pbank = {"i": 0}

        def next_bank(cands):
            b = cands[pbank["i"] % len(cands)]
            pbank["i"] += 1
            return b

        def load_w(g, sl):
            guard = p2_guard()
            p2_first["done"] = True
            P.dma(wsl[sl][:, :, :].rearrange("p k c -> p (k c)"), wi_s.ap()[g], reads=[B_wi[g]], writes=[B_wsl[sl]] + guard)

        def project(g, sl):
            is_diff = g < 4
            for tt in range(8):
                tsl = slice(tt * 512, (tt + 1) * 512)
                for which in range(2):
                    bk = next_bank([0, 1, 2, 3])
                    dstT, B_dst = (qT, B_qT) if which == 0 else (kT, B_kT)
                    off = which * 128
                    for kc in range(8):
                        P.op("pe", lambda e, bk=bk, kc=kc, off=off, tsl=tsl: e.matmul(ps[:, bk, :], lhsT=wsl[sl][:, kc, off:off + 128], rhs=uT[:, kc, tsl],
                                                                                   start=(kc == 0), stop=(kc == 7)),
                             reads=[B_wsl[sl]] + B_uT[tt * 4:tt * 4 + 4], writes=[bank[bk]])
                    dst_ap = dstT[sl][:, tsl]
                    if not is_diff:
                        if which == 0:
                            P.op("act", lambda e, bk=bk, dst_ap=dst_ap: e.mul(dst_ap, ps[:, bk, :], 0.125),
                                 reads=[bank[bk]], writes=[B_dst[sl][tt]])
                        else:
                            P.op("dve", lambda e, bk=bk, dst_ap=dst_ap: e.tensor_copy(out=dst_ap, in_=ps[:, bk, :]),
                                 reads=[bank[bk]], writes=[B_dst[sl][tt]])
                    else:
                        k2 = (tt * 2 + which) % 2
                        b2 = 4 + k2
                        P.op("act", lambda e, bk=bk, k2=k2: e.activation(out=sqb[k2][:, :], in_=ps[:, bk, :], func=AF.Square),
                             reads=[bank[bk]], writes=[B_sqb[k2]])
                        P.op("pe", lambda e, b2=b2, k2=k2: e.matmul(ps[:, b2, :], lhsT=bd64, rhs=sqb[k2][:, :], start=True, stop=True),
                             reads=[B_sqb[k2], B_cb], writes=[bank[b2]])
                        P.op("act", lambda e, b2=b2, k2=k2: e.activation(out=rsb[k2][:, :], in_=ps[:, b2, :], func=AF.Ln, scale=1.0 / 64, bias=EPS),
                             reads=[bank[b2]], writes=[B_rsb[k2]])
                        P.op("act", lambda e, k2=k2: e.activation(out=rsb[k2][:, :], in_=rsb[k2][:, :], func=AF.Exp, scale=-0.5),
                             reads=[B_rsb[k2]], writes=[B_rsb[k2]])
                        gsc = gq8 if which == 0 else gk
                        P.op("dve", lambda e, bk=bk, k2=k2, dst_ap=dst_ap, gsc=gsc: e.scalar_tensor_tensor(
                            out=dst_ap, in0=ps[:, bk, :], scalar=gsc, in1=rsb[k2][:, :], op0=ALU.mult, op1=ALU.mult),
                             reads=[bank[bk], B_rsb[k2], B_pvt], writes=[B_dst[sl][tt]])
                bk = next_bank([6, 7])
                for i4 in range(4):
                    tb = tt * 4 + i4
                    for kc in range(8):
                        P.op("pe", lambda e, bk=bk, kc=kc, tb=tb, i4=i4: e.matmul(ps[:, bk, i4 * 128:(i4 + 1) * 128],
                                                                                 lhsT=uT[:, kc, tb * 128:(tb + 1) * 128], rhs=wsl[sl][:, kc, 256:384],
                                                                                 start=(kc == 0), stop=(kc == 7)),
                             reads=[B_wsl[sl], B_uT[tb]], writes=[bank[bk]])
                veng = "dve" if (tt % 2 == 0) else "act"
                vdst = Vt[sl][:, tt * 4:(tt + 1) * 4, :]
                vsrc = ps[:, bk, :].rearrange("p (a b) -> p a b", a=4)
                if veng == "dve":
                    P.op("dve", lambda e, vdst=vdst, vsrc=vsrc: e.tensor_copy(out=vdst, in_=vsrc), reads=[bank[bk]], writes=[B_V[sl][tt]])
                else:
                    P.op("act", lambda e, vdst=vdst, vsrc=vsrc: e.copy(out=vdst, in_=vsrc), reads=[bank[bk]], writes=[B_V[sl][tt]])

        def project_units(g, sl):
            is_diff = g < 4
            units = []

            def mk_qk(tt, which):
                def unit(bA, bB):
                    tsl = slice(tt * 512, (tt + 1) * 512)
                    dstT, B_dst = (qT, B_qT) if which == 0 else (kT, B_kT)
                    off = which * 128
                    for kc in range(8):
                        P.op("pe", lambda e, kc=kc: e.matmul(ps[:, bA, :], lhsT=wsl[sl][:, kc, off:off + 128], rhs=uT[:, kc, tsl],
                                                             start=(kc == 0), stop=(kc == 7)),
                             reads=[B_wsl[sl]] + B_uT[tt * 4:tt * 4 + 4], writes=[bank[bA]])
                    dst_ap = dstT[sl][:, tsl]
                    if not is_diff:
                        if which == 0:
                            P.op("dve", lambda e: e.tensor_scalar(out=dst_ap, in0=ps[:, bA, :], scalar1=0.125, scalar2=None, op0=ALU.mult),
                                 reads=[bank[bA]], writes=[B_dst[sl][tt]])
                        else:
                            P.op("dve", lambda e: e.tensor_copy(out=dst_ap, in_=ps[:, bA, :]), reads=[bank[bA]], writes=[B_dst[sl][tt]])
                    else:
                        k2 = (tt * 2 + which) % 2
                        P.op("act", lambda e: e.activation(out=sqb[k2][:, :], in_=ps[:, bA, :], func=AF.Square),
                             reads=[bank[bA]], writes=[B_sqb[k2]])
                        P.op("pe", lambda e: e.matmul(ps[:, bB, :], lhsT=bd64, rhs=sqb[k2][:, :], start=True, stop=True),
                             reads=[B_sqb[k2], B_cb], writes=[bank[bB]])
                        P.op("act", lambda e: e.activation(out=rsb[k2][:, :], in_=ps[:, bB, :], func=AF.Ln, scale=1.0 / 64, bias=EPS),
                             reads=[bank[bB]], writes=[B_rsb[k2]])
                        P.op("act", lambda e: e.activation(out=rsb[k2][:, :], in_=rsb[k2][:, :], func=AF.Exp, scale=-0.5),
                             reads=[B_rsb[k2]], writes=[B_rsb[k2]])
                        gsc = gq8 if which == 0 else gk
                        P.op("dve", lambda e: e.scalar_tensor_tensor(out=dst_ap, in0=ps[:, bA, :], scalar=gsc, in1=rsb[k2][:, :], op0=ALU.mult, op1=ALU.mult),
                             reads=[bank[bA], B_rsb[k2], B_pvt], writes=[B_dst[sl][tt]])
                return unit

            def mk_v(tt, half):
                def unit(bA, bB):
                    for i2 in range(2):
                        tb = tt * 4 + half * 2 + i2
                        for kc in range(8):
                            P.op("pe", lambda e, kc=kc, tb=tb, i2=i2: e.matmul(ps[:, bA, i2 * 128:(i2 + 1) * 128],
                                                                                lhsT=uT[:, kc, tb * 128:(tb + 1) * 128], rhs=wsl[sl][:, kc, 256:384],
                                                                                start=(kc == 0), stop=(kc == 7)),
                                 reads=[B_wsl[sl], B_uT[tb]], writes=[bank[bA]])
                    vdst = Vt[sl][:, tt * 4 + half * 2:tt * 4 + half * 2 + 2, :]
                    vsrc = ps[:, bA, 0:256].rearrange("p (a b) -> p a b", a=2)
                    P.op("dve", lambda e: e.tensor_copy(out=vdst, in_=vsrc), reads=[bank[bA]], writes=[B_V[sl][tt]])
                return unit

            for tt in range(8):
                units.append(mk_qk(tt, 0))
                units.append(mk_qk(tt, 1))
                units.append(mk_v(tt, 0))
                units.append(mk_v(tt, 1))
            return units

        def project_units_half(g, sl):
            units = []

            def mk_qk(tt, which, hf):
                def unit():
                    tsl = slice(tt * 512, (tt + 1) * 512)
                    dstT, B_dst = (qT, B_qT) if which == 0 else (kT, B_kT)
                    off = which * 128 + hf * 64
                    for kc in range(8):
                        P.op("pe", lambda e, kc=kc: e.matmul(ps[64:128, 7, :], lhsT=wsl[sl][:, kc, off:off + 64], rhs=uT[:, kc, tsl],
                                                             start=(kc == 0), stop=(kc == 7)),
                             reads=[B_wsl[sl]] + B_uT[tt * 4:tt * 4 + 4], writes=[bank7hi])
                    dst_ap = dstT[sl][hf * 64:(hf + 1) * 64, tsl]
                    if which == 0:
                        P.op("dve", lambda e: e.tensor_scalar(out=dst_ap, in0=ps[64:128, 7, :], scalar1=0.125, scalar2=None, op0=ALU.mult),
                             reads=[bank7hi], writes=[B_dst[sl][tt]])
                    else:
                        P.op("dve", lambda e: e.tensor_copy(out=dst_ap, in_=ps[64:128, 7, :]), reads=[bank7hi], writes=[B_dst[sl][tt]])
                return unit

            def mk_v(tb):
                def unit():
                    for hb in range(2):
                        for kc in range(8):
                            P.op("pe", lambda e, kc=kc, hb=hb: e.matmul(ps[64:128, 7, hb * 128:(hb + 1) * 128],
                                                                         lhsT=uT[:, kc, tb * 128 + hb * 64:tb * 128 + hb * 64 + 64], rhs=wsl[sl][:, kc, 256:384],
                                                                         start=(kc == 0), stop=(kc == 7)),
                                 reads=[B_wsl[sl], B_uT[tb]], writes=[bank7hi])
                    for hb in range(2):
                        P.op("dve", lambda e, hb=hb: e.tensor_copy(out=Vt[sl][hb * 64:(hb + 1) * 64, tb, :], in_=ps[64:128, 7, hb * 128:(hb + 1) * 128]),
                             reads=[bank7hi], writes=[B_V[sl][tb // 4]])
                return unit

            for tt in range(8):
                for hf in range(2):
                    units.append(mk_qk(tt, 0, hf))
                for hf in range(2):
                    units.append(mk_qk(tt, 1, hf))
                for i4 in range(4):
                    units.append(mk_v(tt * 4 + i4))
            return units

        def step_geom(t, j):
            m = j - 4 * t
            if m >= 0:
                return 128 * m, 512 - 128 * m, "diag", 0
            if m == -1:
                return 0, 512, "near", 128
            return 0, 512, "far", 0

        def out_norm_store(g, t, y_ap, y_bufs, gain_ap, lhs_ones, div, ssb):
            k2 = t % 2
            ssbufs = [bank[ssb]] + ([bank7hi] if ssb == 7 else [])
            P.op("act", lambda e: e.activation(out=sqb[k2][:, :], in_=y_ap, func=AF.Square), reads=y_bufs, writes=[B_sqb[k2]])
            P.op("pe", lambda e: e.matmul(ps[:, ssb, :], lhsT=lhs_ones, rhs=sqb[k2][:, :], start=True, stop=True),
                 reads=[B_sqb[k2], B_cb], writes=ssbufs)
            P.op("act", lambda e: e.activation(out=rsb[k2][:, :], in_=ps[:, ssb, :], func=AF.Ln, scale=1.0 / div, bias=EPS),
                 reads=ssbufs, writes=[B_rsb[k2]])
            P.op("act", lambda e: e.activation(out=rsb[k2][:, :], in_=rsb[k2][:, :], func=AF.Exp, scale=-0.5),
                 reads=[B_rsb[k2]], writes=[B_rsb[k2]])
            P.op("dve", lambda e: e.scalar_tensor_tensor(out=mt[k2][:, :], in0=y_ap, scalar=gain_ap, in1=rsb[k2][:, :], op0=ALU.mult, op1=ALU.mult),
                 reads=y_bufs + [B_rsb[k2], B_pvt], writes=[B_mt[k2]])
            P.dma(mix_s.ap()[g, :, t * 512:(t + 1) * 512], mt[k2][:, :], reads=[B_mt[k2]], writes=[B_mix[g][t]])
            emit_convert(1)

        def attn_diff(g, sl):
            h = g
            deferred = []
            for t in range(8):
                ns = 4 * t + 4
                qbuf = [B_qT[sl][t]]

                def Z(s):
                    j = 4 * t + 3 - s
                    qlo, N, kind, u0 = step_geom(t, j)
                    st_ = (s % 2) * 2
                    ksl = slice(j * 128, (j + 1) * 128)
                    qsl = slice(t * 512 + qlo, (t + 1) * 512)
                    far = (kind == "far")
                    for c in range(2):
                        P.op("pe", lambda e, c=c: e.matmul(ps[:, st_ + c, qlo:512], lhsT=kT[sl][c * 64:(c + 1) * 64, ksl], rhs=qT[sl][c * 64:(c + 1) * 64, qsl],
                                                           start=True, stop=far),
                             reads=qbuf + [B_kT[sl][j // 4]], writes=[bank[st_ + c]])
                    if not far:
                        for c in range(2):
                            P.op("pe", lambda e, c=c: e.matmul(ps[:, st_ + c, qlo:512], lhsT=ident, rhs=bhi[:, h, u0:u0 + N], start=False, stop=False),
                                 reads=[B_bband, B_cb], writes=[bank[st_ + c]])
                            P.op("pe", lambda e, c=c: e.matmul(ps[:, st_ + c, qlo:512], lhsT=ident, rhs=blo[:, h, u0:u0 + N], start=False, stop=True),
                                 reads=[B_bband, B_cb], writes=[bank[st_ + c]])

                def E(s):
                    j = 4 * t + 3 - s
                    qlo, N, kind, u0 = step_geom(t, j)
                    st_ = (s % 2) * 2
                    k2 = s % 2
                    if kind == "far":
                        P.op("act", lambda e: e.activation(out=Ab[k2][:, :, qlo:512], in_=ps[:, st_:st_ + 2, qlo:512], func=AF.Exp, bias=b15[:, h:h + 1]),
                             reads=[bank[st_], bank[st_ + 1], B_b15], writes=[B_Ab[k2]])
                    else:
                        P.op("act", lambda e: e.activation(out=Ab[k2][:, :, qlo:512], in_=ps[:, st_:st_ + 2, qlo:512], func=AF.Exp),
                             reads=[bank[st_], bank[st_ + 1]], writes=[B_Ab[k2]])

                def PV(s):
                    j = 4 * t + 3 - s
                    qlo, N, kind, u0 = step_geom(t, j)
                    k2 = s % 2
                    st0 = (s == 0)
                    sp0 = (s == ns - 1)
                    for c in range(2):
                        P.op("pe", lambda e, c=c: e.matmul(ps[:, 4 + c, qlo:512], lhsT=Vt[sl][:, j, :], rhs=Ab[k2][:, c, qlo:512], start=st0, stop=sp0,
                                                           skip_group_check=True),
                             reads=[B_Ab[k2], B_V[sl][j // 4]], writes=[bank[4 + c]])
                        P.op("pe", lambda e, c=c: e.matmul(ps[:, 6 + c, qlo:512], lhsT=ones_b, rhs=Ab[k2][:, c, qlo:512], start=st0, stop=sp0,
                                                           skip_group_check=True),
                             reads=[B_Ab[k2], B_cb], writes=[bank[6 + c]])

                for it in range(-1, ns):
                    if it + 1 < ns:
                        Z(it + 1)
                        E(it + 1)
                    if it >= 0:
                        PV(it)
                    if it == 1 and deferred:
                        deferred.pop(0)()
                for c in range(2):
                    P.op("dve", lambda e, c=c: e.tensor_copy(out=pp[2 + c][:, :], in_=ps[:, 4 + c, :]), reads=[bank[4 + c]], writes=[B_pp[2 + c]])
                    P.op("act", lambda e, c=c: e.activation(out=pp[c][:, :], in_=ps[:, 6 + c, :], func=AF.Ln), reads=[bank[6 + c]], writes=[B_pp[c]])

                def stage2(t=t):
                    for c in range(2):
                        P.op("act", lambda e, c=c: e.activation(out=pp[c][:, :], in_=pp[c][:, :], func=AF.Exp, scale=-1.0), reads=[B_pp[c]], writes=[B_pp[c]])
                        P.op("dve", lambda e, c=c: e.tensor_tensor(out=pp[2 + c][:, :], in0=pp[2 + c][:, :], in1=pp[c][:, :], op=ALU.mult),
                             reads=[B_pp[c]], writes=[B_pp[2 + c]])
                    P.op("dve", lambda e: e.scalar_tensor_tensor(out=pp[4][:, :], in0=pp[3][:, :], scalar=neglam, in1=pp[2][:, :], op0=ALU.mult, op1=ALU.add),
                         reads=[B_pp[2], B_pp[3], B_pvt], writes=[B_pp[4]])
                    out_norm_store(g, t, pp[4][:, :], [B_pp[4]], gdo8, ones_b, 128.0, 0)

                deferred.append(stage2)
                if t == 7:
                    deferred.pop(0)()

        def attn_sb(g, sl, hosted=None):
            deferred = []
            hosted = hosted if hosted is not None else []
            zc = {"n": 0}
            hcount = {"n": 0}
            for t in range(8):
                ns = 4 * t + 4
                qbuf = [B_qT[sl][t]]
                P.op("dve", lambda e: e.memset(c32[:, :], 0.0), writes=[B_c32])
                P.op("dve", lambda e: e.memset(HL[:, :], 0.0), writes=[B_HL])

                def geo(s):
                    j = 4 * t + 3 - s
                    m = j - 4 * t
                    qlo = 128 * m if m >= 0 else 0
                    return j, m, qlo

                zset = {}

                def Astage(s):
                    j, m, qlo = geo(s)
                    zset[s] = (zc["n"] % 3) * 2
                    zc["n"] += 1
                    st_ = zset[s]
                    ksl = slice(j * 128, (j + 1) * 128)
                    qsl = slice(t * 512 + qlo, (t + 1) * 512)
                    for c in range(2):
                        P.op("pe", lambda e, c=c: e.matmul(ps[:, st_ + c, qlo:512], lhsT=kT[sl][c * 64:(c + 1) * 64, ksl], rhs=qT[sl][c * 64:(c + 1) * 64, qsl],
                                                           start=True, stop=True),
                             reads=qbuf + [B_kT[sl][j // 4]], writes=[bank[st_ + c]])
                    k2 = s % 2
                    k3 = s % 3
                    P.op("act", lambda e: e.activation(out=esb[k2][:, :, qlo:512], in_=ps[:, st_:st_ + 2, qlo:512], func=AF.Exp),
                         reads=[bank[st_], bank[st_ + 1]], writes=[B_esb[k2]])
                    P.op("act", lambda e: e.activation(out=spb[k3][:, :, qlo:512], in_=esb[k2][:, :, qlo:512], func=AF.Ln, bias=1.0),
                         reads=[B_esb[k2]], writes=[B_spb[k3]])
                    if m >= 0:
                        for c in range(2):
                            P.op("dve", lambda e, c=c: e.tensor_tensor(out=spb[k3][:, c, qlo:qlo + 128], in0=spb[k3][:, c, qlo:qlo + 128], in1=trim, op=ALU.mult),
                                 reads=[B_cb], writes=[B_spb[k3]])

                def Bstage(s):
                    j, m, qlo = geo(s)
                    st_ = zset[s]
                    k3 = s % 3
                    last = (s == 0)
                    for c in range(2):
                        P.op("pe", lambda e, c=c: e.matmul(ps[:, st_ + c, qlo:512], lhsT=ntri, rhs=spb[k3][:, c, qlo:512], start=False, stop=last, skip_group_check=True),
                             reads=[B_spb[k3], B_cb], writes=[bank[st_ + c]])
                    if s > 0:
                        for c in range(2):
                            P.op("pe", lambda e, c=c: e.matmul(ps[:, st_ + c, qlo:512], lhsT=(nselA if c == 0 else nselB), rhs=HL[0:34, qlo:512], start=False, stop=True, skip_group_check=True),
                                 reads=[B_HL, B_cb], writes=[bank[st_ + c]])
                    if s < ns - 1:
                        P.op("pe", lambda e: e.matmul(ps[0:34, 7, qlo:512], lhsT=EA, rhs=spb[k3][:, 0, qlo:512], start=True, stop=False),
                             reads=[B_spb[k3], B_cb], writes=[bank[7]])
                        P.op("pe", lambda e: e.matmul(ps[0:34, 7, qlo:512], lhsT=EB, rhs=spb[k3][:, 1, qlo:512], start=False, stop=True),
                             reads=[B_spb[k3], B_cb], writes=[bank[7]])
                        P.op("dve", lambda e: e.tensor_tensor(out=c32[:, qlo:512], in0=c32[:, qlo:512], in1=ps[0:34, 7, qlo:512], op=ALU.add),
                             reads=[bank[7]], writes=[B_c32])
                        P.op("dve", lambda e: e.tensor_copy(out=HL[:, qlo:512], in_=c32[:, qlo:512]), reads=[B_c32], writes=[B_HL])
                        P.op("dve", lambda e: e.tensor_tensor(out=HL[32:34, qlo:512], in0=c32[32:34, qlo:512], in1=HL[32:34, qlo:512], op=ALU.subtract),
                             reads=[B_c32], writes=[B_HL])

                def E2(s):
                    j, m, qlo = geo(s)
                    st_ = zset[s]
                    k2 = s % 2
                    P.op("act", lambda e: e.activation(out=Ab[k2][:, :, qlo:512], in_=ps[:, st_:st_ + 2, qlo:512], func=AF.Exp),
                         reads=[bank[st_], bank[st_ + 1]], writes=[B_Ab[k2]])
                    if m >= 0:
                        for c in range(2):
                            P.op("dve", lambda e, c=c: e.tensor_tensor(out=Ab[k2][:, c, qlo:qlo + 128], in0=Ab[k2][:, c, qlo:qlo + 128], in1=trim, op=ALU.mult),
                                 reads=[B_cb], writes=[B_Ab[k2]])

                def PV(s):
                    j, m, qlo = geo(s)
                    k2 = s % 2
                    st0 = (s == 0)
                    sp0 = (s == ns - 1)
                    for c in range(2):
                        P.op("pe", lambda e, c=c: e.matmul(ps[c * 64:(c + 1) * 64, 6, qlo:512], lhsT=Vt[sl][:, j, c * 64:(c + 1) * 64], rhs=Ab[k2][:, c, qlo:512],
                                                           start=st0, stop=sp0, skip_group_check=True),
                             reads=[B_Ab[k2], B_V[sl][j // 4]], writes=[bank[6]])

                for it in range(-2, ns):
                    if 0 <= it + 2 < ns:
                        Astage(it + 2)
                    if 0 <= it + 1 < ns:
                        Bstage(it + 1)
                        E2(it + 1)
                    if it >= 0:
                        PV(it)
                    if it == 1 and deferred:
                        deferred.pop(0)()
                    if hosted and it >= 0:
                        hcount["n"] += 1
                        if hcount["n"] % 2 == 0:
                            hosted.pop(0)()
                P.op("dve", lambda e: e.tensor_copy(out=pp[4][:, :], in_=ps[:, 6, :]), reads=[bank[6]], writes=[B_pp[4]])
                deferred.append(lambda t=t: out_norm_store(g, t, pp[4][:, :], [B_pp[4]], gsb, bd64, 64.0, 7))
                if t == 7:
                    deferred.pop(0)()
            while hosted:
                hosted.pop(0)()

        order = GROUP_ORDER
        load_w(order[0], 0)
        project(order[0], 0)
        for i, g in enumerate(order):
            sl = i % 2
            nxt = order[i + 1] if i + 1 < 8 else None
            if nxt is not None:
                load_w(nxt, (i + 1) % 2)
            if g >= 4:
                hosted = project_units_half(nxt, (i + 1) % 2) if (nxt is not None and nxt >= 4) else None
                attn_sb(g, sl, hosted)
                if nxt is not None and nxt < 4:
                    project(nxt, (i + 1) % 2)
            else:
                attn_diff(g, sl)
                if nxt is not None:
                    project(nxt, (i + 1) % 2)
        emit_convert(100)

        A.reset(m_conv)
        p2_bufs = ([B_wsl[0], B_wsl[1], B_c32, B_HL, B_st32[0], B_st16[0]] + B_esb + B_spb + B_Ab + B_sqb + B_rsb + B_pp + B_mt + B_uT
                   + [b for sl_ in range(2) for b in B_qT[sl_] + B_kT[sl_] + B_V[sl_]])
        wo = A.alloc([128, 8, D], BF16)
        B_wo_sb = Buf()
        w2b = A.alloc([128, D], F32)
        B_w2b = Buf()
        xh = [A.alloc([128, 4, D], F32) for _ in range(2)]
        B_xh = [[Buf() for _ in range(4)] for _ in range(2)]
        mxt = [A.alloc([128, 8, 512], BF16) for _ in range(2)]
        B_mxt = [Buf(), Buf()]
        u2 = [A.alloc([128, D], BF16) for _ in range(4)]
        B_u2 = [Buf() for _ in range(4)]
        u2T = [A.alloc([128, 8, 512], BF16) for _ in range(2)]
        B_u2T = [[Buf() for _ in range(4)] for _ in range(2)]
        actT = A.alloc([128, NFC, 512], BF16)
        B_actT = [Buf() for _ in range(NFC)]
        sg = [A.alloc([128, 512], F32) for _ in range(2)]
        B_sg = [Buf(), Buf()]
        wgu = [A.alloc([128, 2048], BF16) for _ in range(3)]
        B_wgu = [Buf() for _ in range(3)]
        wd = [A.alloc([128, NFC, 512], BF16) for _ in range(2)]
        B_wd = [[Buf(), Buf()], [Buf(), Buf()]]
        ot = [A.alloc([128, 512], F32) for _ in range(4)]
        B_ot = [Buf() for _ in range(4)]
        junk3 = A.alloc([128, D], BF16)
        st3 = A.alloc([128, 8], F32)
        B_st3 = [Buf(), Buf()]
        B_junk3 = Buf()

        def p3w(extra):
            return extra + p2_bufs

        P.dma(wo[:, :, :].rearrange("p k c -> p (k c)"), wo_s.ap(), reads=[B_wo], writes=p3w([B_wo_sb]))
        P.dma(w2b[:, :], bass.AP(n2_d, 0, [[0, 128], [1, D]]), writes=p3w([B_w2b]))
        first_p3 = {"f": True}

        def p3_load_x(tt):
            k = tt % 2
            extra = p2_bufs if tt < 2 else []
            P.dma(xh[k][:, :, :], xa[tt * 512:(tt + 1) * 512, :].rearrange("(a p) d -> p a d", p=128), writes=B_xh[k] + extra)

        def p3_load_m(tt):
            k = tt % 2
            extra = p2_bufs if tt < 2 else []
            P.dma(mxt[k][:, :, :], mix_s.ap()[:, :, tt * 512:(tt + 1) * 512].rearrange("e p t -> p e t"),
                  reads=[B_mix[g_][tt] for g_ in range(8)], writes=[B_mxt[k]] + extra)

        def p3_loads(tt):
            p3_load_x(tt)
            p3_load_m(tt)

        def load_wgu(tt, fc):
            k3 = fc % 3
            P.dma(wgu[k3][:, :], wf_s.ap()[fc, :, 0:2048], reads=[B_wf[fc]], writes=[B_wgu[k3]] + (p2_bufs if (tt == 0 and fc < 3) else []))

        def norm_block3(src_ap, B_src, xn_t, B_xn_t, col, B_stt):
            P.op("act", lambda e: e.activation(out=junk3[:, :], in_=src_ap, func=AF.Square, accum_out=st3[:, col:col + 1]),
                 reads=[B_src], writes=[B_junk3, B_stt])
            P.op("act", lambda e: e.activation(out=st3[:, col + 1:col + 2], in_=st3[:, col:col + 1], func=AF.Ln, scale=1.0 / D, bias=EPS),
                 reads=[B_stt], writes=[B_stt])
            P.op("act", lambda e: e.activation(out=st3[:, col + 2:col + 3], in_=st3[:, col + 1:col + 2], func=AF.Exp, scale=-0.5),
                 reads=[B_stt], writes=[B_stt])
            P.op("dve", lambda e: e.scalar_tensor_tensor(out=xn_t[:, :], in0=src_ap, scalar=st3[:, col + 2:col + 3], in1=w2b[:, :],
                                                          op0=ALU.mult, op1=ALU.mult),
                 reads=[B_src, B_stt, B_w2b], writes=[B_xn_t])

        ya = y_d.ap()

        def X1(tt):
            k = tt % 2
            for tb in range(4):
                for dh in range(2):
                    bk = 4 + (tb * 2 + dh) % 4
                    for ec in range(8):
                        P.op("pe", lambda e, bk=bk, ec=ec, tb=tb, dh=dh, k=k: e.matmul(ps[:, bk, :], lhsT=mxt[k][:, ec, tb * 128:(tb + 1) * 128],
                                                                                       rhs=wo[:, ec, dh * 512:(dh + 1) * 512], start=(ec == 0), stop=(ec == 7)),
                             reads=[B_mxt[k], B_wo_sb], writes=[bank[bk]])
                    P.op("dve", lambda e, bk=bk, tb=tb, dh=dh, k=k: e.tensor_tensor(out=xh[k][:, tb, dh * 512:(dh + 1) * 512], in0=ps[:, bk, :],
                                                                                      in1=xh[k][:, tb, dh * 512:(dh + 1) * 512], op=ALU.add),
                         reads=[bank[bk]], writes=[B_xh[k][tb]])
                norm_block3(xh[k][:, tb, :], B_xh[k][tb], u2[tb], B_u2[tb], 4 * (tb % 2), B_st3[tb % 2])

        def X2(tt):
            for tb in range(4):
                transpose_block(u2[tb], B_u2[tb], 6 + tb % 2, u2T[tt % 2][:, :, tb * 128:(tb + 1) * 128], B_u2T[tt % 2][tb], "dve")

        def Y1(tt):
            for fc in range(NFC):
                k3 = fc % 3
                if fc >= 3 or tt == 0:
                    load_wgu(tt, fc)
                if fc in (2, 6, 10, 14):
                    ci = (2, 6, 10, 14).index(fc)
                    dh_, c_ = ci // 2, ci % 2
                    P.dma(wd[dh_][:, 11 * c_:11 * c_ + 11, :],
                          wf_s.ap()[11 * c_:11 * c_ + 11, :, 2048 + dh_ * 512:2048 + (dh_ + 1) * 512].rearrange("f p d -> p f d"),
                          reads=B_wf[11 * c_:11 * c_ + 11], writes=[B_wd[dh_][c_]] + (p2_bufs if tt == 0 else []))
                bg = 4 + fc % 2
                bu = 6 + fc % 2
                ut = u2T[tt % 2]
                for kc in range(8):
                    P.op("pe", lambda e, kc=kc, k3=k3, bg=bg, ut=ut: e.matmul(ps[:, bg, :], lhsT=wgu[k3][:, kc * 128:(kc + 1) * 128], rhs=ut[:, kc, :],
                                                                                start=(kc == 0), stop=(kc == 7)),
                         reads=[B_wgu[k3]] + B_u2T[tt % 2], writes=[bank[bg]])
                for kc in range(8):
                    P.op("pe", lambda e, kc=kc, k3=k3, bu=bu, ut=ut: e.matmul(ps[:, bu, :], lhsT=wgu[k3][:, 1024 + kc * 128:1024 + (kc + 1) * 128], rhs=ut[:, kc, :],
                                                                                start=(kc == 0), stop=(kc == 7)),
                         reads=[B_wgu[k3]] + B_u2T[tt % 2], writes=[bank[bu]])
                s2 = fc % 2
                P.op("act", lambda e, s2=s2, bg=bg: e.activation(out=sg[s2][:, :], in_=ps[:, bg, :], func=AF.Silu),
                     reads=[bank[bg]], writes=[B_sg[s2]] + (p2_bufs if (tt == 0 and fc < 2) else []))
                P.op("dve", lambda e, s2=s2, bu=bu, fc=fc: e.tensor_tensor(out=actT[:, fc, :], in0=sg[s2][:, :], in1=ps[:, bu, :], op=ALU.mult),
                     reads=[B_sg[s2], bank[bu]], writes=[B_actT[fc]] + (p2_bufs if (tt == 0 and fc == 0) else []))

        def Y2(tt, dh):
            k = tt % 2
            for fc in range(NFC):
                for tb in range(4):
                    P.op("pe", lambda e, fc=fc, tb=tb, dh=dh: e.matmul(ps[:, tb, :], lhsT=actT[:, fc, tb * 128:(tb + 1) * 128], rhs=wd[dh][:, fc, :],
                                                                          start=(fc == 0), stop=(fc == NFC - 1)),
                         reads=[B_actT[fc], B_wd[dh][fc // 11]], writes=[bank[tb]])
            for tb in range(4):
                o = tb
                P.op("dve", lambda e, tb=tb, dh=dh, o=o, k=k: e.tensor_tensor(out=ot[o][:, :], in0=ps[:, tb, :], in1=xh[k][:, tb, dh * 512:(dh + 1) * 512], op=ALU.add),
                     reads=[bank[tb], B_xh[k][tb]], writes=[B_ot[o]] + (p2_bufs if (tt == 0 and dh == 0) else []))
                tk = P.dma(ya[tt * 512 + tb * 128:tt * 512 + (tb + 1) * 128, dh * 512:(dh + 1) * 512], ot[o][:, :], reads=[B_ot[o]])
                P.out_toks.append(tk)

        p3_loads(0)
        p3_loads(1)
        X1(0)
        X2(0)
        for tt in range(8):
            Y1(tt)
            if tt + 1 < 8:
                for fc_ in range(3):
                    load_wgu(tt + 1, fc_)
                X1(tt + 1)
            if tt + 2 < 8:
                p3_load_m(tt + 2)
            Y2(tt, 0)
            if tt + 1 < 8:
                X2(tt + 1)
            Y2(tt, 1)
            if tt + 2 < 8:
                p3_load_x(tt + 2)

        block = stack.enter_context(nc.Block())
        P.finalize(block)
    return nc


def _t5_bucket_np(rel):
    nb = 16
    max_exact = 8
    ret = (rel > 0).astype(np.int32) * nb
    n = np.abs(rel)
    nf = np.maximum(n, 1).astype(np.float32)
    large = max_exact + (np.log(nf / np.float32(max_exact)) / np.float32(math.log(128 / max_exact))
                         * np.float32(nb - max_exact)).astype(np.int32)
    large = np.minimum(large, nb - 1)
    return ret + np.where(n < max_exact, n, large)


def _const_table():
    c = np.zeros((128, NCOL), np.float32)
    i = np.arange(128)
    c[:, C_ID:C_ID + 128] = np.eye(128, dtype=np.float32)
    c[:, C_NTRI:C_NTRI + 128] = -(i[:, None] >= i[None, :]).astype(np.float32)
    c[:, C_ONES:C_ONES + 128] = 1.0
    c[:, C_BD:C_BD + 128] = ((i[:, None] // 64) == (i[None, :] // 64)).astype(np.float32)
    c[0, C_NSA:C_NSA + 128] = -1.0
    c[32, C_NSA:C_NSA + 128] = -1.0
    c[1, C_NSB:C_NSB + 128] = -1.0
    c[33, C_NSB:C_NSB + 128] = -1.0
    c[:, C_EA + 0] = 1.0
    c[:, C_EA + 32] = 1.0
    c[:, C_EB + 1] = 1.0
    c[:, C_EB + 33] = 1.0
    c[:, C_TM:C_TM + 128] = (i[:, None] < i[None, :]).astype(np.float32)
    u = np.arange(640)
    c[:, C_MNEG:C_MNEG + 640] = np.where((i[:, None] // 64) > (u[None, :] // 64), NEG, 0.0).astype(np.float32)
    s = np.arange(767)
    bk = _t5_bucket_np((127 - s).astype(np.int32))
    c[bk, C_OH + s] = 1.0
    return c


_NC_CACHE = {}


def _prep_shared(inp):
    w_in = np.asarray(inp["w_in"][0], np.float32)
    cols = []
    for g in range(8):
        base = 0 if g < 4 else 1536
        gi = g % 4
        cols.append(np.concatenate([np.arange(base + gi * 128, base + gi * 128 + 128),
                                    np.arange(base + 512 + gi * 128, base + 512 + gi * 128 + 128),
                                    np.arange(base + 1024 + gi * 128, base + 1024 + gi * 128 + 128)]))
    win = np.empty((8, 128, 3072), np.float32)
    w4 = w_in.reshape(8, 128, 3072)
    for g in range(8):
        win[g] = np.transpose(w4[:, :, cols[g]], (1, 0, 2)).reshape(128, 3072)
    wout = np.ascontiguousarray(np.transpose(np.asarray(inp["w_out"][0], np.float32).reshape(8, 128, 1024), (1, 0, 2)).reshape(128, 8192))
    wg = np.asarray(inp["w_gate"][0], np.float32).reshape(8, 128, NFC, 128)
    wu = np.asarray(inp["w_up"][0], np.float32).reshape(8, 128, NFC, 128)
    wdn = np.asarray(inp["w_down"][0], np.float32).reshape(NFC, 128, 1024)
    wffn = np.empty((NFC, 128, 3072), np.float32)
    wffn[:, :, 0:1024] = np.transpose(wg, (2, 1, 0, 3)).reshape(NFC, 128, 1024)
    wffn[:, :, 1024:2048] = np.transpose(wu, (2, 1, 0, 3)).reshape(NFC, 128, 1024)
    wffn[:, :, 2048:3072] = wdn
    p = np.arange(128)
    pv = np.stack([np.asarray(inp["q_norm_w"][0])[p % 64], np.asarray(inp["k_norm_w"][0])[p % 64],
                   np.asarray(inp["diff_out_norm_w"][0])[p], np.asarray(inp["sb_out_norm_w"][0])[p % 64]], axis=1).astype(np.float32)
    lam = np.concatenate([np.asarray(inp["lambda_q1"][0]), np.asarray(inp["lambda_k1"][0]),
                          np.asarray(inp["lambda_q2"][0]), np.asarray(inp["lambda_k2"][0])]).astype(np.float32)[None, :]
    return {
        "win": win, "wout": wout, "wffn": wffn,
        "n1": np.asarray(inp["norm1_w"], np.float32).reshape(1, D),
        "n2": np.asarray(inp["norm2_w"], np.float32).reshape(1, D),
        "pv": np.ascontiguousarray(pv), "lam": np.ascontiguousarray(lam),
        "rb": np.ascontiguousarray(np.asarray(inp["rel_bias"], np.float32)),
        "cst": _const_table(),
    }


def kernel(**inputs):
    x = np.asarray(inputs["x"], np.float32)
    nb = x.shape[0]
    shared = _prep_shared(inputs)
    if "nc" not in _NC_CACHE:
        _NC_CACHE["nc"] = build_nc()
    nc = _NC_CACHE["nc"]
    in_maps = []
    for b in range(nb):
        m = dict(shared)
        m["x"] = np.ascontiguousarray(x[b])
        in_maps.append(m)
    res = run_bass_kernel_spmd(nc, in_maps, core_ids=list(range(nb)))
    return np.stack([np.asarray(r["y"], np.float32) for r in res.results], axis=0)
```

```python
import math
from contextlib import ExitStack

import numpy as np
import concourse.bass as bass
import concourse.mybir as mybir
from concourse.bass_utils import run_bass_kernel_spmd

F32 = mybir.dt.float32
BF16 = mybir.dt.bfloat16
AF = mybir.ActivationFunctionType
ALU = mybir.AluOpType
AX = mybir.AxisListType

S = 4096
D = 1024
DFF = 2816
NFC = DFF // 128
EPS = 1e-6
NEG = -30000.0
SELF_SYNC = True
GROUP_ORDER = [0, 1, 2, 3, 4, 5, 6, 7]

C_ID = 0
C_NTRI = 128
C_ONES = 256
C_BD = 384
C_NSA = 512
C_NSB = 640
C_EA = 768
C_EB = 832
C_TM = 896
CBW = 1024
C_MNEG = 1024
C_OH = 1664
NCOL = 2432


class Tok:
    __slots__ = ("eng", "needed", "val", "sem")

    def __init__(self, eng):
        self.eng = eng
        self.needed = False
        self.val = None
        self.sem = None


class Buf:
    __slots__ = ("wr", "rd")

    def __init__(self):
        self.wr = {}
        self.rd = {}


class Prog:
    ENG = ("sp", "act", "pe", "dve", "pool")

    def __init__(self, nc, stack, ndma=32):
        self.nc = nc
        self.ops = {e: [] for e in self.ENG}
        self.esem = {e: stack.enter_context(nc.semaphore("s_" + e)) for e in self.ENG}
        self.dsem = [stack.enter_context(nc.semaphore("d%d" % i)) for i in range(ndma)]
        self.dcount = [0] * ndma
        self.dlast = [None] * ndma
        self.dnext = 0
        self.ndma = ndma
        self.uid = 0
        self.out_toks = []

    def _hazards(self, reads, writes):
        waits = []
        for b in reads:
            waits += list(b.wr.values())
        for b in writes:
            waits += list(b.wr.values())
            waits += list(b.rd.values())
        return waits

    def _update(self, tok, key, reads, writes):
        for b in reads:
            b.rd[key] = tok
        for b in writes:
            b.wr = {key: tok}
            b.rd = {}

    def op(self, eng, fn, reads=(), writes=()):
        waits = self._hazards(reads, writes)
        tok = Tok(eng)
        self.ops[eng].append((waits, fn, tok))
        self._update(tok, eng, reads, writes)
        return tok

    def dma(self, out_ap, in_ap, reads=(), writes=(), eng="sp"):
        k = self.dnext
        self.dnext = (k + 1) % self.ndma
        waits = self._hazards(reads, writes)
        if self.dlast[k] is not None:
            waits.append(self.dlast[k])
        self.dcount[k] += 16
        tok = Tok("dma")
        tok.needed = True
        tok.sem = self.dsem[k]
        tok.val = self.dcount[k]
        sem = self.dsem[k]

        def fn(e, out_ap=out_ap, in_ap=in_ap, sem=sem):
            return e.dma_start(out=out_ap, in_=in_ap).then_inc(sem, 16)

        self.ops[eng].append((waits, fn, tok))
        self.dlast[k] = tok
        self.uid += 1
        self._update(tok, "dma%d" % self.uid, reads, writes)
        return tok

    def finalize(self, block):
        for e in self.ENG:
            for waits, fn, tok in self.ops[e]:
                for w in waits:
                    if w.eng == "dma":
                        continue
                    if w.eng == e and (e == "pe" or e == "sp" or not SELF_SYNC):
                        continue
                    w.needed = True
        for t in self.out_toks:
            t.needed = True
        for e in self.ENG:
            c = 0
            for waits, fn, tok in self.ops[e]:
                if tok.eng != "dma" and tok.needed:
                    c += 1
                    tok.val = c
                    tok.sem = self.esem[e]

        def run(h, e):
            seen = {}
            for waits, fn, tok in self.ops[e]:
                best = {}
                for w in waits:
                    if not w.needed or w.val is None:
                        continue
                    if w.eng == e and (e == "pe" or e == "sp" or not SELF_SYNC):
                        continue
                    sid = id(w.sem)
                    if seen.get(sid, 0) >= w.val:
                        continue
                    if sid not in best or best[sid][1] < w.val:
                        best[sid] = (w.sem, w.val)
                for sid, (sem, val) in best.items():
                    h.wait_ge(sem, val)
                    seen[sid] = val
                ins = fn(h)
                if tok.eng != "dma" and tok.needed:
                    ins.then_inc(self.esem[e], 1)
            if e == "sp":
                for t in self.out_toks:
                    if seen.get(id(t.sem), 0) < t.val:
                        h.wait_ge(t.sem, t.val)
                        seen[id(t.sem)] = t.val

        @block.sync
        def _(h):
            run(h, "sp")

        @block.scalar
        def _(h):
            run(h, "act")

        @block.tensor
        def _(h):
            run(h, "pe")

        @block.vector
        def _(h):
            run(h, "dve")

        @block.gpsimd
        def _(h):
            run(h, "pool")


class Arena:
    BASE = 16640
    END = 229376

    def __init__(self, nc):
        self.nc = nc
        self.off = self.BASE
        self.n = 0

    def alloc(self, shape, dtype):
        nbytes = int(np.prod(shape[1:])) * (4 if dtype == F32 else 2)
        nbytes = (nbytes + 63) // 64 * 64
        assert self.off + nbytes <= self.END, ("SBUF overflow", self.off, nbytes)
        self.n += 1
        t = self.nc.alloc_sbuf_tensor_at("t%d" % self.n, list(shape), dtype, offset=self.off)
        self.off += nbytes
        return t

    def mark(self):
        return self.off

    def reset(self, m):
        self.off = m


def build_nc(debug=False):
    nc = bass.Bass("TRN2", target_bir_lowering=False)
    dt_ = nc.dram_tensor
    x_d = dt_("x", [S, D], F32, kind="ExternalInput")
    win_d = dt_("win", [8, 128, 3072], F32, kind="ExternalInput")
    wout_d = dt_("wout", [128, 8192], F32, kind="ExternalInput")
    wffn_d = dt_("wffn", [NFC, 128, 3072], F32, kind="ExternalInput")
    n1_d = dt_("n1", [1, D], F32, kind="ExternalInput")
    n2_d = dt_("n2", [1, D], F32, kind="ExternalInput")
    pv_d = dt_("pv", [128, 4], F32, kind="ExternalInput")
    lam_d = dt_("lam", [1, 256], F32, kind="ExternalInput")
    rb_d = dt_("rb", [32, 4], F32, kind="ExternalInput")
    cst_d = dt_("cst", [128, NCOL], F32, kind="ExternalInput")
    y_d = dt_("y", [S, D], F32, kind="ExternalOutput")
    kind_s = "ExternalOutput" if debug else "Internal"
    wi_s = dt_("wi_s", [8, 128, 3072], BF16, kind="Internal")
    wo_s = dt_("wo_s", [128, 8192], BF16, kind="Internal")
    wf_s = dt_("wf_s", [NFC, 128, 3072], BF16, kind="Internal")
    mix_s = dt_("mix_s", [8, 128, S], BF16, kind=kind_s)
    flat_s = dt_("flat_s", [4, 128, 768], F32, kind="Internal")

    stack = ExitStack()
    with stack:
        P = Prog(nc, stack)
        ps = stack.enter_context(nc.psum_tensor("ps", [128, 8, 512], F32))
        bank = [Buf() for _ in range(8)]
        bank7hi = Buf()

        def psT(b):
            return ps[:, b, :].bitcast(BF16)

        A = Arena(nc)
        dbg_list = []

        def dbg(name, ap, shape, dtype, reads):
            if not debug:
                return
            t = dt_(name, list(shape), dtype, kind="ExternalOutput")
            tk = P.dma(t.ap(), ap, reads=reads)
            P.out_toks.append(tk)
        cb = A.alloc([128, CBW], BF16)
        bhi = A.alloc([128, 4, 640], BF16)
        blo = A.alloc([128, 4, 640], BF16)
        pvt = A.alloc([128, 16], F32)
        b15 = A.alloc([128, 4], F32)
        B_cb, B_bband, B_pvt, B_b15 = Buf(), Buf(), Buf(), Buf()
        ident = cb[:, C_ID:C_ID + 128]
        ntri = cb[:, C_NTRI:C_NTRI + 128]
        ones_b = cb[:, C_ONES:C_ONES + 128]
        bd64 = cb[:, C_BD:C_BD + 128]
        nselA = cb[:, C_NSA:C_NSA + 128]
        nselB = cb[:, C_NSB:C_NSB + 128]
        EA = cb[:, C_EA:C_EA + 34]
        EB = cb[:, C_EB:C_EB + 34]
        trim = cb[:, C_TM:C_TM + 128]
        gq8 = pvt[:, 4:5]
        gk = pvt[:, 1:2]
        gdo8 = pvt[:, 5:6]
        gsb = pvt[:, 3:4]
        neglam = pvt[:, 6:7]
        m_conv = A.mark()
        st32 = [A.alloc([128, 3072], F32)]
        st16 = [A.alloc([128, 3072], BF16)]
        m_persist = A.mark()
        uT = A.alloc([128, 8, S], BF16)
        B_uT = [Buf() for _ in range(32)]
        m_p2 = A.mark()

        cst = A.alloc([128, NCOL], F32)
        bband = A.alloc([128, 4, 640], F32)
        B_bb32 = Buf()
        lamb = A.alloc([128, 256], F32)
        ltmp = A.alloc([128, 128], F32)
        rb32 = A.alloc([32, 4], F32)
        rbrep = A.alloc([32, 4, 128], F32)
        gb = [A.alloc([128, 768], F32) for _ in range(2)]
        B_cst, B_lamb, B_ltmp, B_rb = Buf(), Buf(), Buf(), Buf()
        B_g = [Buf(), Buf()]
        B_rbrep = Buf()
        B_flat = Buf()

        P.dma(cst[:, :], cst_d.ap(), writes=[B_cst])
        P.dma(pvt[:, 0:4], pv_d.ap(), writes=[B_pvt])
        P.dma(lamb[:, :], bass.AP(lam_d, 0, [[0, 128], [1, 256]]), writes=[B_lamb])
        P.dma(b15[:, :], bass.AP(rb_d, 15 * 4, [[0, 128], [1, 4]]), writes=[B_b15])
        P.dma(rb32[:, :], rb_d.ap(), writes=[B_rb])
        P.op("dve", lambda e: e.tensor_copy(out=cb[:, :], in_=cst[:, 0:CBW]), reads=[B_cst], writes=[B_cb])
        P.op("dve", lambda e: e.tensor_scalar(out=pvt[:, 4:5], in0=pvt[:, 0:1], scalar1=0.125, scalar2=None, op0=ALU.mult),
             reads=[], writes=[B_pvt])
        P.op("dve", lambda e: e.tensor_scalar(out=pvt[:, 5:6], in0=pvt[:, 2:3], scalar1=0.8, scalar2=None, op0=ALU.mult),
             reads=[], writes=[B_pvt])
        P.op("dve", lambda e: e.tensor_tensor(out=ltmp[:, 0:64], in0=lamb[:, 0:64], in1=lamb[:, 64:128], op=ALU.mult),
             reads=[B_lamb], writes=[B_ltmp])
        P.op("dve", lambda e: e.tensor_tensor(out=ltmp[:, 64:128], in0=lamb[:, 128:192], in1=lamb[:, 192:256], op=ALU.mult),
             reads=[B_lamb], writes=[B_ltmp])
        P.op("dve", lambda e: e.reduce_sum(out=pvt[:, 9:10], in_=ltmp[:, 0:64], axis=AX.X), reads=[B_ltmp], writes=[B_pvt])
        P.op("dve", lambda e: e.reduce_sum(out=pvt[:, 10:11], in_=ltmp[:, 64:128], axis=AX.X), reads=[B_ltmp], writes=[B_pvt])
        P.op("act", lambda e: e.activation(out=pvt[:, 7:9], in_=pvt[:, 9:11], func=AF.Exp), reads=[B_pvt], writes=[B_pvt])
        P.op("dve", lambda e: e.tensor_tensor(out=pvt[:, 6:7], in0=pvt[:, 8:9], in1=pvt[:, 7:8], op=ALU.subtract),
             reads=[B_pvt], writes=[B_pvt])
        P.op("dve", lambda e: e.tensor_scalar(out=pvt[:, 6:7], in0=pvt[:, 6:7], scalar1=-0.2, scalar2=None, op0=ALU.add),
             reads=[B_pvt], writes=[B_pvt])
        A.reset(A.mark())

        st32.append(A.alloc([128, 3072], F32))
        st16.append(A.alloc([128, 3072], BF16))
        B_st32 = [Buf(), Buf()]
        B_st16 = [Buf(), Buf()]
        B_wi = [Buf() for _ in range(8)]
        B_wo = Buf()
        B_wf = [Buf() for _ in range(NFC)]
        slabs = []
        for g in GROUP_ORDER:
            slabs.append((win_d.ap()[g], wi_s.ap()[g], B_wi[g], 3072))
        for c, (lo, hi) in enumerate([(0, 3072), (3072, 6144), (6144, 8192)]):
            slabs.append((wout_d.ap()[:, lo:hi], wo_s.ap()[:, lo:hi], B_wo, hi - lo))
        for fc in range(NFC):
            slabs.append((wffn_d.ap()[fc], wf_s.ap()[fc], B_wf[fc], 3072))
        conv_state = {"i": 0, "pending": None, "npair": 2}

        def conv_flush():
            pend = conv_state["pending"]
            if pend is None:
                return
            conv_state["pending"] = None
            dst, bdst, w, k = pend
            old_wr = dict(bdst.wr)
            P.dma(dst, st16[k][:, 0:w], reads=[B_st16[k]], writes=[bdst])
            for kk, vv in old_wr.items():
                if kk.startswith("dma"):
                    bdst.wr[kk] = vv

        def conv_cast_bg():
            ld = conv_state.get("loaded")
            if ld is None:
                return
            conv_state["loaded"] = None
            dst, bdst, w = ld
            for c0 in range(0, w, 1024):
                c1 = min(w, c0 + 1024)
                P.op("dve", lambda e, c0=c0, c1=c1: e.tensor_copy(out=st16[0][:, c0:c1], in_=st32[0][:, c0:c1]),
                     reads=[B_st32[0]], writes=[B_st16[0]])
            conv_state["pending"] = (dst, bdst, w, 0)
            conv_flush()

        def emit_convert(n=1):
            for _ in range(n):
                if conv_state["npair"] == 2:
                    i = conv_state["i"]
                    if i >= len(slabs):
                        conv_flush()
                        return
                    conv_state["i"] = i + 1
                    src, dst, bdst, w = slabs[i]
                    k = i % 2
                    P.dma(st32[k][:, 0:w], src, writes=[B_st32[k]])
                    conv_flush()
                    P.op("pool", lambda e, k=k, w=w: e.tensor_copy(out=st16[k][:, 0:w], in_=st32[k][:, 0:w]),
                         reads=[B_st32[k]], writes=[B_st16[k]])
                    conv_state["pending"] = (dst, bdst, w, k)
                else:
                    conv_cast_bg()
                    i = conv_state["i"]
                    if i >= len(slabs):
                        return
                    conv_state["i"] = i + 1
                    src, dst, bdst, w = slabs[i]
                    P.dma(st32[0][:, 0:w], src, writes=[B_st32[0]])
                    conv_state["loaded"] = (dst, bdst, w)

        w1b = A.alloc([128, D], F32)
        B_w1b = Buf()
        xb = [A.alloc([128, D], F32) for _ in range(3)]
        B_xb = [Buf() for _ in range(3)]
        xn = [A.alloc([128, D], BF16) for _ in range(2)]
        B_xn = [Buf(), Buf()]
        junk = A.alloc([128, D], BF16)
        B_junk = Buf()
        st1 = A.alloc([128, 8], F32)
        B_st1 = [Buf(), Buf()]
        P.dma(w1b[:, :], bass.AP(n1_d, 0, [[0, 128], [1, D]]), writes=[B_w1b])

        def norm_block(src_ap, B_src, wbt, B_wbt, xn_t, B_xn_t, stt, B_stt, col):
            P.op("act", lambda e: e.activation(out=junk[:, :], in_=src_ap, func=AF.Square, accum_out=stt[:, col:col + 1]),
                 reads=[B_src], writes=[B_junk, B_stt])
            P.op("act", lambda e: e.activation(out=stt[:, col + 1:col + 2], in_=stt[:, col:col + 1], func=AF.Ln, scale=1.0 / D, bias=EPS),
                 reads=[B_stt], writes=[B_stt])
            P.op("act", lambda e: e.activation(out=stt[:, col + 2:col + 3], in_=stt[:, col + 1:col + 2], func=AF.Exp, scale=-0.5),
                 reads=[B_stt], writes=[B_stt])
            P.op("dve", lambda e: e.scalar_tensor_tensor(out=xn_t[:, :], in0=src_ap, scalar=stt[:, col + 2:col + 3], in1=wbt[:, :],
                                                          op0=ALU.mult, op1=ALU.mult),
                 reads=[B_src, B_stt, B_wbt], writes=[B_xn_t])

        def transpose_block(xn_t, B_xn_t, bk, dst_ap, B_dst, evac_eng):
            pv = psT(bk)
            for kc in range(8):
                P.op("pe", lambda e, kc=kc: e.transpose(out=pv[:, kc * 128:(kc + 1) * 128], in_=xn_t[:, kc * 128:(kc + 1) * 128], identity=ident),
                     reads=[B_xn_t, B_cb], writes=[bank[bk]])
            src = pv.rearrange("p (k t) -> p k t", k=8)
            if evac_eng == "act":
                P.op("act", lambda e: e.copy(out=dst_ap, in_=src), reads=[bank[bk]], writes=[B_dst])
            else:
                P.op("dve", lambda e: e.tensor_copy(out=dst_ap, in_=src), reads=[bank[bk]], writes=[B_dst])

        xa = x_d.ap()

        def p1_A(tb):
            k3 = tb % 3
            k2 = tb % 2
            P.dma(xb[k3][:, :], xa[tb * 128:(tb + 1) * 128, :], writes=[B_xb[k3]])
            if tb < 2:
                emit_convert(1)
            norm_block(xb[k3][:, :], B_xb[k3], w1b, B_w1b, xn[k2], B_xn[k2], st1, B_st1[k2], 4 * k2)

        def p1_B(tb):
            k2 = tb % 2
            transpose_block(xn[k2], B_xn[k2], 6 + k2, uT[:, :, tb * 128:(tb + 1) * 128], B_uT[tb], "dve")

        for tb in range(33):
            if tb < 32:
                p1_A(tb)
            if tb >= 1:
                p1_B(tb - 1)
        for h in range(4):
            P.op("dve", lambda e, h=h: e.tensor_scalar(out=rbrep[:, h, :], in0=cst[0:32, C_ONES:C_ONES + 128], scalar1=rb32[:, h:h + 1], scalar2=None, op0=ALU.mult),
                 reads=[B_cst, B_rb], writes=[B_rbrep])
        for h in range(4):
            k = h % 2
            P.op("pe", lambda e, h=h: e.matmul(ps[:, 0, :], lhsT=rbrep[:, h, :], rhs=cst[0:32, C_OH:C_OH + 512], start=True, stop=True),
                 reads=[B_rbrep, B_cst], writes=[bank[0]])
            P.op("pe", lambda e, h=h: e.matmul(ps[:, 1, 0:256], lhsT=rbrep[:, h, :], rhs=cst[0:32, C_OH + 512:C_OH + 768], start=True, stop=True),
                 reads=[B_rbrep, B_cst], writes=[bank[1]])
            P.op("dve", lambda e, k=k: e.tensor_copy(out=gb[k][:, 0:512], in_=ps[:, 0, :]), reads=[bank[0]], writes=[B_g[k]])
            P.op("dve", lambda e, k=k: e.tensor_copy(out=gb[k][:, 512:768], in_=ps[:, 1, 0:256]), reads=[bank[1]], writes=[B_g[k]])
            P.dma(flat_s.ap()[h], gb[k][:, :], reads=[B_g[k]], writes=[B_flat])
            P.dma(bband[:, h, :], bass.AP(flat_s, h * 128 * 768 + 127, [[767, 128], [1, 640]]), reads=[B_flat], writes=[B_bb32])
            P.op("dve", lambda e, h=h: e.tensor_tensor(out=bband[:, h, :], in0=bband[:, h, :], in1=cst[:, C_MNEG:C_MNEG + 640], op=ALU.add),
                 reads=[B_cst], writes=[B_bb32])
            P.op("dve", lambda e, h=h: e.tensor_copy(out=bhi[:, h, :], in_=bband[:, h, :]), reads=[B_bb32], writes=[B_bband])
            P.op("dve", lambda e, h=h: e.tensor_tensor(out=blo[:, h, :], in0=bband[:, h, :], in1=bhi[:, h, :], op=ALU.subtract),
                 reads=[B_bb32], writes=[B_bband])
        conv_flush()
        conv_state["npair"] = 1
        dbg("dbg_uT", uT[:, :, :], [128, 8, S], BF16, B_uT)

        A.reset(m_p2)
        qT = [A.alloc([128, S], BF16) for _ in range(2)]
        kT = [A.alloc([128, S], BF16) for _ in range(2)]
        Vt = [A.alloc([128, 32, 128], BF16) for _ in range(2)]
        B_qT = [[Buf() for _ in range(8)] for _ in range(2)]
        B_kT = [[Buf() for _ in range(8)] for _ in range(2)]
        B_V = [[Buf() for _ in range(8)] for _ in range(2)]
        wsl = [A.alloc([128, 8, 384], BF16) for _ in range(2)]
        B_wsl = [Buf(), Buf()]
        esb = [A.alloc([128, 2, 512], F32) for _ in range(2)]
        B_esb = [Buf(), Buf()]
        spb = [A.alloc([128, 2, 512], BF16) for _ in range(3)]
        B_spb = [Buf() for _ in range(3)]
        Ab = [A.alloc([128, 2, 512], BF16) for _ in range(2)]
        B_Ab = [Buf(), Buf()]
        c32 = A.alloc([34, 512], F32)
        HL = A.alloc([128, 512], BF16)
        B_c32, B_HL = Buf(), Buf()
        sqb = [A.alloc([128, 512], BF16) for _ in range(2)]
        B_sqb = [Buf(), Buf()]
        rsb = [A.alloc([128, 512], F32) for _ in range(2)]
        B_rsb = [Buf(), Buf()]
        pp = [A.alloc([128, 512], F32) for _ in range(5)]
        B_pp = [Buf() for _ in range(5)]
        mt = [A.alloc([128, 512], BF16) for _ in range(2)]
        B_mt = [Buf(), Buf()]
        B_mix = [[Buf() for _ in range(8)] for _ in range(8)]
        p1_bufs = [B_w1b, B_junk] + B_xb + B_xn + B_st1 + [B_cst, B_lamb, B_ltmp, B_rb, B_rbrep, B_bb32, B_st32[1], B_st16[1]] + B_g
        p2_first = {"done": False}

        def p2_guard():
            return p1_bufs if not p2_first["done"] else []

        pbank = {"i": 0}

        def next_bank(cands):
            b = cands[pbank["i"] % len(cands)]
            pbank["i"] += 1
            return b

        def load_w(g, sl):
            guard = p2_guard()
            p2_first["done"] = True
            P.dma(wsl[sl][:, :, :].rearrange("p k c -> p (k c)"), wi_s.ap()[g], reads=[B_wi[g]], writes=[B_wsl[sl]] + guard)

        def project(g, sl):
            is_diff = g < 4
            for tt in range(8):
                tsl = slice(tt * 512, (tt + 1) * 512)
                for which in range(2):
                    bk = next_bank([0, 1, 2, 3])
                    dstT, B_dst = (qT, B_qT) if which == 0 else (kT, B_kT)
                    off = which * 128
                    for kc in range(8):
                        P.op("pe", lambda e, bk=bk, kc=kc, off=off, tsl=tsl: e.matmul(ps[:, bk, :], lhsT=wsl[sl][:, kc, off:off + 128], rhs=uT[:, kc, tsl],
                                                                                   start=(kc == 0), stop=(kc == 7)),
                             reads=[B_wsl[sl]] + B_uT[tt * 4:tt * 4 + 4], writes=[bank[bk]])
                    dst_ap = dstT[sl][:, tsl]
                    if not is_diff:
                        if which == 0:
                            P.op("act", lambda e, bk=bk, dst_ap=dst_ap: e.mul(dst_ap, ps[:, bk, :], 0.125),
                                 reads=[bank[bk]], writes=[B_dst[sl][tt]])
                        else:
                            P.op("dve", lambda e, bk=bk, dst_ap=dst_ap: e.tensor_copy(out=dst_ap, in_=ps[:, bk, :]),
                                 reads=[bank[bk]], writes=[B_dst[sl][tt]])
                    else:
                        k2 = (tt * 2 + which) % 2
                        b2 = 4 + k2
                        P.op("act", lambda e, bk=bk, k2=k2: e.activation(out=sqb[k2][:, :], in_=ps[:, bk, :], func=AF.Square),
                             reads=[bank[bk]], writes=[B_sqb[k2]])
                        P.op("pe", lambda e, b2=b2, k2=k2: e.matmul(ps[:, b2, :], lhsT=bd64, rhs=sqb[k2][:, :], start=True, stop=True),
                             reads=[B_sqb[k2], B_cb], writes=[bank[b2]])
                        P.op("act", lambda e, b2=b2, k2=k2: e.activation(out=rsb[k2][:, :], in_=ps[:, b2, :], func=AF.Ln, scale=1.0 / 64, bias=EPS),
                             reads=[bank[b2]], writes=[B_rsb[k2]])
                        P.op("act", lambda e, k2=k2: e.activation(out=rsb[k2][:, :], in_=rsb[k2][:, :], func=AF.Exp, scale=-0.5),
                             reads=[B_rsb[k2]], writes=[B_rsb[k2]])
                        gsc = gq8 if which == 0 else gk
                        P.op("dve", lambda e, bk=bk, k2=k2, dst_ap=dst_ap, gsc=gsc: e.scalar_tensor_tensor(
                            out=dst_ap, in0=ps[:, bk, :], scalar=gsc, in1=rsb[k2][:, :], op0=ALU.mult, op1=ALU.mult),
                             reads=[bank[bk], B_rsb[k2], B_pvt], writes=[B_dst[sl][tt]])
                bk = next_bank([6, 7])
                for i4 in range(4):
                    tb = tt * 4 + i4
                    for kc in range(8):
                        P.op("pe", lambda e, bk=bk, kc=kc, tb=tb, i4=i4: e.matmul(ps[:, bk, i4 * 128:(i4 + 1) * 128],
                                                                                 lhsT=uT[:, kc, tb * 128:(tb + 1) * 128], rhs=wsl[sl][:, kc, 256:384],
                                                                                 start=(kc == 0), stop=(kc == 7)),
                             reads=[B_wsl[sl], B_uT[tb]], writes=[bank[bk]])
                veng = "dve" if (tt % 2 == 0) else "act"
                vdst = Vt[sl][:, tt * 4:(tt + 1) * 4, :]
                vsrc = ps[:, bk, :].rearrange("p (a b) -> p a b", a=4)
                if veng == "dve":
                    P.op("dve", lambda e, vdst=vdst, vsrc=vsrc: e.tensor_copy(out=vdst, in_=vsrc), reads=[bank[bk]], writes=[B_V[sl][tt]])
                else:
                    P.op("act", lambda e, vdst=vdst, vsrc=vsrc: e.copy(out=vdst, in_=vsrc), reads=[bank[bk]], writes=[B_V[sl][tt]])

        def project_units(g, sl):
            is_diff = g < 4
            units = []

            def mk_qk(tt, which):
                def unit(bA, bB):
                    tsl = slice(tt * 512, (tt + 1) * 512)
                    dstT, B_dst = (qT, B_qT) if which == 0 else (kT, B_kT)
                    off = which * 128
                    for kc in range(8):
                        P.op("pe", lambda e, kc=kc: e.matmul(ps[:, bA, :], lhsT=wsl[sl][:, kc, off:off + 128], rhs=uT[:, kc, tsl],
                                                             start=(kc == 0), stop=(kc == 7)),
                             reads=[B_wsl[sl]] + B_uT[tt * 4:tt * 4 + 4], writes=[bank[bA]])
                    dst_ap = dstT[sl][:, tsl]
                    if not is_diff:
                        if which == 0:
                            P.op("dve", lambda e: e.tensor_scalar(out=dst_ap, in0=ps[:, bA, :], scalar1=0.125, scalar2=None, op0=ALU.mult),
                                 reads=[bank[bA]], writes=[B_dst[sl][tt]])
                        else:
                            P.op("dve", lambda e: e.tensor_copy(out=dst_ap, in_=ps[:, bA, :]), reads=[bank[bA]], writes=[B_dst[sl][tt]])
                    else:
                        k2 = (tt * 2 + which) % 2
                        P.op("act", lambda e: e.activation(out=sqb[k2][:, :], in_=ps[:, bA, :], func=AF.Square),
                             reads=[bank[bA]], writes=[B_sqb[k2]])
                        P.op("pe", lambda e: e.matmul(ps[:, bB, :], lhsT=bd64, rhs=sqb[k2][:, :], start=True, stop=True),
                             reads=[B_sqb[k2], B_cb], writes=[bank[bB]])
                        P.op("act", lambda e: e.activation(out=rsb[k2][:, :], in_=ps[:, bB, :], func=AF.Ln, scale=1.0 / 64, bias=EPS),
                             reads=[bank[bB]], writes=[B_rsb[k2]])
                        P.op("act", lambda e: e.activation(out=rsb[k2][:, :], in_=rsb[k2][:, :], func=AF.Exp, scale=-0.5),
                             reads=[B_rsb[k2]], writes=[B_rsb[k2]])
                        gsc = gq8 if which == 0 else gk
                        P.op("dve", lambda e: e.scalar_tensor_tensor(out=dst_ap, in0=ps[:, bA, :], scalar=gsc, in1=rsb[k2][:, :], op0=ALU.mult, op1=ALU.mult),
                             reads=[bank[bA], B_rsb[k2], B_pvt], writes=[B_dst[sl][tt]])
                return unit

            def mk_v(tt, half):
                def unit(bA, bB):
                    for i2 in range(2):
                        tb = tt * 4 + half * 2 + i2
                        for kc in range(8):
                            P.op("pe", lambda e, kc=kc, tb=tb, i2=i2: e.matmul(ps[:, bA, i2 * 128:(i2 + 1) * 128],
                                                                                lhsT=uT[:, kc, tb * 128:(tb + 1) * 128], rhs=wsl[sl][:, kc, 256:384],
                                                                                start=(kc == 0), stop=(kc == 7)),
                                 reads=[B_wsl[sl], B_uT[tb]], writes=[bank[bA]])
                    vdst = Vt[sl][:, tt * 4 + half * 2:tt * 4 + half * 2 + 2, :]
                    vsrc = ps[:, bA, 0:256].rearrange("p (a b) -> p a b", a=2)
                    P.op("dve", lambda e: e.tensor_copy(out=vdst, in_=vsrc), reads=[bank[bA]], writes=[B_V[sl][tt]])
                return unit

            for tt in range(8):
                units.append(mk_qk(tt, 0))
                units.append(mk_qk(tt, 1))
                units.append(mk_v(tt, 0))
                units.append(mk_v(tt, 1))
            return units

        def project_units_half(g, sl):
            units = []

            def mk_qk(tt, which, hf):
                def unit():
                    tsl = slice(tt * 512, (tt + 1) * 512)
                    dstT, B_dst = (qT, B_qT) if which == 0 else (kT, B_kT)
                    off = which * 128 + hf * 64
                    for kc in range(8):
                        P.op("pe", lambda e, kc=kc: e.matmul(ps[64:128, 7, :], lhsT=wsl[sl][:, kc, off:off + 64], rhs=uT[:, kc, tsl],
                                                             start=(kc == 0), stop=(kc == 7)),
                             reads=[B_wsl[sl]] + B_uT[tt * 4:tt * 4 + 4], writes=[bank7hi])
                    dst_ap = dstT[sl][hf * 64:(hf + 1) * 64, tsl]
                    if which == 0:
                        P.op("dve", lambda e: e.tensor_scalar(out=dst_ap, in0=ps[64:128, 7, :], scalar1=0.125, scalar2=None, op0=ALU.mult),
                             reads=[bank7hi], writes=[B_dst[sl][tt]])
                    else:
                        P.op("dve", lambda e: e.tensor_copy(out=dst_ap, in_=ps[64:128, 7, :]), reads=[bank7hi], writes=[B_dst[sl][tt]])
                return unit

            def mk_v(tb):
                def unit():
                    for hb in range(2):
                        for kc in range(8):
                            P.op("pe", lambda e, kc=kc, hb=hb: e.matmul(ps[64:128, 7, hb * 128:(hb + 1) * 128],
                                                                         lhsT=uT[:, kc, tb * 128 + hb * 64:tb * 128 + hb * 64 + 64], rhs=wsl[sl][:, kc, 256:384],
                                                                         start=(kc == 0), stop=(kc == 7)),
                                 reads=[B_wsl[sl], B_uT[tb]], writes=[bank7hi])
                    for hb in range(2):
                        P.op("dve", lambda e, hb=hb: e.tensor_copy(out=Vt[sl][hb * 64:(hb + 1) * 64, tb, :], in_=ps[64:128, 7, hb * 128:(hb + 1) * 128]),
                             reads=[bank7hi], writes=[B_V[sl][tb // 4]])
                return unit

            for tt in range(8):
                for hf in range(2):
                    units.append(mk_qk(tt, 0, hf))
                for hf in range(2):
                    units.append(mk_qk(tt, 1, hf))
                for i4 in range(4):
                    units.append(mk_v(tt * 4 + i4))
            return units

        def step_geom(t, j):
            m = j - 4 * t
            if m >= 0:
                return 128 * m, 512 - 128 * m, "diag", 0
            if m == -1:
                return 0, 512, "near", 128
            return 0, 512, "far", 0

        def out_norm_store(g, t, y_ap, y_bufs, gain_ap, lhs_ones, div, ssb):
            k2 = t % 2
            ssbufs = [bank[ssb]] + ([bank7hi] if ssb == 7 else [])
            P.op("act", lambda e: e.activation(out=sqb[k2][:, :], in_=y_ap, func=AF.Square), reads=y_bufs, writes=[B_sqb[k2]])
            P.op("pe", lambda e: e.matmul(ps[:, ssb, :], lhsT=lhs_ones, rhs=sqb[k2][:, :], start=True, stop=True),
                 reads=[B_sqb[k2], B_cb], writes=ssbufs)
            P.op("act", lambda e: e.activation(out=rsb[k2][:, :], in_=ps[:, ssb, :], func=AF.Ln, scale=1.0 / div, bias=EPS),
                 reads=ssbufs, writes=[B_rsb[k2]])
            P.op("act", lambda e: e.activation(out=rsb[k2][:, :], in_=rsb[k2][:, :], func=AF.Exp, scale=-0.5),
                 reads=[B_rsb[k2]], writes=[B_rsb[k2]])
            P.op("dve", lambda e: e.scalar_tensor_tensor(out=mt[k2][:, :], in0=y_ap, scalar=gain_ap, in1=rsb[k2][:, :], op0=ALU.mult, op1=ALU.mult),
                 reads=y_bufs + [B_rsb[k2], B_pvt], writes=[B_mt[k2]])
            P.dma(mix_s.ap()[g, :, t * 512:(t + 1) * 512], mt[k2][:, :], reads=[B_mt[k2]], writes=[B_mix[g][t]])
            emit_convert(1)

        def attn_diff(g, sl):
            h = g
            deferred = []
            for t in range(8):
                ns = 4 * t + 4
                qbuf = [B_qT[sl][t]]

                def Z(s):
                    j = 4 * t + 3 - s
                    qlo, N, kind, u0 = step_geom(t, j)
                    st_ = (s % 2) * 2
                    ksl = slice(j * 128, (j + 1) * 128)
                    qsl = slice(t * 512 + qlo, (t + 1) * 512)
                    far = (kind == "far")
                    for c in range(2):
                        P.op("pe", lambda e, c=c: e.matmul(ps[:, st_ + c, qlo:512], lhsT=kT[sl][c * 64:(c + 1) * 64, ksl], rhs=qT[sl][c * 64:(c + 1) * 64, qsl],
                                                           start=True, stop=far),
                             reads=qbuf + [B_kT[sl][j // 4]], writes=[bank[st_ + c]])
                    if not far:
                        for c in range(2):
                            P.op("pe", lambda e, c=c: e.matmul(ps[:, st_ + c, qlo:512], lhsT=ident, rhs=bhi[:, h, u0:u0 + N], start=False, stop=False),
                                 reads=[B_bband, B_cb], writes=[bank[st_ + c]])
                            P.op("pe", lambda e, c=c: e.matmul(ps[:, st_ + c, qlo:512], lhsT=ident, rhs=blo[:, h, u0:u0 + N], start=False, stop=True),
                                 reads=[B_bband, B_cb], writes=[bank[st_ + c]])

                def E(s):
                    j = 4 * t + 3 - s
                    qlo, N, kind, u0 = step_geom(t, j)
                    st_ = (s % 2) * 2
                    k2 = s % 2
                    if kind == "far":
                        P.op("act", lambda e: e.activation(out=Ab[k2][:, :, qlo:512], in_=ps[:, st_:st_ + 2, qlo:512], func=AF.Exp, bias=b15[:, h:h + 1]),
                             reads=[bank[st_], bank[st_ + 1], B_b15], writes=[B_Ab[k2]])
                    else:
                        P.op("act", lambda e: e.activation(out=Ab[k2][:, :, qlo:512], in_=ps[:, st_:st_ + 2, qlo:512], func=AF.Exp),
                             reads=[bank[st_], bank[st_ + 1]], writes=[B_Ab[k2]])

                def PV(s):
                    j = 4 * t + 3 - s
                    qlo, N, kind, u0 = step_geom(t, j)
                    k2 = s % 2
                    st0 = (s == 0)
                    sp0 = (s == ns - 1)
                    for c in range(2):
                        P.op("pe", lambda e, c=c: e.matmul(ps[:, 4 + c, qlo:512], lhsT=Vt[sl][:, j, :], rhs=Ab[k2][:, c, qlo:512], start=st0, stop=sp0,
                                                           skip_group_check=True),
                             reads=[B_Ab[k2], B_V[sl][j // 4]], writes=[bank[4 + c]])
                        P.op("pe", lambda e, c=c: e.matmul(ps[:, 6 + c, qlo:512], lhsT=ones_b, rhs=Ab[k2][:, c, qlo:512], start=st0, stop=sp0,
                                                           skip_group_check=True),
                             reads=[B_Ab[k2], B_cb], writes=[bank[6 + c]])

                for it in range(-1, ns):
                    if it + 1 < ns:
                        Z(it + 1)
                        E(it + 1)
                    if it >= 0:
                        PV(it)
                    if it == 1 and deferred:
                        deferred.pop(0)()
                for c in range(2):
                    P.op("dve", lambda e, c=c: e.tensor_copy(out=pp[2 + c][:, :], in_=ps[:, 4 + c, :]), reads=[bank[4 + c]], writes=[B_pp[2 + c]])
                    P.op("act", lambda e, c=c: e.activation(out=pp[c][:, :], in_=ps[:, 6 + c, :], func=AF.Ln), reads=[bank[6 + c]], writes=[B_pp[c]])

                def stage2(t=t):
                    for c in range(2):
                        P.op("act", lambda e, c=c: e.activation(out=pp[c][:, :], in_=pp[c][:, :], func=AF.Exp, scale=-1.0), reads=[B_pp[c]], writes=[B_pp[c]])
                        P.op("dve", lambda e, c=c: e.tensor_tensor(out=pp[2 + c][:, :], in0=pp[2 + c][:, :], in1=pp[c][:, :], op=ALU.mult),
                             reads=[B_pp[c]], writes=[B_pp[2 + c]])
                    P.op("dve", lambda e: e.scalar_tensor_tensor(out=pp[4][:, :], in0=pp[3][:, :], scalar=neglam, in1=pp[2][:, :], op0=ALU.mult, op1=ALU.add),
                         reads=[B_pp[2], B_pp[3], B_pvt], writes=[B_pp[4]])
                    out_norm_store(g, t, pp[4][:, :], [B_pp[4]], gdo8, ones_b, 128.0, 0)

                deferred.append(stage2)
                if t == 7:
                    deferred.pop(0)()

        def attn_sb(g, sl, hosted=None):
            deferred = []
            hosted = hosted if hosted is not None else []
            zc = {"n": 0}
            hcount = {"n": 0}
            for t in range(8):
                ns = 4 * t + 4
                qbuf = [B_qT[sl][t]]
                P.op("dve", lambda e: e.memset(c32[:, :], 0.0), writes=[B_c32])
                P.op("dve", lambda e: e.memset(HL[:, :], 0.0), writes=[B_HL])

                def geo(s):
                    j = 4 * t + 3 - s
                    m = j - 4 * t
                    qlo = 128 * m if m >= 0 else 0
                    return j, m, qlo

                zset = {}

                def Astage(s):
                    j, m, qlo = geo(s)
                    zset[s] = (zc["n"] % 3) * 2
                    zc["n"] += 1
                    st_ = zset[s]
                    ksl = slice(j * 128, (j + 1) * 128)
                    qsl = slice(t * 512 + qlo, (t + 1) * 512)
                    for c in range(2):
                        P.op("pe", lambda e, c=c: e.matmul(ps[:, st_ + c, qlo:512], lhsT=kT[sl][c * 64:(c + 1) * 64, ksl], rhs=qT[sl][c * 64:(c + 1) * 64, qsl],
                                                           start=True, stop=True),
                             reads=qbuf + [B_kT[sl][j // 4]], writes=[bank[st_ + c]])
                    k2 = s % 2
                    k3 = s % 3
                    P.op("act", lambda e: e.activation(out=esb[k2][:, :, qlo:512], in_=ps[:, st_:st_ + 2, qlo:512], func=AF.Exp),
                         reads=[bank[st_], bank[st_ + 1]], writes=[B_esb[k2]])
                    P.op("act", lambda e: e.activation(out=spb[k3][:, :, qlo:512], in_=esb[k2][:, :, qlo:512], func=AF.Ln, bias=1.0),
                         reads=[B_esb[k2]], writes=[B_spb[k3]])
                    if m >= 0:
                        for c in range(2):
                            P.op("dve", lambda e, c=c: e.tensor_tensor(out=spb[k3][:, c, qlo:qlo + 128], in0=spb[k3][:, c, qlo:qlo + 128], in1=trim, op=ALU.mult),
                                 reads=[B_cb], writes=[B_spb[k3]])

                def Bstage(s):
                    j, m, qlo = geo(s)
                    st_ = zset[s]
                    k3 = s % 3
                    last = (s == 0)
                    for c in range(2):
                        P.op("pe", lambda e, c=c: e.matmul(ps[:, st_ + c, qlo:512], lhsT=ntri, rhs=spb[k3][:, c, qlo:512], start=False, stop=last, skip_group_check=True),
                             reads=[B_spb[k3], B_cb], writes=[bank[st_ + c]])
                    if s > 0:
                        for c in range(2):
                            P.op("pe", lambda e, c=c: e.matmul(ps[:, st_ + c, qlo:512], lhsT=(nselA if c == 0 else nselB), rhs=HL[:, qlo:512], start=False, stop=True, skip_group_check=True),
                                 reads=[B_HL, B_cb], writes=[bank[st_ + c]])
                    if s < ns - 1:
                        P.op("pe", lambda e: e.matmul(ps[0:34, 7, qlo:512], lhsT=EA, rhs=spb[k3][:, 0, qlo:512], start=True, stop=False),
                             reads=[B_spb[k3], B_cb], writes=[bank[7]])
                        P.op("pe", lambda e: e.matmul(ps[0:34, 7, qlo:512], lhsT=EB, rhs=spb[k3][:, 1, qlo:512], start=False, stop=True),
                             reads=[B_spb[k3], B_cb], writes=[bank[7]])
                        P.op("dve", lambda e: e.tensor_tensor(out=c32[:, qlo:512], in0=c32[:, qlo:512], in1=ps[0:34, 7, qlo:512], op=ALU.add),
                             reads=[bank[7]], writes=[B_c32])
                        P.op("dve", lambda e: e.tensor_copy(out=HL[0:34, qlo:512], in_=c32[:, qlo:512]), reads=[B_c32], writes=[B_HL])
                        P.op("dve", lambda e: e.tensor_tensor(out=HL[32:34, qlo:512], in0=c32[32:34, qlo:512], in1=HL[32:34, qlo:512], op=ALU.subtract),
                             reads=[B_c32], writes=[B_HL])

                def E2(s):
                    j, m, qlo = geo(s)
                    st_ = zset[s]
                    k2 = s % 2
                    P.op("act", lambda e: e.activation(out=Ab[k2][:, :, qlo:512], in_=ps[:, st_:st_ + 2, qlo:512], func=AF.Exp),
                         reads=[bank[st_], bank[st_ + 1]], writes=[B_Ab[k2]])
                    if m >= 0:
                        for c in range(2):
                            P.op("dve", lambda e, c=c: e.tensor_tensor(out=Ab[k2][:, c, qlo:qlo + 128], in0=Ab[k2][:, c, qlo:qlo + 128], in1=trim, op=ALU.mult),
                                 reads=[B_cb], writes=[B_Ab[k2]])

                def PV(s):
                    j, m, qlo = geo(s)
                    k2 = s % 2
                    st0 = (s == 0)
                    sp0 = (s == ns - 1)
                    for c in range(2):
                        P.op("pe", lambda e, c=c: e.matmul(ps[c * 64:(c + 1) * 64, 6, qlo:512], lhsT=Vt[sl][:, j, c * 64:(c + 1) * 64], rhs=Ab[k2][:, c, qlo:512],
                                                           start=st0, stop=sp0, skip_group_check=True),
                             reads=[B_Ab[k2], B_V[sl][j // 4]], writes=[bank[6]])

                for it in range(-2, ns):
                    if 0 <= it + 2 < ns:
                        Astage(it + 2)
                    if 0 <= it + 1 < ns:
                        Bstage(it + 1)
                        E2(it + 1)
                    if it >= 0:
                        PV(it)
                    if it == 1 and deferred:
                        deferred.pop(0)()
                    if hosted and it >= 0:
                        hcount["n"] += 1
                        if hcount["n"] % 2 == 0:
                            hosted.pop(0)()
                P.op("dve", lambda e: e.tensor_copy(out=pp[4][:, :], in_=ps[:, 6, :]), reads=[bank[6]], writes=[B_pp[4]])
                deferred.append(lambda t=t: out_norm_store(g, t, pp[4][:, :], [B_pp[4]], gsb, bd64, 64.0, 7))
                if t == 7:
                    deferred.pop(0)()
            while hosted:
                hosted.pop(0)()

        order = GROUP_ORDER
        load_w(order[0], 0)
        project(order[0], 0)
        for i, g in enumerate(order):
            sl = i % 2
            nxt = order[i + 1] if i + 1 < 8 else None
            if nxt is not None:
                load_w(nxt, (i + 1) % 2)
            if g >= 4:
                hosted = project_units_half(nxt, (i + 1) % 2) if (nxt is not None and nxt >= 4) else None
                attn_sb(g, sl, hosted)
                if nxt is not None and nxt < 4:
                    project(nxt, (i + 1) % 2)
            else:
                attn_diff(g, sl)
                if nxt is not None:
                    project(nxt, (i + 1) % 2)
        emit_convert(100)

        A.reset(m_conv)
        p2_bufs = ([B_wsl[0], B_wsl[1], B_c32, B_HL, B_st32[0], B_st16[0]] + B_esb + B_spb + B_Ab + B_sqb + B_rsb + B_pp + B_mt + B_uT
                   + [b for sl_ in range(2) for b in B_qT[sl_] + B_kT[sl_] + B_V[sl_]])
        wo = A.alloc([128, 8, D], BF16)
        B_wo_sb = Buf()
        w2b = A.alloc([128, D], F32)
        B_w2b = Buf()
        xh = [A.alloc([128, 4, D], F32) for _ in range(2)]
        B_xh = [[Buf() for _ in range(4)] for _ in range(2)]
        mxt = [A.alloc([128, 8, 512], BF16) for _ in range(2)]
        B_mxt = [Buf(), Buf()]
        u2 = [A.alloc([128, D], BF16) for _ in range(4)]
        B_u2 = [Buf() for _ in range(4)]
        u2T = [A.alloc([128, 8, 512], BF16) for _ in range(2)]
        B_u2T = [[Buf() for _ in range(4)] for _ in range(2)]
        actT = A.alloc([128, NFC, 512], BF16)
        B_actT = [Buf() for _ in range(NFC)]
        sg = [A.alloc([128, 512], F32) for _ in range(2)]
        B_sg = [Buf(), Buf()]
        wgu = [A.alloc([128, 2048], BF16) for _ in range(3)]
        B_wgu = [Buf() for _ in range(3)]
        wd = [A.alloc([128, NFC, 512], BF16) for _ in range(2)]
        B_wd = [[Buf(), Buf()], [Buf(), Buf()]]
        ot = [A.alloc([128, 512], F32) for _ in range(4)]
        B_ot = [Buf() for _ in range(4)]
        junk3 = A.alloc([128, D], BF16)
        st3 = A.alloc([128, 8], F32)
        B_st3 = [Buf(), Buf()]
        B_junk3 = Buf()

        def p3w(extra):
            return extra + p2_bufs

        P.dma(wo[:, :, :].rearrange("p k c -> p (k c)"), wo_s.ap(), reads=[B_wo], writes=p3w([B_wo_sb]))
        P.dma(w2b[:, :], bass.AP(n2_d, 0, [[0, 128], [1, D]]), writes=p3w([B_w2b]))
        first_p3 = {"f": True}

        def p3_load_x(tt):
            k = tt % 2
            extra = p2_bufs if tt < 2 else []
            P.dma(xh[k][:, :, :], xa[tt * 512:(tt + 1) * 512, :].rearrange("(a p) d -> p a d", p=128), writes=B_xh[k] + extra)

        def p3_load_m(tt):
            k = tt % 2
            extra = p2_bufs if tt < 2 else []
            P.dma(mxt[k][:, :, :], mix_s.ap()[:, :, tt * 512:(tt + 1) * 512].rearrange("e p t -> p e t"),
                  reads=[B_mix[g_][tt] for g_ in range(8)], writes=[B_mxt[k]] + extra)

        def p3_loads(tt):
            p3_load_x(tt)
            p3_load_m(tt)

        def load_wgu(tt, fc):
            k3 = fc % 3
            P.dma(wgu[k3][:, :], wf_s.ap()[fc, :, 0:2048], reads=[B_wf[fc]], writes=[B_wgu[k3]] + (p2_bufs if (tt == 0 and fc < 3) else []))

        def norm_block3(src_ap, B_src, xn_t, B_xn_t, col, B_stt):
            P.op("act", lambda e: e.activation(out=junk3[:, :], in_=src_ap, func=AF.Square, accum_out=st3[:, col:col + 1]),
                 reads=[B_src], writes=[B_junk3, B_stt])
            P.op("act", lambda e: e.activation(out=st3[:, col + 1:col + 2], in_=st3[:, col:col + 1], func=AF.Ln, scale=1.0 / D, bias=EPS),
                 reads=[B_stt], writes=[B_stt])
            P.op("act", lambda e: e.activation(out=st3[:, col + 2:col + 3], in_=st3[:, col + 1:col + 2], func=AF.Exp, scale=-0.5),
                 reads=[B_stt], writes=[B_stt])
            P.op("dve", lambda e: e.scalar_tensor_tensor(out=xn_t[:, :], in0=src_ap, scalar=st3[:, col + 2:col + 3], in1=w2b[:, :],
                                                          op0=ALU.mult, op1=ALU.mult),
                 reads=[B_src, B_stt, B_w2b], writes=[B_xn_t])

        ya = y_d.ap()

        def X1(tt):
            k = tt % 2
            for tb in range(4):
                for dh in range(2):
                    bk = 4 + (tb * 2 + dh) % 4
                    for ec in range(8):
                        P.op("pe", lambda e, bk=bk, ec=ec, tb=tb, dh=dh, k=k: e.matmul(ps[:, bk, :], lhsT=mxt[k][:, ec, tb * 128:(tb + 1) * 128],
                                                                                       rhs=wo[:, ec, dh * 512:(dh + 1) * 512], start=(ec == 0), stop=(ec == 7)),
                             reads=[B_mxt[k], B_wo_sb], writes=[bank[bk]])
                    P.op("dve", lambda e, bk=bk, tb=tb, dh=dh, k=k: e.tensor_tensor(out=xh[k][:, tb, dh * 512:(dh + 1) * 512], in0=ps[:, bk, :],
                                                                                      in1=xh[k][:, tb, dh * 512:(dh + 1) * 512], op=ALU.add),
                         reads=[bank[bk]], writes=[B_xh[k][tb]])
                norm_block3(xh[k][:, tb, :], B_xh[k][tb], u2[tb], B_u2[tb], 4 * (tb % 2), B_st3[tb % 2])

        def X2(tt):
            for tb in range(4):
                transpose_block(u2[tb], B_u2[tb], 6 + tb % 2, u2T[tt % 2][:, :, tb * 128:(tb + 1) * 128], B_u2T[tt % 2][tb], "dve")

        def Y1(tt):
            for fc in range(NFC):
                k3 = fc % 3
                if fc >= 3 or tt == 0:
                    load_wgu(tt, fc)
                if fc in (2, 6, 10, 14):
                    ci = (2, 6, 10, 14).index(fc)
                    dh_, c_ = ci // 2, ci % 2
                    P.dma(wd[dh_][:, 11 * c_:11 * c_ + 11, :],
                          wf_s.ap()[11 * c_:11 * c_ + 11, :, 2048 + dh_ * 512:2048 + (dh_ + 1) * 512].rearrange("f p d -> p f d"),
                          reads=B_wf[11 * c_:11 * c_ + 11], writes=[B_wd[dh_][c_]] + (p2_bufs if tt == 0 else []))
                bg = 4 + fc % 2
                bu = 6 + fc % 2
                ut = u2T[tt % 2]
                for kc in range(8):
                    P.op("pe", lambda e, kc=kc, k3=k3, bg=bg, ut=ut: e.matmul(ps[:, bg, :], lhsT=wgu[k3][:, kc * 128:(kc + 1) * 128], rhs=ut[:, kc, :],
                                                                                start=(kc == 0), stop=(kc == 7)),
                         reads=[B_wgu[k3]] + B_u2T[tt % 2], writes=[bank[bg]])
                for kc in range(8):
                    P.op("pe", lambda e, kc=kc, k3=k3, bu=bu, ut=ut: e.matmul(ps[:, bu, :], lhsT=wgu[k3][:, 1024 + kc * 128:1024 + (kc + 1) * 128], rhs=ut[:, kc, :],
                                                                                start=(kc == 0), stop=(kc == 7)),
                         reads=[B_wgu[k3]] + B_u2T[tt % 2], writes=[bank[bu]])
                s2 = fc % 2
                P.op("act", lambda e, s2=s2, bg=bg: e.activation(out=sg[s2][:, :], in_=ps[:, bg, :], func=AF.Silu),
                     reads=[bank[bg]], writes=[B_sg[s2]] + (p2_bufs if (tt == 0 and fc < 2) else []))
                P.op("dve", lambda e, s2=s2, bu=bu, fc=fc: e.tensor_tensor(out=actT[:, fc, :], in0=sg[s2][:, :], in1=ps[:, bu, :], op=ALU.mult),
                     reads=[B_sg[s2], bank[bu]], writes=[B_actT[fc]] + (p2_bufs if (tt == 0 and fc == 0) else []))

        def Y2(tt, dh):
            k = tt % 2
            for fc in range(NFC):
                for tb in range(4):
                    P.op("pe", lambda e, fc=fc, tb=tb, dh=dh: e.matmul(ps[:, tb, :], lhsT=actT[:, fc, tb * 128:(tb + 1) * 128], rhs=wd[dh][:, fc, :],
                                                                          start=(fc == 0), stop=(fc == NFC - 1)),
                         reads=[B_actT[fc], B_wd[dh][fc // 11]], writes=[bank[tb]])
            for tb in range(4):
                o = tb
                P.op("dve", lambda e, tb=tb, dh=dh, o=o, k=k: e.tensor_tensor(out=ot[o][:, :], in0=ps[:, tb, :], in1=xh[k][:, tb, dh * 512:(dh + 1) * 512], op=ALU.add),
                     reads=[bank[tb], B_xh[k][tb]], writes=[B_ot[o]] + (p2_bufs if (tt == 0 and dh == 0) else []))
                tk = P.dma(ya[tt * 512 + tb * 128:tt * 512 + (tb + 1) * 128, dh * 512:(dh + 1) * 512], ot[o][:, :], reads=[B_ot[o]])
                P.out_toks.append(tk)

        p3_loads(0)
        p3_loads(1)
        X1(0)
        X2(0)
        for tt in range(8):
            Y1(tt)
            if tt + 1 < 8:
                for fc_ in range(3):
                    load_wgu(tt + 1, fc_)
                X1(tt + 1)
            if tt + 2 < 8:
                p3_load_m(tt + 2)
            Y2(tt, 0)
            if tt + 1 < 8:
                X2(tt + 1)
            Y2(tt, 1)
            if tt + 2 < 8:
                p3_load_x(tt + 2)

        block = stack.enter_context(nc.Block())
        P.finalize(block)
    return nc


def _t5_bucket_np(rel):
    nb = 16
    max_exact = 8
    ret = (rel > 0).astype(np.int32) * nb
    n = np.abs(rel)
    nf = np.maximum(n, 1).astype(np.float32)
    large = max_exact + (np.log(nf / np.float32(max_exact)) / np.float32(math.log(128 / max_exact))
                         * np.float32(nb - max_exact)).astype(np.int32)
    large = np.minimum(large, nb - 1)
    return ret + np.where(n < max_exact, n, large)


def _const_table():
    c = np.zeros((128, NCOL), np.float32)
    i = np.arange(128)
    c[:, C_ID:C_ID + 128] = np.eye(128, dtype=np.float32)
    c[:, C_NTRI:C_NTRI + 128] = -(i[:, None] >= i[None, :]).astype(np.float32)
    c[:, C_ONES:C_ONES + 128] = 1.0
    c[:, C_BD:C_BD + 128] = ((i[:, None] // 64) == (i[None, :] // 64)).astype(np.float32)
    c[0, C_NSA:C_NSA + 128] = -1.0
    c[32, C_NSA:C_NSA + 128] = -1.0
    c[1, C_NSB:C_NSB + 128] = -1.0
    c[33, C_NSB:C_NSB + 128] = -1.0
    c[:, C_EA + 0] = 1.0
    c[:, C_EA + 32] = 1.0
    c[:, C_EB + 1] = 1.0
    c[:, C_EB + 33] = 1.0
    c[:, C_TM:C_TM + 128] = (i[:, None] < i[None, :]).astype(np.float32)
    u = np.arange(640)
    c[:, C_MNEG:C_MNEG + 640] = np.where((i[:, None] // 64) > (u[None, :] // 64), NEG, 0.0).astype(np.float32)
    s = np.arange(767)
    bk = _t5_bucket_np((127 - s).astype(np.int32))
    c[bk, C_OH + s] = 1.0
    return c


_NC_CACHE = {}


def _prep_shared(inp):
    w_in = np.asarray(inp["w_in"][0], np.float32)
    cols = []
    for g in range(8):
        base = 0 if g < 4 else 1536
        gi = g % 4
        cols.append(np.concatenate([np.arange(base + gi * 128, base + gi * 128 + 128),
                                    np.arange(base + 512 + gi * 128, base + 512 + gi * 128 + 128),
                                    np.arange(base + 1024 + gi * 128, base + 1024 + gi * 128 + 128)]))
    win = np.empty((8, 128, 3072), np.float32)
    w4 = w_in.reshape(8, 128, 3072)
    for g in range(8):
        win[g] = np.transpose(w4[:, :, cols[g]], (1, 0, 2)).reshape(128, 3072)
    wout = np.ascontiguousarray(np.transpose(np.asarray(inp["w_out"][0], np.float32).reshape(8, 128, 1024), (1, 0, 2)).reshape(128, 8192))
    wg = np.asarray(inp["w_gate"][0], np.float32).reshape(8, 128, NFC, 128)
    wu = np.asarray(inp["w_up"][0], np.float32).reshape(8, 128, NFC, 128)
    wdn = np.asarray(inp["w_down"][0], np.float32).reshape(NFC, 128, 1024)
    wffn = np.empty((NFC, 128, 3072), np.float32)
    wffn[:, :, 0:1024] = np.transpose(wg, (2, 1, 0, 3)).reshape(NFC, 128, 1024)
    wffn[:, :, 1024:2048] = np.transpose(wu, (2, 1, 0, 3)).reshape(NFC, 128, 1024)
    wffn[:, :, 2048:3072] = wdn
    p = np.arange(128)
    pv = np.stack([np.asarray(inp["q_norm_w"][0])[p % 64], np.asarray(inp["k_norm_w"][0])[p % 64],
                   np.asarray(inp["diff_out_norm_w"][0])[p], np.asarray(inp["sb_out_norm_w"][0])[p % 64]], axis=1).astype(np.float32)
    lam = np.concatenate([np.asarray(inp["lambda_q1"][0]), np.asarray(inp["lambda_k1"][0]),
                          np.asarray(inp["lambda_q2"][0]), np.asarray(inp["lambda_k2"][0])]).astype(np.float32)[None, :]
    return {
        "win": win, "wout": wout, "wffn": wffn,
        "n1": np.asarray(inp["norm1_w"], np.float32).reshape(1, D),
        "n2": np.asarray(inp["norm2_w"], np.float32).reshape(1, D),
        "pv": np.ascontiguousarray(pv), "lam": np.ascontiguousarray(lam),
        "rb": np.ascontiguousarray(np.asarray(inp["rel_bias"], np.float32)),
        "cst": _const_table(),
    }


def kernel(**inputs):
    x = np.asarray(inputs["x"], np.float32)
    nb = x.shape[0]
    shared = _prep_shared(inputs)
    if "nc" not in _NC_CACHE:
        _NC_CACHE["nc"] = build_nc()
    nc = _NC_CACHE["nc"]
    in_maps = []
    for b in range(nb):
        m = dict(shared)
        m["x"] = np.ascontiguousarray(x[b])
        in_maps.append(m)
    res = run_bass_kernel_spmd(nc, in_maps, core_ids=list(range(nb)))
    return np.stack([np.asarray(r["y"], np.float32) for r in res.results], axis=0)
```
